# Optimizing a Trainium2 kernel written in Bass

```python
import math
import jax, jax.numpy as jnp
from jax import lax
import numpy as np

D_MODEL = 1024
BATCH = 16
SEQ = 256
DEPTH = 1
DEC_BATCH = 8
DEC_SEQ = 1024
PAST_LEN = 256

GRID_W = 64
D_ATTN = 512
D_SSM = 512
N_HEADS = 4
QK_DIM = 64
V_DIM = 2 * QK_DIM
SSM_GROUP_CH = 16
N_SSM_GROUPS = D_SSM // SSM_GROUP_CH
SSM_STATE = 64
D_IN = 4 * D_ATTN + 2 * D_SSM
SPLITS = [D_ATTN, 2 * D_ATTN, 3 * D_ATTN, 4 * D_ATTN, 4 * D_ATTN + D_SSM]
ROPE_THETA = 10000.0
Q_BLOCK = 128
EPS = 1e-6
DT_MIN = 0.001
DT_MAX = 0.1

kernel_name = "hybrid_diffattn_s5_prefix_step"


def rmsnorm(x, g):
    xf = x.astype(jnp.float32)
    y = xf * lax.rsqrt(jnp.mean(xf * xf, axis=-1, keepdims=True) + EPS)
    return (y * g.astype(jnp.float32)).astype(x.dtype)


def adaln(cond, w_ada, b_ada):
    m = jax.nn.silu(cond) @ w_ada + b_ada
    return jnp.split(m, 3, axis=-1)


def rope_2d(x):
    L = x.shape[1]
    rows = L // GRID_W
    row = jnp.repeat(jnp.arange(rows, dtype=jnp.float32), GRID_W)
    col = jnp.tile(jnp.arange(GRID_W, dtype=jnp.float32), rows)
    n_freq = QK_DIM // 4
    inv = ROPE_THETA ** (-jnp.arange(n_freq, dtype=jnp.float32) / n_freq)

    def rot(seg, pos):
        ang = pos[:, None] * inv
        cos = jnp.cos(ang)[None, :, None, None, :]
        sin = jnp.sin(ang)[None, :, None, None, :]
        s1 = seg[..., :n_freq].astype(jnp.float32)
        s2 = seg[..., n_freq:].astype(jnp.float32)
        return jnp.concatenate([s1 * cos - s2 * sin, s2 * cos + s1 * sin], axis=-1)

    half = QK_DIM // 2
    return jnp.concatenate([rot(x[..., :half], row), rot(x[..., half:], col)], axis=-1).astype(x.dtype)


def diff_attention(q, k, v, lam):
    b, Lq = q.shape[0], q.shape[1]
    nblk = Lq // Q_BLOCK
    qb = jnp.moveaxis(q.reshape(b, nblk, Q_BLOCK, N_HEADS, 2, QK_DIM), 1, 0)
    scale = QK_DIM ** -0.5

    def block(qblk):
        s = jnp.einsum('bqhcd,bkhcd->bhcqk', qblk, k).astype(jnp.float32) * scale
        p = jax.nn.softmax(s, axis=-1)
        w = p[:, :, 0] - lam * p[:, :, 1]
        return jnp.einsum('bhqk,bkhe->bqhe', w.astype(v.dtype), v)

    o = lax.map(block, qb)
    return jnp.moveaxis(o, 0, 1).reshape(b, Lq, N_HEADS, V_DIM)


def zoh(A_re, A_im, log_dt, B_re, B_im):
    A_re = A_re.astype(jnp.float32)
    A_im = A_im.astype(jnp.float32)
    dt = jnp.exp(log_dt.astype(jnp.float32))[:, None]
    mag = jnp.exp(A_re * dt)
    ab_re = mag * jnp.cos(A_im * dt)
    ab_im = mag * jnp.sin(A_im * dt)
    nr, ni = ab_re - 1.0, ab_im
    den = A_re * A_re + A_im * A_im
    f_re = (nr * A_re + ni * A_im) / den
    f_im = (ni * A_re - nr * A_im) / den
    B_re = B_re.astype(jnp.float32)
    B_im = B_im.astype(jnp.float32)
    bb_re = f_re[..., None] * B_re - f_im[..., None] * B_im
    bb_im = f_re[..., None] * B_im + f_im[..., None] * B_re
    return ab_re, ab_im, bb_re, bb_im


def complex_combine(e1, e2):
    a1r, a1i, b1r, b1i = e1
    a2r, a2i, b2r, b2i = e2
    return (a1r * a2r - a1i * a2i,
            a1r * a2i + a1i * a2r,
            a2r * b1r - a2i * b1i + b2r,
            a2r * b1i + a2i * b1r + b2i)


def ssm_scan(u, ab_re, ab_im, bb_re, bb_im, C_re, C_im, h0_re, h0_im, reverse):
    bu_re = jnp.einsum('blgh,gph->blgp', u, bb_re)
    bu_im = jnp.einsum('blgh,gph->blgp', u, bb_im)
    if reverse:
        bu_re = jnp.flip(bu_re, axis=1)
        bu_im = jnp.flip(bu_im, axis=1)
    first_re = bu_re[:, 0] + ab_re * h0_re - ab_im * h0_im
    first_im = bu_im[:, 0] + ab_re * h0_im + ab_im * h0_re
    bu_re = bu_re.at[:, 0].set(first_re)
    bu_im = bu_im.at[:, 0].set(first_im)
    a_re = jnp.broadcast_to(ab_re, bu_re.shape)
    a_im = jnp.broadcast_to(ab_im, bu_im.shape)
    _, _, h_re, h_im = lax.associative_scan(complex_combine, (a_re, a_im, bu_re, bu_im), axis=1)
    fin_re, fin_im = h_re[:, -1], h_im[:, -1]
    if reverse:
        h_re = jnp.flip(h_re, axis=1)
        h_im = jnp.flip(h_im, axis=1)
    y = (jnp.einsum('blgp,ghp->blgh', h_re, C_re.astype(jnp.float32))
         - jnp.einsum('blgp,ghp->blgh', h_im, C_im.astype(jnp.float32)))
    return y, fin_re, fin_im


def ssm_branch(u, h0, A_re, A_im, log_dt, B_re, B_im, C_re, C_im, D_skip, with_state):
    b, L = u.shape[0], u.shape[1]
    uf = u.astype(jnp.float32).reshape(b, L, N_SSM_GROUPS, SSM_GROUP_CH)
    h0 = h0.astype(jnp.float32)
    ys, fins = [], []
    for d in range(2):
        ab_re, ab_im, bb_re, bb_im = zoh(A_re[d], A_im[d], log_dt[d], B_re[d], B_im[d])
        y, fr, fi = ssm_scan(uf, ab_re, ab_im, bb_re, bb_im, C_re[d], C_im[d],
                             h0[:, d, 0], h0[:, d, 1], reverse=(d == 1))
        ys.append(y)
        fins.append(jnp.stack([fr, fi], axis=1))
    y = (ys[0] + ys[1]).reshape(b, L, D_SSM) + D_skip.astype(jnp.float32) * uf.reshape(b, L, D_SSM)
    if with_state:
        return y, jnp.stack(fins, axis=1)
    return y, None


def mixer_inputs(x, shift, scale, norm_pre, w_in):
    h = rmsnorm(x, norm_pre) * (1 + scale) + shift
    proj = h @ w_in
    q, k, v, g_attn, u, g_ssm = jnp.split(proj, SPLITS, axis=-1)
    b, L = x.shape[0], x.shape[1]
    q = q.reshape(b, L, N_HEADS, 2, QK_DIM)
    k = k.reshape(b, L, N_HEADS, 2, QK_DIM)
    v = v.reshape(b, L, N_HEADS, V_DIM)
    return q, k, v, g_attn, u, g_ssm


def mixer_output(x, o_attn, lam_init, g_attn, y_ssm, g_ssm, gate,
                 subln, w_glu, b_glu, w_out, norm_post):
    b, L = x.shape[0], x.shape[1]
    o = (rmsnorm(o_attn, subln) * (1 - lam_init)).reshape(b, L, D_ATTN) * jax.nn.silu(g_attn)
    ys = jax.nn.gelu(y_ssm.astype(x.dtype))
    ys = ys * jax.nn.sigmoid(ys @ w_glu + b_glu) * jax.nn.silu(g_ssm)
    out = jnp.concatenate([o, ys], axis=-1) @ w_out
    return x + gate * rmsnorm(out, norm_post)


def setup_inputs(seed: int = 0) -> dict:
    key = jax.random.key(seed)
    ks = jax.random.split(key, 26)
    f32 = jnp.float32
    G, P, Hc = N_SSM_GROUPS, SSM_STATE, SSM_GROUP_CH
    n_idx = jnp.arange(P, dtype=f32)
    return {
        "x_prompt": jax.random.normal(ks[0], (BATCH, SEQ, D_MODEL), f32),
        "x_sample": jax.random.normal(ks[1], (DEC_BATCH, DEC_SEQ, D_MODEL), f32),
        "cache_k": jax.random.normal(ks[2], (DEC_BATCH, DEPTH, PAST_LEN, N_HEADS, 2 * QK_DIM), f32),
        "cache_v": jax.random.normal(ks[3], (DEC_BATCH, DEPTH, PAST_LEN, N_HEADS, V_DIM), f32),
        "state_ssm": 0.1 * jax.random.normal(ks[4], (DEC_BATCH, DEPTH, 2, 2, G, P), f32),
        "c": jax.random.normal(ks[5], (DEC_BATCH, D_MODEL), f32),
        "c_ctx": jax.random.normal(ks[6], (D_MODEL,), f32),
        "w_ada": jax.random.normal(ks[7], (DEPTH, D_MODEL, 3 * D_MODEL), f32) * D_MODEL ** -0.5,
        "b_ada": 0.01 * jax.random.normal(ks[8], (DEPTH, 3 * D_MODEL), f32),
        "norm_pre": 1.0 + 0.05 * jax.random.normal(ks[9], (DEPTH, D_MODEL), f32),
        "norm_post": 1.0 + 0.05 * jax.random.normal(ks[10], (DEPTH, D_MODEL), f32),
        "w_in": jax.random.normal(ks[11], (DEPTH, D_MODEL, D_IN), f32) * D_MODEL ** -0.5,
        "lambda_qk": 0.1 * jax.random.normal(ks[12], (DEPTH, 4, QK_DIM), f32),
        "subln": 1.0 + 0.05 * jax.random.normal(ks[13], (DEPTH, V_DIM), f32),
        "ssm_A_re": -0.5 + 0.01 * jax.random.normal(ks[14], (DEPTH, 2, G, P), f32),
        "ssm_A_im": math.pi * n_idx + 0.01 * jax.random.normal(ks[15], (DEPTH, 2, G, P), f32),
        "ssm_log_dt": jax.random.uniform(ks[16], (DEPTH, 2, G), f32,
                                         minval=math.log(DT_MIN), maxval=math.log(DT_MAX)),
        "ssm_B_re": jax.random.normal(ks[17], (DEPTH, 2, G, P, Hc), f32) * (2 * Hc) ** -0.5,
        "ssm_B_im": jax.random.normal(ks[18], (DEPTH, 2, G, P, Hc), f32) * (2 * Hc) ** -0.5,
        "ssm_C_re": jax.random.normal(ks[19], (DEPTH, 2, G, Hc, P), f32) * (2 * P) ** -0.5,
        "ssm_C_im": jax.random.normal(ks[20], (DEPTH, 2, G, Hc, P), f32) * (2 * P) ** -0.5,
        "ssm_D": jax.random.normal(ks[21], (DEPTH, D_SSM), f32),
        "w_glu": jax.random.normal(ks[22], (DEPTH, D_SSM, D_SSM), f32) * D_SSM ** -0.5,
        "b_glu": 0.01 * jax.random.normal(ks[23], (DEPTH, D_SSM), f32),
        "w_out": jax.random.normal(ks[24], (DEPTH, D_ATTN + D_SSM, D_MODEL), f32) * (D_ATTN + D_SSM) ** -0.5,
    }


def reference(x_prompt, x_sample, cache_k, cache_v, state_ssm, c, c_ctx, w_ada, b_ada,
              norm_pre, norm_post, w_in, lambda_qk, subln, ssm_A_re, ssm_A_im, ssm_log_dt,
              ssm_B_re, ssm_B_im, ssm_C_re, ssm_C_im, ssm_D, w_glu, b_glu, w_out):
    xp, xs = x_prompt, x_sample
    bp, Lp = xp.shape[0], xp.shape[1]
    new_k, new_v, new_s = [], [], []
    for l in range(DEPTH):
        lam_init = 0.8 - 0.6 * math.exp(-0.3 * l)
        lq = lambda_qk[l].astype(jnp.float32)
        lam = jnp.exp(jnp.sum(lq[0] * lq[1])) - jnp.exp(jnp.sum(lq[2] * lq[3])) + lam_init
        ssm_p = (ssm_A_re[l], ssm_A_im[l], ssm_log_dt[l], ssm_B_re[l], ssm_B_im[l],
                 ssm_C_re[l], ssm_C_im[l], ssm_D[l])
        out_p = (subln[l], w_glu[l], b_glu[l], w_out[l], norm_post[l])

        shift, scale, gate = adaln(c_ctx, w_ada[l], b_ada[l])
        q, k, v, g_attn, u, g_ssm = mixer_inputs(xp, shift, scale, norm_pre[l], w_in[l])
        o = diff_attention(q, k, v, lam)
        h0 = jnp.zeros((bp, 2, 2, N_SSM_GROUPS, SSM_STATE), jnp.float32)
        y, fin = ssm_branch(u, h0, *ssm_p, with_state=True)
        new_k.append(k.reshape(bp, Lp, N_HEADS, 2 * QK_DIM))
        new_v.append(v)
        new_s.append(fin.astype(xp.dtype))
        xp = mixer_output(xp, o, lam_init, g_attn, y, g_ssm, gate, *out_p)

        bs, Ls = xs.shape[0], xs.shape[1]
        shift, scale, gate = [m[:, None, :] for m in adaln(c, w_ada[l], b_ada[l])]
        q, k, v, g_attn, u, g_ssm = mixer_inputs(xs, shift, scale, norm_pre[l], w_in[l])
        q = rope_2d(q)
        k = rope_2d(k)
        ck = cache_k[:, l]
        k_all = jnp.concatenate([ck.reshape(bs, ck.shape[1], N_HEADS, 2, QK_DIM).astype(k.dtype), k], axis=1)
        v_all = jnp.concatenate([cache_v[:, l].astype(v.dtype), v], axis=1)
        o = diff_attention(q, k_all, v_all, lam)
        y, _ = ssm_branch(u, state_ssm[:, l], *ssm_p, with_state=False)
        xs = mixer_output(xs, o, lam_init, g_attn, y, g_ssm, gate, *out_p)

    new_cache_k = jnp.stack(new_k, axis=1)
    new_cache_v = jnp.stack(new_v, axis=1)
    new_state_ssm = jnp.stack(new_s, axis=1)
    return (xp, xs, new_cache_k, new_cache_v, new_state_ssm)
```

```python
import contextlib
import math
import os
import numpy as np
import concourse.bass as bass
import concourse.mybir as mybir
from concourse.bass_utils import run_bass_kernel_spmd

F32 = mybir.dt.float32
BF16 = mybir.dt.bfloat16
I32 = mybir.dt.int32
AF = mybir.ActivationFunctionType
ALU = mybir.AluOpType
AX = mybir.AxisListType

NTOK = 1536
NCH = 192
TWO_PI = 2.0 * math.pi
KE, KQ, KP, K1, K8, NK = 0, 8, 16, 24, 25, 26


class Prog:
    ENG = ('pe', 'act', 'dve', 'pool', 'sp')

    def __init__(self, nc):
        self.nc = nc
        self.ops = {e: [] for e in self.ENG}
        self.cnt = {e: 0 for e in self.ENG}
        self.seen = {e: {} for e in self.ENG}
        self.res = {}
        self.dcnt = {}
        self.out_tokens = []
        self.capture = None

    def _deps(self, eng, reads, writes):
        deps = {}

        def add(tok):
            if tok is None:
                return
            s, v = tok
            if deps.get(s, 0) < v:
                deps[s] = v
        for r in reads:
            st = self.res.get(r)
            if st:
                add(st['w'])
        for w in writes:
            st = self.res.get(w)
            if st:
                add(st['w'])
                for s, v in st['r'].items():
                    add((s, v))
        for s, v in deps.items():
            if s == 'pe' and eng == 'pe':
                continue
            if self.seen[eng].get(s, 0) < v:
                self.seen[eng][s] = v
                self.ops[eng].append(('wait', s, v))

    def _commit(self, tok, reads, writes):
        for w in writes:
            self.res[w] = {'w': tok, 'r': {}}
        for r in reads:
            st = self.res.setdefault(r, {'w': None, 'r': {}})
            s, v = tok
            if st['r'].get(s, 0) < v:
                st['r'][s] = v

    @staticmethod
    def _excl(reads, writes):
        isb = lambda n: len(n) == 2 and n[0] == 'b' and n[1].isdigit()
        w = list(writes) + [r for r in reads if isb(r)]
        r = [r for r in reads if not isb(r)]
        return r, w

    def op(self, eng, fn, reads=(), writes=()):
        if self.capture is not None:
            self.capture.append(lambda: self.op(eng, fn, reads, writes))
            return
        reads, writes = self._excl(reads, writes)
        self._deps(eng, reads, writes)
        self.cnt[eng] += 1
        tok = (eng, self.cnt[eng])
        self.ops[eng].append(('op', fn))
        self._commit(tok, reads, writes)

    def dma(self, q, out, in_, reads=(), writes=(), key=None, final=False, **kw):
        if self.capture is not None:
            self.capture.append(lambda: self.dma(q, out, in_, reads, writes, key, final, **kw))
            return
        self._deps(q, reads, writes)
        k = ('dma', key)
        self.dcnt[k] = self.dcnt.get(k, 0) + 16
        tok = (k, self.dcnt[k])
        self.ops[q].append(('dma', out, in_, k, kw))
        self._commit(tok, reads, writes)
        if final:
            self.out_tokens.append(tok)

    def tt(self, eng, out, in0, in1, op, r, w):
        self.op(eng, lambda e: e.tensor_tensor(out=out, in0=in0, in1=in1, op=op), r, w)

    def ts(self, eng, out, in0, s1, s2, op0, op1, r, w):
        if s2 is None:
            self.op(eng, lambda e: e.tensor_scalar(out=out, in0=in0, scalar1=s1, scalar2=None, op0=op0), r, w)
        else:
            self.op(eng, lambda e: e.tensor_scalar(out=out, in0=in0, scalar1=s1, scalar2=s2, op0=op0, op1=op1), r, w)

    def stt(self, eng, out, in0, scalar, in1, op0, op1, r, w, accum=None):
        if accum is None:
            self.op(eng, lambda e: e.scalar_tensor_tensor(out=out, in0=in0, scalar=scalar, in1=in1, op0=op0, op1=op1), r, w)
        else:
            self.op(eng, lambda e: e.scalar_tensor_tensor(out=out, in0=in0, scalar=scalar, in1=in1, op0=op0, op1=op1,
                                                          accum_out=accum), r, w)

    def act(self, out, in_, func, r, w, bias=None, scale=None, accum=None):
        kw = {}
        if bias is not None:
            kw['bias'] = bias
        if scale is not None:
            kw['scale'] = scale
        if accum is not None:
            kw['accum_out'] = accum
        self.op('act', lambda e: e.activation(out=out, in_=in_, func=func, **kw), r, w)

    def cp(self, eng, out, in_, r, w):
        if eng == 'act':
            self.op(eng, lambda e: e.copy(out=out, in_=in_), r, w)
        else:
            self.op(eng, lambda e: e.tensor_copy(out=out, in_=in_), r, w)

    def memset(self, eng, ap, val, w):
        self.op(eng, lambda e: e.memset(ap, val), (), w)

    def mm(self, out, lhsT, rhs, start, stop, r, w):
        self.op('pe', lambda e: e.matmul(out, lhsT=lhsT, rhs=rhs, start=start, stop=stop,
                                         skip_group_check=True), r, w)

    def tr(self, out, in_, ident, r, w):
        self.op('pe', lambda e: e.transpose(out=out, in_=in_, identity=ident), r, w)

    def recip(self, out, in_, r, w):
        self.op('dve', lambda e: e.reciprocal(out=out, in_=in_), r, w)

    def rsum(self, out, in_, r, w):
        self.op('dve', lambda e: e.reduce_sum(out=out, in_=in_, axis=AX.X), r, w)

    def run_q(self, q, n):
        cap, self.capture = self.capture, None
        for _ in range(n):
            if not q:
                break
            q.pop(0)()
        self.capture = cap

    def finish(self):
        last = {}
        for s, v in self.out_tokens:
            last[s] = max(last.get(s, 0), v)
        for s, v in last.items():
            self.ops['sp'].append(('wait', s, v))

    def emit(self):
        nc = self.nc
        keys = list(self.ENG[:4]) + list(self.dcnt.keys())
        with contextlib.ExitStack() as st:
            sems = {}
            for i, k in enumerate(keys):
                sems[k] = st.enter_context(nc.semaphore("s%d" % i))
            block = st.enter_context(nc.Block())

            need = {e_: set() for e_ in self.ENG[:4]}
            for en in self.ENG:
                for item in self.ops[en]:
                    if item[0] == 'wait' and item[1] in need:
                        need[item[1]].add(item[2])
            newidx = {e_: {v: i + 1 for i, v in enumerate(sorted(need[e_]))} for e_ in need}

            def run(engname, e):
                k = 0
                for item in self.ops[engname]:
                    if item[0] == 'wait':
                        v = item[2]
                        if item[1] in newidx:
                            v = newidx[item[1]][v]
                        e.wait_ge(sems[item[1]], v)
                    elif item[0] == 'op':
                        k += 1
                        ins = item[1](e)
                        if k in need[engname]:
                            ins.then_inc(sems[engname], 1)
                    else:
                        _, out, in_, kk, kw = item
                        e.dma_start(out=out, in_=in_, **kw).then_inc(sems[kk], 16)

            @block.tensor
            def _(e):
                run('pe', e)

            @block.scalar
            def _(e):
                run('act', e)

            @block.vector
            def _(e):
                run('dve', e)

            @block.gpsimd
            def _(e):
                run('pool', e)

            @block.sync
            def _(e):
                run('sp', e)


def V(t, row, col, dims, nrows=128):
    a = t[:]
    ps = a.ap[0][0]
    return bass.AP(a.tensor, a.offset + row * ps + col, [[ps, nrows]] + [list(d) for d in dims])


def build():
    nc = bass.Bass("TRN2", target_bir_lowering=False)

    def din(name, shape, dt=F32):
        return nc.dram_tensor(name, list(shape), dt, kind="ExternalInput").ap()

    def dout(name, shape, dt=F32):
        return nc.dram_tensor(name, list(shape), dt, kind="ExternalOutput").ap()

    def dscr(name, shape, dt):
        return nc.dram_tensor(name, list(shape), dt, kind="Internal").ap()

    x_d = din("x", [NTOK, 1024])
    ck_d = din("ck", [256, 512])
    cv_d = din("cv", [256, 512])
    st_d = din("st", [2, 64, 64])
    cvec_d = din("cvec", [2, 1024])
    wada_d = din("w_ada", [1024, 3072])
    bada_d = din("b_ada", [3072])
    npre_d = din("norm_pre", [1024])
    npost_d = din("norm_post", [1024])
    win_d = din("w_in", [1024, 3072])
    lq_d = din("lambda_qk", [256])
    subln_d = din("subln", [128])
    are_d = din("ssm_A_re", [2, 32, 64])
    aim_d = din("ssm_A_im", [2, 32, 64])
    ldt_d = din("ssm_log_dt", [2, 32])
    bre_d = din("ssm_B_re", [2, 32, 64, 16])
    bim_d = din("ssm_B_im", [2, 32, 64, 16])
    cre_d = din("ssm_C_re", [2, 512, 64])
    cim_d = din("ssm_C_im", [2, 512, 64])
    dsk_d = din("ssm_D", [512])
    wglu_d = din("w_glu", [512, 512])
    bglu_d = din("b_glu", [512])
    wout_d = din("w_out", [1024, 1024])
    ident_d = din("ident", [128, 128])
    ropec_d = din("ropec", [1024, 64])
    ropes_d = din("ropes", [1024, 64])
    karr_d = din("karr", [128, NK * 32])
    mask_d = din("mask01", [128, 256])

    y_d = dout("y", [NTOK, 1024])
    nk_d = dout("nk", [512, 512])
    nv_d = dout("nv", [512, 512])
    ns_d = dout("ns", [2, 2, 64, 64])

    QK_s = dscr("QKs", [NTOK, 1024], BF16)
    V_s = dscr("Vs", [NTOK, 512], BF16)
    GA_s = dscr("GAs", [NTOK, 512], BF16)
    UD = dscr("UD", [512, 8, NCH], BF16)
    GSD = dscr("GSD", [512, 8, NCH], BF16)
    YD = dscr("YD", [32, 128, NCH], F32)

    P = Prog(nc)
    es = contextlib.ExitStack()

    def sb(name, shape, dt):
        return es.enter_context(nc.sbuf_tensor("s_" + name, list(shape), dt))

    with es:
        psall = es.enter_context(nc.psum_tensor("psall", [128, 4096], F32))
        banks = [psall[:, i * 512:(i + 1) * 512] for i in range(8)]
        bkb = [b.bitcast(BF16) for b in banks]

        ident = sb("ident", [128, 128], F32)
        identb = sb("identb", [128, 128], BF16)
        epst = sb("epst", [128, 1], F32)
        gm = sb("gm", [128, 8, 2], F32)
        shf = sb("shf", [128, 8, 2], F32)
        gn = sb("gn", [128, 2, 1024], F32)
        sub08 = sb("sub08", [128, 512], F32)
        ropec = sb("ropec", [128, 8, 64], F32)
        ropes = sb("ropes", [128, 8, 64], F32)
        neglam = sb("neglam", [128, 1], F32)
        al4 = sb("al4", [128, 128], F32)
        dsk = sb("dsk", [128, 4], F32)
        bglu = sb("bglu", [128, 4], F32)
        stat = sb("stat", [128, 64], F32)
        W_S = sb("W_S", [128, 32 * 2 * 128], BF16)
        W_Y = sb("W_Y", [128, 32 * 2 * 128], BF16)
        M0 = sb("M0", [128, 32 * 128], BF16)
        hT = sb("hT", [128, 8 * NTOK], BF16)
        catT = hT
        WA = sb("WA", [128, 3 * 4096], BF16)
        xt = [sb("xt%d" % i, [128, 1024], F32) for i in range(2)]
        xn = [sb("xn%d" % i, [128, 1024], BF16) for i in range(2)]
        junk = sb("junk", [128, 1024], BF16)
        R1 = sb("R1", [128, 7680], F32)
        SHb = sb("SHb", [128, 32 * 2 * 198], BF16)
        Xs = sb("Xs", [128, 32 * 196], BF16)
        R2 = sb("R2", [128, 1920], F32)
        ust = sb("ust", [128, 2 * 1536], BF16)

        P.dma('sp', ident[:], ident_d, writes=['ident'], key='c0')
        P.cp('dve', identb[:], ident[:], ['ident'], ['identb'])
        P.dma('sp', V(R1, 0, 7168, [[1, 256]]), mask_d, writes=['mask'], key='c0m')
        P.memset('dve', epst[:], 1e-6, ['epst'])
        P.memset('dve', stat[:], 0.0, ['st_l', 'st_l2', 'st_l3', 'bar60', 'bar61', 'bar62', 'bar63'] + [n % t for t in range(12) for n in ('ssq%d', 'sd%d', 'rs%d')])

        cT = V(R1, 0, 0, [[1, 16]])
        for cond in range(2):
            P.dma('sp', V(R1, 0, cond * 8, [[1, 8]]), cvec_d[cond].rearrange("(j p) -> p j", p=128),
                  writes=['cT'], key='c1', allow_slow_non_contiguous=True)
        csig = V(R1, 0, 16, [[1, 16]])
        P.act(csig, cT, AF.Sigmoid, ['cT'], ['csig'])
        csf = V(R1, 0, 32, [[1, 16]])
        P.tt('dve', csf, cT, csig, ALU.mult, ['cT', 'csig'], ['csf'])
        R1b = R1[:].bitcast(BF16)
        csT = bass.AP(R1b.tensor, R1b.offset + 128, [[R1b.ap[0][0], 128], [1, 16]])
        P.cp('dve', csT, csf, ['csf'], ['csT'])
        csbc = bass.AP(R1b.tensor, R1b.offset + 256, [[R1b.ap[0][0], 128], [256, 8], [128, 2], [1, 128]])
        csf_b = V(R1, 0, 32, [[1, 8], [8, 2], [0, 128]])
        P.cp('dve', csbc, csf_b, ['csf'], ['csbc'])
        bsh = V(R1, 0, 2400, [[1, 8]])
        bsc = V(R1, 0, 2408, [[1, 8]])
        npf = V(R1, 0, 2416, [[1, 8]])
        P.dma('sp', bsh, bada_d[0:1024].rearrange("(j p) -> p j", p=128), writes=['bsh'], key='c2',
              allow_slow_non_contiguous=True)
        P.dma('sp', bsc, bada_d[1024:2048].rearrange("(j p) -> p j", p=128), writes=['bsc'], key='c3',
              allow_slow_non_contiguous=True)
        P.dma('sp', npf, npre_d.rearrange("(j p) -> p j", p=128), writes=['npf'], key='c4',
              allow_slow_non_contiguous=True)
        bgate = V(R1, 0, 2560, [[1, 1024]])
        npost = V(R1, 0, 3584, [[1, 1024]])
        P.dma('sp', bgate, bass.AP(bada_d.tensor, 2048, [[0, 128], [1, 1024]]), writes=['bgate'], key='c5')
        P.dma('sp', npost, bass.AP(npost_d.tensor, 0, [[0, 128], [1, 1024]]), writes=['npost'], key='c6')
        P.dma('sp', V(sub08, 0, 0, [[128, 4], [1, 128]]),
              bass.AP(subln_d.tensor, 0, [[0, 128], [0, 4], [1, 128]]), writes=['sub08'], key='c7')
        P.ts('dve', sub08[:], sub08[:], 0.8, None, ALU.mult, None, ['sub08'], ['sub08'])

        WAv = [V(WA, 0, i * 4096, [[512, 8], [1, 512]]) for i in range(3)]
        wsrc = lambda wd, cb: bass.AP(wd.tensor, cb * 512, [[3072, 128], [128 * 3072, 8], [1, 512]])
        nblk = [0]

        def load_block(wd, cb):
            i = nblk[0] % 3
            nblk[0] += 1
            P.dma('pool', WAv[i], wsrc(wd, cb), writes=['WA%d' % i], key='wa%d' % i)
            return i

        ADA = [(WA, 'adaA'), (hT, 'adaB')]
        P.dma('pool', V(WA, 0, 0, [[1536, 8], [1, 1536]]),
              bass.AP(wada_d.tensor, 0, [[3072, 128], [128 * 3072, 8], [1, 1536]]),
              writes=['WA0', 'WA1', 'WA2', 'adaA'], key='adaA')
        P.dma('pool', V(hT, 0, 0, [[1536, 8], [1, 1536]]),
              bass.AP(wada_d.tensor, 1536, [[3072, 128], [128 * 3072, 8], [1, 1536]]),
              writes=['adaB'] + ['hT%d' % t for t in range(12)], key='adaB')
        for cb in range(6):
            buf, bres = ADA[cb // 3]
            coff = (cb % 3) * 512
            rds = [bres] + (['WA0', 'WA1', 'WA2'] if cb < 3 else [])
            if cb < 4:
                for nt in range(4):
                    col = (cb * 4 + nt) * 2
                    for j in range(8):
                        P.mm(banks[0][:, col:col + 2], V(buf, 0, j * 1536 + coff + nt * 128, [[1, 128]]),
                             bass.AP(R1b.tensor, R1b.offset + 128 + j, [[R1b.ap[0][0], 128], [8, 2]]),
                             j == 0, j == 7, rds + ['csT'], ['b0'])
            else:
                for cond in range(2):
                    bk = banks[1 + (cb - 4) * 2 + cond]
                    bn = 'b%d' % (1 + (cb - 4) * 2 + cond)
                    for j in range(8):
                        P.mm(bk[:, :], bass.AP(R1b.tensor, R1b.offset + 256 + j * 256 + cond * 128, [[R1b.ap[0][0], 128], [1, 128]]),
                             V(buf, 0, j * 1536 + coff, [[1, 512]]), j == 0, j == 7, rds + ['csbc'], [bn])
                    h0_ = (cb - 4) * 512
                    P.tt('dve', gn[:, cond, h0_:h0_ + 512], bk[:, :], V(R1, 0, 2560 + h0_, [[1, 512]]), ALU.add,
                         [bn, 'bgate'], ['gn'])
                    P.tt('pool', gn[:, cond, h0_:h0_ + 512], gn[:, cond, h0_:h0_ + 512], V(R1, 0, 3584 + h0_, [[1, 512]]),
                         ALU.mult, ['gn', 'npost'], ['gn'])
        CB_ORDER = [int(c) for c in os.environ.get("CBO", "450123")]
        win_blk = {}

        def issue_win(k):
            if k < 6 and CB_ORDER[k] not in win_blk:
                win_blk[CB_ORDER[k]] = load_block(win_d, CB_ORDER[k])
        b0v = lambda off: V(banks[0], 0, off, [[2, 8], [1, 2]])
        P.tt('dve', shf[:], b0v(0), V(R1, 0, 2400, [[1, 8], [0, 2]]), ALU.add, ['b0', 'bsh'], ['shf'])
        P.tt('dve', gm[:], b0v(16), V(R1, 0, 2408, [[1, 8], [0, 2]]), ALU.add, ['b0', 'bsc'], ['gm'])
        P.ts('dve', gm[:], gm[:], 1.0, None, ALU.add, None, ['gm'], ['gm'])
        P.tt('dve', gm[:], gm[:], V(R1, 0, 2416, [[1, 8], [0, 2]]), ALU.mult, ['gm', 'npf'], ['gm'])

        lqb = V(R1, 0, 4608, [[1, 256]])
        P.dma('sp', lqb, bass.AP(lq_d.tensor, 0, [[0, 128], [1, 256]]), writes=['lqb'], key='c8')
        lpr = V(R1, 0, 4864, [[64, 2], [1, 64]])
        P.tt('dve', lpr, V(R1, 0, 4608, [[128, 2], [1, 64]]), V(R1, 0, 4672, [[128, 2], [1, 64]]), ALU.mult,
             ['lqb'], ['lpr'])
        P.rsum(stat[:, 0:2], lpr, ['lpr'], ['st_l'])
        P.act(stat[:, 2:4], stat[:, 0:2], AF.Exp, ['st_l'], ['st_l2'])
        P.tt('dve', stat[:, 4:5], stat[:, 3:4], stat[:, 2:3], ALU.subtract, ['st_l2'], ['st_l3'])
        P.ts('dve', neglam[:], stat[:, 4:5], -0.2, None, ALU.add, None, ['st_l3'], ['neglam'])

        P.dma('sp', dsk[:], dsk_d.rearrange("(j p) -> p j", p=128), writes=['dsk'], key='c9',
              allow_slow_non_contiguous=True)
        P.dma('sp', bglu[:], bglu_d.rearrange("(j p) -> p j", p=128), writes=['bglu'], key='c10',
              allow_slow_non_contiguous=True)
        P.dma('sp', ropec[:], ropec_d.rearrange("(t p) f -> p t f", p=128), writes=['ropec'], key='c11')
        P.dma('sp', ropes[:], ropes_d.rearrange("(t p) f -> p t f", p=128), writes=['ropes'], key='c12')


        def p1a_stats(tt):
            xb_ = xt[tt % 2]
            xn_ = xn[tt % 2]
            xr, xnr = 'xt%d' % (tt % 2), 'xn%d' % (tt % 2)
            P.dma('sp', xb_[:], x_d[tt * 128:(tt + 1) * 128, :], writes=[xr], key=xr)
            P.act(junk[:], xb_[:], AF.Square, [xr], ['junk', 'ssq%d' % tt], accum=stat[:, 8 + tt:9 + tt])
            P.act(stat[:, 24 + tt:25 + tt], stat[:, 8 + tt:9 + tt], AF.Sqrt, ['ssq%d' % tt, 'epst'], ['sd%d' % tt],
                  bias=epst[:], scale=1.0 / 1024.0)
            P.recip(stat[:, 40 + tt:41 + tt], stat[:, 24 + tt:25 + tt], ['sd%d' % tt], ['rs%d' % tt])
            P.ts('dve', xn_[:], xb_[:], stat[:, 40 + tt:41 + tt], None, ALU.mult, None, [xr, 'rs%d' % tt], [xnr])

        def p1a_tr(tt):
            cond = 1 if tt < 8 else 0
            xn_ = xn[tt % 2]
            xnr = 'xn%d' % (tt % 2)
            bn = 'b%d' % (tt % 2)
            for j in range(8):
                P.tr(bkb[tt % 2][:, j * 128:(j + 1) * 128], xn_[:, j * 128:(j + 1) * 128], identb[:],
                     [xnr, 'identb'], [bn])
            for j in range(8):
                dst = V(hT, 0, j * NTOK + tt * 128, [[1, 128]])
                if j % 4 != 3:
                    P.act(dst, bkb[tt % 2][:, j * 128:(j + 1) * 128], AF.Identity, [bn, 'gm', 'shf'], ['hT%d' % tt],
                          bias=shf[:, j, cond:cond + 1], scale=gm[:, j, cond:cond + 1])
                else:
                    P.ts('dve', dst, bkb[tt % 2][:, j * 128:(j + 1) * 128], gm[:, j, cond:cond + 1],
                         shf[:, j, cond:cond + 1], ALU.mult, ALU.add, [bn, 'gm', 'shf'], ['hT%d' % tt])

        p1a_stats(0)
        for tt in range(12):
            if tt + 1 < 12:
                p1a_stats(tt + 1)
            p1a_tr(tt)

        KSTOP = os.environ.get('KSTOP', '')
        if KSTOP == '1a':
            P.finish(); P.emit(); return nc
        SHf = SHb[:].bitcast(F32)
        Xf = Xs[:].bitcast(F32)
        WYf = W_Y[:].bitcast(F32)

        def mk(apb):
            def f(row, col, dims, nrows=128):
                ps = apb.ap[0][0]
                return bass.AP(apb.tensor, apb.offset + row * ps + col, [[ps, nrows]] + [list(d) for d in dims])
            return f
        SF, XF, WYF = mk(SHf), mk(Xf), mk(WYf)
        WSF = mk(W_S[:].bitcast(F32))
        SBF = mk(SHb[:])
        XBF = mk(Xs[:])
        setup_q = []
        P.capture = setup_q
        G = 32
        SM = 0
        for d in range(2):
            P.dma('sp', WSF(0, SM + d * 64, [[1, 64]], G), are_d[d], writes=['are'], key='c14')
            P.dma('sp', WSF(0, SM + 128 + d * 64, [[1, 64]], G), aim_d[d], writes=['aim'], key='c15')
        P.dma('sp', WSF(0, SM + 256, [[1, 2]], G), ldt_d.rearrange("d g -> g d"), writes=['ldt'], key='c16',
              allow_slow_non_contiguous=True)
        P.act(WSF(0, SM + 258, [[1, 2]], G), WSF(0, SM + 256, [[1, 2]], G), AF.Exp, ['ldt'], ['dtt'])
        dtb = WSF(0, SM + 258, [[1, 2], [0, 64]], G)
        P.tt('pool', WSF(0, SM + 384, [[64, 2], [1, 64]], G), WSF(0, SM, [[64, 2], [1, 64]], G), dtb, ALU.mult,
             ['are', 'dtt'], ['ardt'])
        P.tt('pool', WSF(0, SM + 512, [[64, 2], [1, 64]], G), WSF(0, SM + 128, [[64, 2], [1, 64]], G), dtb, ALU.mult,
             ['aim', 'dtt'], ['thh'])
        for i_, (off, nm) in enumerate(((SM, 'are'), (SM + 128, 'aim'), (SM + 384, 'ardt'), (SM + 512, 'thh'))):
            P.tr(banks[6][:, i_ * 32:(i_ + 1) * 32], WSF(0, off, [[1, 128]], G), ident[0:32, 0:32], [nm, 'ident'], ['b6'])
        PWRE, PWIM, FT = 0, 832, 1664
        PS = 1728
        P.cp('dve', V(R2, 0, PS, [[1, 128]]), banks[6][:, 0:128], ['b6'], ['PSm'])
        NE = NK * 32
        KA = SF(0, 0, [[1, NE]])
        KAi = bass.AP(SHf.tensor, SHf.offset, [[SHf.ap[0][0], 128], [1, NE]]).bitcast(I32)
        ANG = SF(0, NE, [[1, NE]])
        MAG = SF(0, 2 * NE, [[1, NE]])
        P.dma('sp', KA, karr_d, writes=['B0'], key='c13')
        P.tt('dve', SF(0, NE, [[32, NK], [1, 32]]), SF(0, 0, [[32, NK], [1, 32]]), V(R2, 0, PS + 96, [[0, NK], [1, 32]]),
             ALU.mult, ['B0', 'PSm'], ['B1'])
        P.tt('pool', SF(0, 2 * NE, [[32, NK], [1, 32]]), SF(0, 0, [[32, NK], [1, 32]]), V(R2, 0, PS + 64, [[0, NK], [1, 32]]),
             ALU.mult, ['B0', 'PSm'], ['B2'])
        P.act(MAG, MAG, AF.Exp, ['B2'], ['B2'])
        INV2PI = 1.0 / TWO_PI
        for (woff, dn, shift, dstoff) in ((3 * NE, 'B3', 0.0, PWIM), (4 * NE, 'B4', 0.5 * math.pi, PWRE)):
            W = SF(0, woff, [[1, NE]])
            if shift != 0.0:
                P.ts('pool', ANG, ANG, shift, None, ALU.add, None, ['B1'], ['B1'])
            P.ts('dve', KAi, ANG, INV2PI, None, ALU.mult, None, ['B1', 'B0'], ['B0'])
            P.cp('dve', W, KAi, ['B0'], [dn])
            P.stt('dve', W, W, -TWO_PI, ANG, ALU.mult, ALU.add, [dn, 'B1'], [dn])
            P.ts('pool', W, W, 3.14159, -3.14159, ALU.min, ALU.max, [dn], [dn])
            P.act(W, W, AF.Sin, [dn], [dn])
            P.tt('dve', V(R2, 0, dstoff, [[1, NE]]), W, MAG, ALU.mult, [dn, 'B2'], ['PWT'])
        a_re = V(R2, 0, PWRE + K1 * 32, [[1, 32]])
        a_im = V(R2, 0, PWIM + K1 * 32, [[1, 32]])
        Are_ = V(R2, 0, PS, [[1, 32]])
        Aim_ = V(R2, 0, PS + 32, [[1, 32]])
        q = lambda i: SF(0, 5 * NE + i * 32, [[1, 32]])
        P.ts('pool', q(0), a_re, -1.0, None, ALU.add, None, ['PWT'], ['q0'])
        P.tt('pool', q(1), Are_, Are_, ALU.mult, ['PSm'], ['q1'])
        P.tt('pool', q(2), Aim_, Aim_, ALU.mult, ['PSm'], ['q2'])
        P.tt('pool', q(1), q(1), q(2), ALU.add, ['q1', 'q2'], ['q1'])
        P.recip(q(1), q(1), ['q1'], ['q1'])
        P.tt('pool', q(2), q(0), Are_, ALU.mult, ['q0', 'PSm', 'q2'], ['q2'])
        P.tt('pool', q(3), a_im, Aim_, ALU.mult, ['PWT', 'PSm'], ['q3'])
        P.tt('pool', q(2), q(2), q(3), ALU.add, ['q2', 'q3'], ['q2'])
        P.tt('pool', V(R2, 0, FT, [[1, 32]]), q(2), q(1), ALU.mult, ['q2', 'q1'], ['FTt'])
        P.tt('pool', q(2), a_im, Are_, ALU.mult, ['PWT', 'PSm', 'q2'], ['q2'])
        P.tt('pool', q(3), q(0), Aim_, ALU.mult, ['q0', 'PSm', 'q3'], ['q3'])
        P.tt('pool', q(2), q(2), q(3), ALU.subtract, ['q2', 'q3'], ['q2'])
        P.tt('pool', V(R2, 0, FT + 32, [[1, 32]]), q(2), q(1), ALU.mult, ['q2', 'q1'], ['FTt'])
        if KSTOP == '0b1':
            dbg_d = dout("dbg", [128, 1920])
            P.dma('sp', dbg_d, R2[:], reads=['PWT', 'FTt', 'PSm'], key='dbg', final=True)
            P.finish(); P.emit(); return nc
        P.cp('pool', al4[:, 0:32], V(R2, 0, PWRE + K8 * 32, [[1, 32]]), ['PWT'], ['al4'])
        P.cp('pool', al4[:, 96:128], V(R2, 0, PWRE + K8 * 32, [[1, 32]]), ['PWT'], ['al4'])
        P.cp('pool', al4[:, 64:96], V(R2, 0, PWIM + K8 * 32, [[1, 32]]), ['PWT'], ['al4'])
        P.ts('pool', al4[:, 32:64], V(R2, 0, PWIM + K8 * 32, [[1, 32]]), -1.0, None, ALU.mult, None, ['PWT'], ['al4'])

        GM_RES = ['B0', 'B1', 'B2', 'B3', 'B4', 'are', 'aim', 'ldt', 'dtt', 'ardt', 'thh', 'q0', 'q1', 'q2', 'q3', 'q4', 'q5']
        P.memset('pool', stat[:, 60:61], 0.0, GM_RES + ['gdone', 'bar60'])
        BT, CT = 4992, 6016
        for d in range(2):
            for ri, bd in enumerate((bre_d, bim_d)):
                P.dma('sp', V(R1, d * 64, BT + ri * 512, [[16, 32], [1, 16]], 64), bd[d].rearrange("g p h -> p g h"),
                      writes=['Bt'], key='c17')
            for ri, cd in enumerate((cre_d, cim_d)):
                P.dma('sp', XF(0, 2048 + ri * 512 + d * 64, [[128, 4], [1, 64]]),
                      cd[d].rearrange("(gh q) p -> q gh p", gh=4), reads=['gdone'], writes=['Cin'], key='c18')
        fre = V(R2, 0, FT, [[1, 32], [0, 16]])
        fim = V(R2, 0, FT + 32, [[1, 32], [0, 16]])
        Bre = V(R1, 0, BT, [[16, 32], [1, 16]])
        Bim = V(R1, 0, BT + 512, [[16, 32], [1, 16]])
        t1 = XF(0, 1024, [[16, 32], [1, 16]])
        t2 = XF(0, 1536, [[16, 32], [1, 16]])
        P.tt('pool', t1, Bre, fre, ALU.mult, ['Bt', 'FTt', 'gdone'], ['xt1'])
        P.tt('pool', t2, Bim, fim, ALU.mult, ['Bt', 'FTt', 'gdone'], ['xt2'])
        P.tt('pool', XF(0, 0, [[16, 32], [1, 16]]), t1, t2, ALU.subtract, ['xt1', 'xt2', 'gdone'], ['Bb'])
        P.tt('pool', t1, Bim, fre, ALU.mult, ['Bt', 'FTt', 'xt1'], ['xt1'])
        P.tt('pool', t2, Bre, fim, ALU.mult, ['Bt', 'FTt', 'xt2'], ['xt2'])
        P.tt('pool', XF(0, 512, [[16, 32], [1, 16]]), t1, t2, ALU.add, ['xt1', 'xt2', 'gdone'], ['Bb'])
        for ri in range(2):
            bk, bn = (banks[6], 'b6') if ri == 0 else (banks[7], 'b7')
            for gh in range(4):
                P.tr(bk[:, gh * 128:(gh + 1) * 128], XF(0, 2048 + ri * 512 + gh * 128, [[1, 128]]), ident[:],
                     ['Cin', 'ident'], [bn])
            P.cp('act', V(R1, 0, CT + ri * 512, [[1, 512]]), bk[:, :], [bn], ['Ct'])
        P.memset('pool', stat[:, 61:62], 0.0, ['Cin', 'xt1', 'xt2', 'cpfree', 'bar61'])

        if KSTOP == '0b2':
            P.finish(); P.emit(); return nc

        def pwv(off, kset, g0):
            return V(R2, 0, off + kset * 32 + g0, [[1, 16], [32, 8], [0, 16]])

        T1 = SF(0, 0, [[128, 16], [16, 8], [1, 16]])
        T2 = SF(0, 2048, [[128, 16], [16, 8], [1, 16]])
        engs = ['pool', 'dve']
        ei = [0]

        PWSO = 4992
        P.tt('dve', V(R1, 0, PWSO, [[1, NK * 32]]), V(R2, 0, PWRE, [[1, NK * 32]]), V(R2, 0, PWIM, [[1, NK * 32]]), ALU.add,
             ['PWT', 'gdone', 'Bb'], ['PWS', 'Bt'])
        P.tt('dve', WSF(0, 2048, [[1, 512]]), XF(0, 0, [[1, 512]]), XF(0, 512, [[1, 512]]), ALU.add, ['Bb', 'gdone'], ['Gtab'])
        P.tt('dve', WSF(0, 2560, [[1, 512]]), XF(0, 512, [[1, 512]]), XF(0, 0, [[1, 512]]), ALU.subtract, ['Bb', 'gdone'], ['Gtab'])
        P.tt('dve', WSF(0, 3072, [[1, 512]]), V(R1, 0, CT, [[1, 512]]), V(R1, 0, CT + 512, [[1, 512]]), ALU.add, ['Ct', 'gdone'], ['Gtab'])
        P.tt('dve', WSF(0, 3584, [[1, 512]]), V(R1, 0, CT, [[1, 512]]), V(R1, 0, CT + 512, [[1, 512]]), ALU.subtract, ['Ct', 'gdone'], ['Gtab'])

        def pwsv(kset, g0):
            return V(R1, 0, PWSO + kset * 32 + g0, [[1, 16], [32, 8], [0, 16]])

        def cmul(kset, g0, xre, xs, xd, out_re, out_im, neg_im, rn, wn):
            rn = rn + ['PWT', 'PWS', 'Gtab', 'gdone']
            P.tt('dve', T1, xre, pwsv(kset, g0), ALU.mult, rn, ['T1'])
            P.tt('dve', T2, xs, pwv(PWIM, kset, g0), ALU.mult, rn, ['T2'])
            P.tt('dve', out_re, T1, T2, ALU.subtract, ['T1', 'T2'], [wn])
            P.tt('dve', T2, xd, pwv(PWRE, kset, g0), ALU.mult, rn, ['T2'])
            if neg_im:
                P.tt('dve', out_im, T2, T1, ALU.subtract, ['T1', 'T2'], [wn])
            else:
                P.tt('dve', out_im, T1, T2, ALU.add, ['T1', 'T2'], [wn])

        def half_elem(gh2):
            g0 = gh2 * 16
            Bbre = XF(0, g0 * 16, [[16, 16], [0, 8], [1, 16]])
            Bbim = XF(0, 512 + g0 * 16, [[16, 16], [0, 8], [1, 16]])
            Ctre = V(R1, 0, CT + g0 * 16, [[16, 16], [0, 8], [1, 16]])
            Ctim = V(R1, 0, CT + 512 + g0 * 16, [[16, 16], [0, 8], [1, 16]])
            BeRe = SBF(0, 8192, [[128, 16], [1, 8], [8, 16]])
            BeIm = SBF(0, 10240, [[128, 16], [1, 8], [8, 16]])
            P.memset('pool', stat[:, 62:63], 0.0, ['cpfree', 'Cp', 'Be', 'bar62'])
            CpRe = XBF(0, 2048, [[128, 16], [1, 8], [8, 16]])
            CpNi = XBF(0, 4096, [[128, 16], [1, 8], [8, 16]])
            WYre = V(W_Y, 0, g0 * 256, [[256, 16], [1, 8], [8, 16]])
            WYni = V(W_Y, 0, g0 * 256 + 128, [[256, 16], [1, 8], [8, 16]])
            gv = lambda off: WSF(0, off + g0 * 16, [[16, 16], [0, 8], [1, 16]])
            cmul(KE, g0, Bbre, gv(2048), gv(2560), BeRe, BeIm, False, ['Bb'], 'Be')
            cmul(KQ, g0, Ctre, gv(3072), gv(3584), WYre, WYni, True, ['Ct'], 'W_Y')
            cmul(KP, g0, Ctre, gv(3072), gv(3584), CpRe, CpNi, True, ['Ct', 'Bb'], 'Cp')

        def half_pe(gh2):
            g0 = gh2 * 16
            for q4 in range(4):
                bi = 6 + (q4 % 2)
                for gi in range(4):
                    gl = q4 * 4 + gi
                    for ri in range(2):
                        src = SBF(0, (8192 if ri == 0 else 10240) + gl * 128, [[1, 128]])
                        P.tr(bkb[bi][:, (gi * 2 + ri) * 128:(gi * 2 + ri + 1) * 128], src, identb[:],
                             ['Be', 'identb'], ['b%d' % bi])
                P.cp('act', V(W_S, 0, (g0 + q4 * 4) * 256, [[1, 1024]]), bkb[bi][:, :], ['b%d' % bi, 'gdone'], ['W_S', 'Gtab'])
            for qd in range(4):
                for gi in range(4):
                    gl = qd * 4 + gi
                    for d in range(2):
                        o = banks[6 + d][:, gi * 128:(gi + 1) * 128]
                        P.mm(o, SBF(d * 64, 8192 + gl * 128, [[1, 128]], 64), XBF(d * 64, 2048 + gl * 128, [[1, 128]], 64),
                             True, False, ['Be', 'Cp'], ['b%d' % (6 + d)])
                        P.mm(o, SBF(d * 64, 10240 + gl * 128, [[1, 128]], 64), XBF(d * 64, 4096 + gl * 128, [[1, 128]], 64),
                             False, True, ['Be', 'Cp'], ['b%d' % (6 + d)])
                mt0 = SF(0, 0, [[128, 4], [1, 128]])
                mt1 = SF(0, 2048, [[128, 4], [1, 128]])
                P.tt('dve', mt0, V(banks[6], 0, 0, [[128, 4], [1, 128]]), V(R1, 0, 7168, [[0, 4], [1, 128]]), ALU.mult,
                     ['b6', 'mask', 'T2'], ['T1'])
                P.tt('dve', mt1, V(banks[7], 0, 0, [[128, 4], [1, 128]]), V(R1, 0, 7168 + 128, [[0, 4], [1, 128]]), ALU.mult,
                     ['b7', 'mask', 'T1'], ['T2'])
                P.tt('pool', V(M0, 0, (g0 + qd * 4) * 128, [[128, 4], [1, 128]]), mt0, mt1, ALU.add, ['T1', 'T2'], ['M0'])
        half_elem(0)
        half_pe(0)
        half_elem(1)
        half_pe(1)
        P.capture = None
        SETUP_RES = ['csbc', 'csT', 'csf', 'bsh', 'bsc', 'npf', 'bgate', 'npost', 'lqb', 'lpr', 'cT', 'csig']
        P.memset('dve', stat[:, 63:64], 0.0, SETUP_RES + ['r1free', 'bar63'])
        R1bf = mk(R1b)
        qkst = [R1bf(0, i * 512, [[1, 512]]) for i in range(3)]
        tmpA = [V(R1, 0, 768 + i * 512, [[1, 512]]) for i in range(2)]
        tmpB = [V(R1, 0, 1792 + i * 512, [[1, 512]]) for i in range(2)]
        f32st = [V(R1, 0, 2816 + i * 512, [[1, 512]]) for i in range(2)]
        nb = [0]
        nst = [0]
        pbanks = [2, 3, 4, 5]
        for k_cb, cb in enumerate(CB_ORDER):
            for kk in range(k_cb, min(6, k_cb + 3)):
                issue_win(kk)
            i = win_blk[cb]
            wr = 'WA%d' % i
            if cb < 4:
                for tt in range(12):
                    bi = pbanks[nb[0] % 4]
                    nb[0] += 1
                    bn = 'b%d' % bi
                    bk = banks[bi]
                    for j in range(8):
                        P.mm(bk[:, :], V(hT, 0, j * NTOK + tt * 128, [[1, 128]]),
                             V(WA, 0, i * 4096 + j * 512, [[1, 512]]), j == 0, j == 7, ['hT%d' % tt, wr], [bn])
                    si = nst[0] % 3
                    nst[0] += 1
                    st_, sr = qkst[si], 'qkst%d' % si
                    rows = slice(tt * 128, (tt + 1) * 128)
                    if cb < 2:
                        if tt < 8:
                            ti = tt % 2
                            ta, tb_ = tmpA[ti], tmpB[ti]
                            P.tt('dve', V(R1, 0, 768 + ti * 512, [[64, 8], [1, 64]]), V(bk, 0, 0, [[64, 8], [1, 64]]),
                                 V(ropec, 0, tt * 64, [[0, 8], [1, 64]]), ALU.mult, [bn, 'ropec', 'r1free'], ['tmpA%d' % ti])
                            P.tt('dve', V(R1, 0, 1792 + ti * 512, [[64, 8], [32, 2], [1, 16]]),
                                 V(bk, 0, 16, [[64, 8], [32, 2], [1, 16]]),
                                 V(ropes, 0, tt * 64, [[0, 8], [32, 2], [1, 16]]), ALU.mult, [bn, 'ropes', 'r1free'],
                                 ['tmpB%d' % ti])
                            P.tt('dve', V(R1, 0, 1792 + ti * 512 + 16, [[64, 8], [32, 2], [1, 16]]),
                                 V(bk, 0, 0, [[64, 8], [32, 2], [1, 16]]),
                                 V(ropes, 0, tt * 64 + 16, [[0, 8], [32, 2], [1, 16]]), ALU.mult, [bn, 'ropes', 'r1free'],
                                 ['tmpB%d' % ti])
                            P.tt('pool', st_, ta, tb_, ALU.add, ['tmpA%d' % ti, 'tmpB%d' % ti, 'r1free'], [sr])
                        else:
                            P.cp('act', st_, bk[:, :], [bn, 'r1free'], [sr])
                        P.dma('sp', QK_s[rows, cb * 512:(cb + 1) * 512], st_, reads=[sr], writes=['QKs'], key='qks')
                    elif cb == 2:
                        P.cp('act', st_, bk[:, :], [bn, 'r1free'], [sr])
                        P.dma('sp', V_s[rows, :], st_, reads=[sr], writes=['Vs'], key='vs')
                    else:
                        ti = tt % 2
                        P.act(tmpA[ti], bk[:, :], AF.Silu, [bn, 'r1free'], ['tmpA%d' % ti])
                        P.tt('pool', st_, tmpA[ti], sub08[:], ALU.mult, ['tmpA%d' % ti, 'sub08'], [sr])
                        P.dma('sp', GA_s[rows, :], st_, reads=[sr], writes=['GAs'], key='gas')
                    P.run_q(setup_q, 3)
                    if cb in (1, 2) and tt >= 8:
                        fi = tt % 2
                        P.cp('act', f32st[fi], bk[:, :], [bn, 'r1free'], ['f32st%d' % fi])
                        dst = (nk_d if cb == 1 else nv_d)[(tt - 8) * 128:(tt - 7) * 128, :]
                        P.dma('sp', dst, f32st[fi], reads=['f32st%d' % fi], key='o%d' % fi, final=True)
            else:
                dstD = UD if cb == 4 else GSD
                for ct in range(4):
                    for tb in range(3):
                        bi = pbanks[nb[0] % 4]
                        nb[0] += 1
                        bn = 'b%d' % bi
                        bk = banks[bi]
                        for j in range(8):
                            P.mm(bk[:, :], V(WA, 0, i * 4096 + j * 512 + ct * 128, [[1, 128]]),
                                 V(hT, 0, j * NTOK + tb * 512, [[1, 512]]), j == 0, j == 7,
                                 ['hT%d' % t for t in range(tb * 4, tb * 4 + 4)] + [wr], [bn])
                        ui_ = (ct + (0 if cb == 4 else 4)) % 2
                        sr = 'ust%d' % ui_
                        so = V(ust, 0, ui_ * 1536 + tb * 64, [[192, 8], [1, 64]])
                        src = V(bk, 0, 0, [[1, 8], [8, 64]])
                        if cb == 4:
                            P.cp('act', so, src, [bn], [sr])
                        else:
                            P.act(so, src, AF.Silu, [bn], [sr])
                        if tb == 2:
                            P.dma('sp', bass.AP(dstD.tensor, ct * 128 * 8 * NCH, [[8 * NCH, 128], [1, 8 * NCH]]),
                                  V(ust, 0, ui_ * 1536, [[1, 1536]]), reads=[sr],
                                  writes=['UD' if cb == 4 else 'GSD'], key='ud' if cb == 4 else 'gsd')
                        P.run_q(setup_q, 3)
                        P.run_q(setup_q, 3)

        if KSTOP == '1b':
            P.finish(); P.emit(); return nc
        st2 = sb("st2", [128, 64], F32)
        WAb = mk(WA[:])
        WAf = mk(WA[:].bitcast(F32))
        BUF = [R1bf, WAb]
        QTOK, KTOK, KC, VAUG, QT, KT, PTO, GAH = 0, 1024, 2048, 2304, 3604, 4628, 5908, 8212
        OSEQ = 6932
        GAHS = [GAH, 5908]
        O1 = WAf(0, 5514, [[1, 128]])
        OO = WAf(0, 5514 + 128, [[1, 128]])
        att_state = {'init': False, 'ptc': 0, 'stc': 0, 'units': 0, 'fslot': 0}
        pendB = []

        def att_init():
            P.memset('pool', st2[:], 0.0, ['st2'] + [n_ + str(k_) for k_ in range(3) for n_ in ('r0', 'r1', 'r1n', 'ossq', 'osd', 'ors')] + ['fs0', 'fs1', 'fss', 'fsd', 'frs',
                                            'bar2_60', 'bar2_61', 'bar2_62', 'bar2_63'])
            R1_ALL = ['qkst0', 'qkst1', 'qkst2', 'tmpA0', 'tmpA1', 'tmpB0', 'tmpB1', 'f32st0', 'f32st1',
                      'WA0', 'WA1', 'WA2']
            P.memset('pool', st2[:, 63:64], 0.0, R1_ALL + ['r1att', 'bar2_63'])
            for si in range(2):
                P.memset('pool', BUF[si](0, VAUG + 128, [[130, 10], [1, 2]]), 1.0, ['vaug%d' % si, 'r1att'])

        def att_loads(unit, si):
            (tok0, L, hasc), h = unit
            nt = L // 128
            B = BUF[si]
            sx = str(si)
            P.dma('sp', B(0, QTOK, [[128, nt], [1, 128]]),
                  bass.AP(QK_s.tensor, tok0 * 1024 + h * 128, [[1024, 128], [128 * 1024, nt], [1, 128]]),
                  reads=['r1att', 'QKs'], writes=['qtok' + sx], key='aq' + sx)
            P.dma('sp', B(0, KTOK, [[128, nt], [1, 128]]),
                  bass.AP(QK_s.tensor, tok0 * 1024 + 512 + h * 128, [[1024, 128], [128 * 1024, nt], [1, 128]]),
                  reads=['r1att', 'QKs'], writes=['ktok' + sx], key='ak' + sx)
            P.dma('sp', B(0, VAUG, [[130, nt], [1, 128]]),
                  bass.AP(V_s.tensor, tok0 * 512 + h * 128, [[512, 128], [128 * 512, nt], [1, 128]]),
                  reads=['r1att', 'Vs', 'vaug' + sx], writes=['vaug' + sx], key='av' + sx)
            P.dma('sp', B(0, GAHS[si], [[128, nt], [1, 128]]),
                  bass.AP(GA_s.tensor, tok0 * 512 + h * 128, [[512, 128], [128 * 512, nt], [1, 128]]),
                  reads=['r1att', 'GAs'], writes=['gah' + sx], key='ag' + sx)
            if hasc:
                P.dma('pool', B(0, KC, [[128, 2], [1, 128]]),
                      bass.AP(ck_d.tensor, h * 128, [[512, 128], [128 * 512, 2], [1, 128]]),
                      reads=['r1att'], writes=['kc' + sx], key='akc' + sx)
                P.dma('pool', B(0, VAUG + nt * 130, [[130, 2], [1, 128]]),
                      bass.AP(cv_d.tensor, h * 128, [[512, 128], [128 * 512, 2], [1, 128]]),
                      reads=['r1att', 'vaug' + sx], writes=['vaug' + sx], key='avc' + sx)

        def attention(units, pump_fn=None):
            if not att_state['init']:
                att_init()
                att_state['init'] = True
            att_loads(units[0], att_state['units'] % 2)
            for ui, unit in enumerate(units):
                (tok0, L, hasc), h = unit
                si = att_state['units'] % 2
                att_state['units'] += 1
                B = BUF[si]
                sx = str(si)
                nt = L // 128
                nkt = nt + (2 if hasc else 0)
                for t in range(nt):
                    P.tr(bkb[6][:, t * 128:(t + 1) * 128], B(0, QTOK + t * 128, [[1, 128]]), identb[:],
                         ['qtok' + sx, 'identb'], ['b6'])
                P.cp('act', B(0, QT, [[1, L]]), bkb[6][:, 0:L], ['b6', 'r1att'], ['qT' + sx])
                for t in range(nt):
                    P.tr(bkb[7][:, t * 128:(t + 1) * 128], B(0, KTOK + t * 128, [[1, 128]]), identb[:],
                         ['ktok' + sx, 'identb'], ['b7'])
                P.cp('act', B(0, KT, [[1, L]]), bkb[7][:, 0:L], ['b7', 'r1att'], ['kT' + sx])
                if hasc:
                    for t in range(2):
                        P.tr(bkb[6][:, t * 128:(t + 1) * 128], B(0, KC + t * 128, [[1, 128]]), identb[:],
                             ['kc' + sx, 'identb'], ['b6'])
                    P.cp('act', B(0, KT + L, [[1, 256]]), bkb[6][:, 0:256], ['b6', 'r1att'], ['kT' + sx])
                while pendB:
                    pendB.pop(0)()
                if ui + 1 < len(units):
                    att_loads(units[ui + 1], 1 - si)
                its = []
                for qb, q0 in enumerate(range(0, L, 384)):
                    bs = min(384, L - q0)
                    for kt in range(nkt):
                        its.append((qb, q0, bs, bs // 128, kt))
                slots = {}

                def QK(i):
                    qb, q0, bs, nq, kt = its[i]
                    sb_ = att_state['stc'] % 2
                    att_state['stc'] += 1
                    pb_ = att_state['ptc'] % 3
                    att_state['ptc'] += 1
                    slots[i] = (sb_, pb_)
                    for c in range(2):
                        bi = sb_ * 2 + c
                        P.mm(banks[bi][:, 0:bs], B(c * 64, KT + kt * 128, [[1, 128]], 64),
                             B(c * 64, QT + q0, [[1, bs]], 64), True, True, ['kT' + sx, 'qT' + sx], ['b%d' % bi])

                def EXPPV(i):
                    qb, q0, bs, nq, kt = its[i]
                    sb_, pb_ = slots[i]
                    prn = 'PT%d' % pb_
                    pv = (4, 5) if qb % 2 == 0 else (6, 7)
                    P.act(R1bf(0, PTO + pb_ * 768, [[384, 2], [1, bs]]), V(psall, 0, sb_ * 1024, [[512, 2], [1, bs]]), AF.Exp,
                          ['b%d' % (sb_ * 2), 'b%d' % (sb_ * 2 + 1), 'r1att'], [prn], scale=0.125)
                    for c in range(2):
                        for qt in range(nq):
                            P.mm(banks[pv[c]][:, qt * 129:(qt + 1) * 129],
                                 R1bf(0, PTO + pb_ * 768 + c * 384 + qt * 128, [[1, 128]]),
                                 B(0, VAUG + kt * 130, [[1, 129]]), (kt == 0 and qt == 0), (kt == nkt - 1),
                                 [prn, 'vaug' + sx], ['b%d' % pv[c]])
                    if kt == nkt - 1:
                        for qt in range(nq):
                            t = (q0 // 128) + qt
                            k_ = att_state['fslot'] % 3
                            att_state['fslot'] += 1
                            ks = str(k_)
                            c_ = 16 + k_ * 8
                            OOk = WAf(0, 5514 + 128 + k_ * 128, [[1, 128]])
                            b4n, b5n = 'b%d' % pv[0], 'b%d' % pv[1]
                            a0 = banks[pv[0]][:, qt * 129:qt * 129 + 128]
                            a1 = banks[pv[1]][:, qt * 129:qt * 129 + 128]
                            P.recip(st2[:, c_:c_ + 1], banks[pv[0]][:, qt * 129 + 128:qt * 129 + 129], [b4n], ['r0' + ks])
                            P.recip(st2[:, c_ + 1:c_ + 2], banks[pv[1]][:, qt * 129 + 128:qt * 129 + 129], [b5n], ['r1' + ks])
                            P.tt('dve', st2[:, c_ + 2:c_ + 3], st2[:, c_ + 1:c_ + 2], neglam[:], ALU.mult, ['r1' + ks, 'neglam'],
                                 ['r1n' + ks])
                            P.ts('dve', O1, a1, st2[:, c_ + 2:c_ + 3], None, ALU.mult, None, [b5n, 'r1n' + ks, 'r1att'], ['O1'])
                            P.stt('dve', OOk, a0, st2[:, c_:c_ + 1], O1, ALU.mult, ALU.add, [b4n, 'r0' + ks, 'O1', 'r1att'],
                                  ['OO' + ks])
                            P.stt('dve', O1, OOk, 1.0, OOk, ALU.mult, ALU.mult, ['OO' + ks], ['O1', 'ossq' + ks],
                                  accum=st2[:, c_ + 3:c_ + 4])

                            def stageB(t=t, ks=ks, c_=c_, OOk=OOk, h=h, si=si, sx=sx, B=B):
                                P.act(st2[:, c_ + 4:c_ + 5], st2[:, c_ + 3:c_ + 4], AF.Ln, ['ossq' + ks, 'epst'], ['osd' + ks],
                                      bias=epst[:], scale=1.0 / 128.0)
                                P.act(st2[:, c_ + 5:c_ + 6], st2[:, c_ + 4:c_ + 5], AF.Exp, ['osd' + ks], ['ors' + ks], scale=-0.5)
                                P.stt('dve', WAb(0, OSEQ + t * 512 + h * 128, [[1, 128]]), OOk, st2[:, c_ + 5:c_ + 6],
                                      B(0, GAHS[si] + t * 128, [[1, 128]]), ALU.mult, ALU.mult,
                                      ['OO' + ks, 'ors' + ks, 'gah' + sx, 'r1att'], ['oseq'])
                            pendB.append(stageB)

                QK(0)
                for i in range(len(its)):
                    if i + 1 < len(its):
                        QK(i + 1)
                    npend = len(pendB)
                    EXPPV(i)
                    if npend:
                        pendB.pop(0)()
                    if pump_fn is not None:
                        pump_fn(1)
                while len(pendB) > 3:
                    pendB.pop(0)()
                if h == 3:
                    while pendB:
                        pendB.pop(0)()
                    for t in range(nt):
                        for hh in range(4):
                            P.tr(bkb[6][:, hh * 128:(hh + 1) * 128], WAb(0, OSEQ + t * 512 + hh * 128, [[1, 128]]), identb[:],
                                 ['oseq', 'identb'], ['b6'])
                        P.cp('act', V(catT, 0, tok0 + t * 128, [[NTOK, 4], [1, 128]]),
                             bass.AP(bkb[6].tensor, bkb[6].offset, [[bkb[6].ap[0][0], 128], [128, 4], [1, 128]]),
                             ['b6'] + ['hT%d' % i for i in range(12)], ['catA'])

        SEQ_P = [(1024, 256, False), (1280, 256, False)]
        SEQ_S = [(0, 1024, True)]

        P.run_q(setup_q, 100000)
        if KSTOP == '0b':
            P.finish(); P.emit(); return nc
        for pc in (128, 162):
            P.memset('pool', V(Xs, 0, pc, [[196, 32], [1, 2]]), 0.0, ['Xs', 'Bb', 'Cp', 'Cin', 'xt1', 'xt2', 'cpfree'])
        for (c0_, n_, s0_) in [(0, 128, 1), (128, 32, 131), (160, 32, 165)]:
            P.dma('sp', V(Xs, 0, s0_ - 1, [[196, 32], [1, n_]]),
                  bass.AP(UD.tensor, c0_, [[NCH, 128], [128 * NCH, 32], [1, n_]]),
                  reads=['UD'], writes=['Xs', 'Bb', 'Cp', 'Cin', 'xt1', 'xt2', 'cpfree'], key='xs')
        SEG = [(0, 128, 1), (128, 32, 131), (160, 32, 165)]
        P.memset('pool', SHb[:], 0.0, ['SHb', 'Be', 'T1', 'T2', 'M0tmp'])
        for g in range(32):
            bi = pbanks[nb[0] % 4]
            nb[0] += 1
            bn = 'b%d' % bi
            bk = banks[bi]
            for ri in range(2):
                P.mm(bk[:, ri * 196:(ri + 1) * 196], V(W_S, 0, g * 256 + ri * 128, [[1, 128]]),
                     V(Xs, 0, g * 196, [[1, 196]]), True, True, ['W_S', 'Xs'], [bn])
            P.cp('act' if g % 2 else 'dve', V(SHb, 0, g * 396 + 1, [[198, 2], [1, 196]], 64),
                 V(bk, 0, 0, [[196, 2], [1, 196]], 64), [bn], ['SHb'])
            for k_, (c0, n, s0) in enumerate(SEG):
                P.cp('dve' if g % 2 else 'act', V(SHb, 64, g * 396 + s0 + n - 1, [[198, 2], [-1, n]], 64),
                     V(bk, 64, s0 - 1, [[196, 2], [1, n]], 64), [bn], ['SHb'])
        P.dma('pool', V(W_S, 0, 0, [[1024, 8], [1, 1024]]), bass.AP(wout_d.tensor, 0, [[1024, 128], [128 * 1024, 8], [1, 1024]]),
              writes=['W_S', 'wout'], key='wout')
        P.dma('pool', R1bf(0, 13000, [[512, 4], [1, 512]]), bass.AP(wglu_d.tensor, 0, [[512, 128], [128 * 512, 4], [1, 512]]),
              writes=['Ct', 'mask', 'Bt', 'wglu'], key='wglu')
        h0in = V(R2, 0, 0, [[1, 128]], 64)
        for d in range(2):
            P.dma('sp', V(R2, 0, d * 64, [[1, 64]], 64), st_d[d], writes=['h0in', 'PWT', 'FTt'], key='h0')
        P.tr(banks[6][:, 0:64], h0in, ident[0:64, 0:64], ['h0in', 'ident'], ['b6'])
        Zst = V(R2, 0, 256, [[1, 64]])
        T4 = V(R2, 0, 384, [[1, 128]])
        U2 = V(R2, 0, 512, [[1, 64]])
        FIN = V(R2, 0, 640, [[1, 64]])

        H0SB = 576
        P.cp('dve', V(R2, 0, H0SB, [[1, 64]]), banks[6][:, 0:64], ['b6'], ['h0sb'])

        def chain(eng, seg, zero_init, zoff, toff, uoff, fin_off, fin_name, q, tag, split=None):
            c0, n, s0 = seg
            zr = 'Z' + tag
            Z = V(R2, 0, zoff, [[32, 2], [1, 32]])
            Zb = V(R2, 0, zoff, [[0, 2], [32, 2], [1, 32]])
            A4 = V(al4, 0, 0, [[64, 2], [32, 2], [1, 32]])

            def init():
                if zero_init:
                    P.memset(eng, Z, 0.0, [zr])
                else:
                    P.cp(eng, Z, V(R2, 0, H0SB, [[32, 2], [1, 32]]), ['h0sb'], [zr])
                    P.cp(eng, V(SHb, 0, s0 - 1, [[198, 2], [396, 32]]), Z, [zr], ['SHbw' + tag])
            q.append(init)

            def step(i_, en, to, uo):
                tr_, ur = 'T4' + tag + en, 'U' + tag + en
                T = V(R2, 0, to, [[64, 2], [32, 2], [1, 32]])
                Ta = V(R2, 0, to, [[64, 2], [1, 32]])
                Tb = V(R2, 0, to + 32, [[64, 2], [1, 32]])
                U = V(R2, 0, uo, [[32, 2], [1, 32]])
                Sv = V(SHb, 0, s0 + i_, [[198, 2], [396, 32]])
                P.tt(en, T, Zb, A4, ALU.mult, [zr, 'al4'], [tr_])
                P.tt(en, U, Ta, Tb, ALU.add, [tr_], [ur])
                P.tt(en, Z, U, Sv, ALU.add, [ur, 'SHb'], [zr])
                P.cp(en, Sv, Z, [zr], ['SHbw' + tag])
            for i_ in range(n):
                if split is not None and i_ >= split[0]:
                    split[4].append(lambda i_=i_: step(i_, split[1], split[2], split[3]))
                else:
                    q.append(lambda i_=i_: step(i_, eng, toff, uoff))
            if fin_name is not None:
                q.append(lambda: P.cp(eng, V(R2, 0, fin_off, [[1, 64]]), V(R2, 0, zoff, [[1, 64]]), [zr], [fin_name]))

        rec_q = {'dve': [], 'pool': []}

        def fin_out(si_):
            fn = 'fin%d' % si_
            foff = 640 + (si_ - 1) * 64
            fo_off = 768 + (si_ - 1) * 128
            P.tr(banks[7][0:64, 0:128], V(R2, 0, foff, [[1, 64]]), ident[:], [fn, 'ident'], ['b7'])
            P.cp('act', V(R2, 0, fo_off, [[1, 128]], 64), banks[7][0:64, 0:128], ['b7'], ['fo%d' % si_])
            P.dma('sp', bass.AP(ns_d.tensor, (si_ - 1) * 8192, [[64, 64], [4096, 2], [1, 64]]),
                  V(R2, 0, fo_off, [[64, 2], [1, 64]], 64), reads=['fo%d' % si_], key='ons', final=True)

        for si_, seg in ((1, SEG[1]), (2, SEG[2])):
            chain('pool', seg, True, 320, 1024, 1152, 640 + (si_ - 1) * 64, 'fin%d' % si_, rec_q['pool'], 'p')
        SPLIT = int(os.environ.get('SPLIT', '64'))
        chain('dve', SEG[0], False, 256, 384, 512, 0, None, rec_q['dve'], 's',
              split=(SPLIT, 'pool', 1024, 1152, rec_q['pool']))

        def pump(n=1):
            for e_ in ('dve', 'pool'):
                for _ in range(n):
                    if rec_q[e_]:
                        rec_q[e_].pop(0)()

        if KSTOP == '2':
            P.finish(); P.emit(); return nc

        attention([(sq, h) for sq in SEQ_P + SEQ_S for h in range(4)], pump)
        pump(1000)
        fin_out(1)
        fin_out(2)
        if KSTOP == '3':
            P.finish(); P.emit(); return nc
        YDp = dscr("YDp", [32, 128, 196], F32)
        ATT_RES = ['oseq', 'O1', 'OO0', 'OO1', 'OO2'] + [n + s_ for n in ('qtok', 'ktok', 'kc', 'vaug', 'gah', 'qT', 'kT') for s_ in '01']
        P.memset('pool', st2[:, 60:61], 0.0, ATT_RES + ['WA0', 'WA1', 'WA2', 'wafree', 'bar2_60'])
        for ct in range(4):
            P.dma('sp', WAb(0, 6144 + ct * 1536, [[1, 1536]]),
                  bass.AP(UD.tensor, ct * 128 * 8 * NCH, [[8 * NCH, 128], [1, 8 * NCH]]),
                  reads=['UD', 'wafree'], writes=['uTp%d' % ct], key='utp')
            P.dma('sp', WAb(0, ct * 1536, [[1, 1536]]),
                  bass.AP(GSD.tensor, ct * 128 * 8 * NCH, [[8 * NCH, 128], [1, 8 * NCH]]),
                  reads=['GSD', 'wafree'], writes=['gSp%d' % ct], key='gsp')
        for (c0, n, s0) in SEG:
            P.cp('act', R1bf(0, s0 - 1, [[198, 64], [1, n]], 64), V(SHb, 0, s0 - 1, [[198, 64], [1, n]], 64),
                 ['SHb', 'SHbws', 'SHbwp', 'wafree'], ['HAL'])
            P.cp('dve', R1bf(64, s0 - 1, [[198, 64], [1, n]], 64), V(SHb, 64, s0 + n - 2, [[198, 64], [-1, n]], 64),
                 ['SHb', 'SHbws', 'SHbwp', 'wafree'], ['HAL'])
        for g in range(32):
            bi = g % 4
            bn = 'b%d' % bi
            bk = banks[bi]
            P.mm(bk[:, 0:196], V(W_Y, 0, g * 256, [[1, 128]]), R1bf(0, g * 396, [[1, 196]]), True, False,
                 ['W_Y', 'HAL'], [bn])
            P.mm(bk[:, 0:196], V(W_Y, 0, g * 256 + 128, [[1, 128]]), R1bf(0, g * 396 + 198, [[1, 196]]), False, False,
                 ['W_Y', 'HAL'], [bn])
            P.mm(bk[:, 0:196], V(M0, 0, g * 128, [[1, 128]]), V(Xs, 0, g * 196, [[1, 196]]),
                 False, True, ['M0', 'Xs'], [bn])
            yi = g % 3
            yst = V(R2, 0, 1216 + yi * 196, [[1, 196]])
            P.cp('act' if g % 2 else 'dve', yst, bk[:, 0:196], [bn, 'fo1', 'fo2'], ['yst%d' % yi])
            P.dma('sp', YDp[g], yst, reads=['yst%d' % yi], writes=['YD%d' % (g // 8)], key='yd%d' % (g // 8))
        if KSTOP == '4':
            P.finish(); P.emit(); return nc

        SSM_DEAD = ['SHb', 'SHbws', 'SHbwp', 'HAL', 'Xs', 'W_Y', 'M0', 'PT0', 'PT1', 'PT2', 'r1att'] + ATT_RES
        P.memset('pool', st2[:, 62:63], 0.0, SSM_DEAD + ['gfree', 'bar2_62'])
        M0f = mk(M0[:].bitcast(F32))
        WSb = mk(W_S[:])
        ysf = [WYF(0, 0, [[1, 1536]]), WYF(0, 1536, [[1, 1536]]), M0f(0, 0, [[1, 1536]]), V(R1, 0, 0, [[1, 1536]])]
        for ct in range(4):
            yT = SF(0, ct * 1536, [[192, 8], [1, 192]])
            for (c0, n, s0) in SEG:
                P.dma('sp', SF(0, ct * 1536 + c0, [[192, 8], [1, n]]),
                      bass.AP(YDp.tensor, ct * 1024 * 196 + s0 - 1, [[8 * 196, 128], [196, 8], [1, n]]),
                      reads=['YD%d' % ct, 'gfree'], writes=['yT%d' % ct], key='yt%d' % ct)
            P.stt('dve', ysf[ct], WAb(0, 6144 + ct * 1536, [[1, 1536]]), dsk[:, ct:ct + 1], SF(0, ct * 1536, [[1, 1536]]),
                  ALU.mult, ALU.add, ['uTp%d' % ct, 'yT%d' % ct, 'dsk', 'gfree'], ['ysf%d' % ct])
            P.act(ysf[ct], ysf[ct], AF.Gelu, ['ysf%d' % ct], ['ysf%d' % ct])
            P.cp('dve', R1bf(0, 3072 + ct * 1536, [[1, 1536]]), ysf[ct], ['ysf%d' % ct, 'gfree'], ['ysb%d' % ct])
        for nt_ in range(4):
            for sbk in range(4):
                bi = (nt_ * 4 + sbk) % 4
                bn = 'b%d' % bi
                for mt in range(4):
                    P.mm(banks[bi][:, 0:384], R1bf(0, 13000 + mt * 512 + nt_ * 128, [[1, 128]]),
                         R1bf(0, 3072 + mt * 1536 + sbk * 384, [[1, 384]]), mt == 0, mt == 3,
                         ['wglu'] + ['ysb%d' % m for m in range(4)], [bn])
                sgi = (nt_ * 4 + sbk) % 2
                sg = V(R1, 0, 4608 + sgi * 384, [[1, 384]])
                tq = V(R1, 0, 5632 + sgi * 384, [[1, 384]])
                P.act(sg, banks[bi][:, 0:384], AF.Sigmoid, [bn, 'bglu', 'gfree'], ['sg%d' % sgi], bias=bglu[:, nt_:nt_ + 1])
                yv = bass.AP(ysf[nt_].tensor, ysf[nt_].offset + sbk * 384, [[ysf[nt_].ap[0][0], 128], [1, 384]])
                P.tt('dve', tq, yv, sg, ALU.mult, ['ysf%d' % nt_, 'sg%d' % sgi], ['tq%d' % sgi])
                P.tt('dve' if (nt_ * 4 + sbk) % 2 == 0 else 'pool', V(catT, 0, (4 + nt_) * NTOK + 2 * sbk, [[1, 2], [8, 192]]),
                     V(R1, 0, 5632 + sgi * 384, [[192, 2], [1, 192]]), WAb(0, nt_ * 1536 + sbk * 384, [[192, 2], [1, 192]]),
                     ALU.mult, ['tq%d' % sgi, 'gSp%d' % nt_] + ['hT%d' % i for i in range(12)], ['catS'])
        if KSTOP == '5':
            P.finish(); P.emit(); return nc

        P.memset('pool', st2[:, 61:62], 0.0, ['ysb0', 'ysb1', 'ysb2', 'ysb3', 'sg0', 'sg1', 'tq0', 'tq1', 'ysf3', 'r1out', 'bar2_61'])
        xbufs = [xt[0][:], xt[1][:], V(R1, 0, 2048, [[1, 1024]]), V(R1, 0, 3072, [[1, 1024]])]
        for tt in range(12):
            cond = 1 if tt < 8 else 0
            xb_ap = xbufs[tt % 4]
            xr = 'xo%d' % (tt % 4)
            P.dma('sp', xb_ap, x_d[tt * 128:(tt + 1) * 128, :], reads=['r1out'], writes=[xr, 'xt%d' % (tt % 2)], key=xr)
            roff = [0, 1024, 4096, 5120][tt % 4]
            res = V(R1, 0, roff, [[1, 1024]])
            rr = 'res%d' % (tt % 4)
            for cb in range(2):
                bi = (tt % 4) * 2 + cb
                bn = 'b%d' % bi
                for j in range(8):
                    P.mm(banks[bi][:, :], V(catT, 0, j * NTOK + tt * 128, [[1, 128]]),
                         V(W_S, 0, j * 1024 + cb * 512, [[1, 512]]), j == 0, j == 7, ['catA', 'catS', 'wout'], [bn])
                P.act(junk[:, 0:512], banks[bi][:, :], AF.Square, [bn], ['junk', 'fs%d' % cb], accum=st2[:, 8 + cb:9 + cb])
            P.tt('dve', st2[:, 10:11], st2[:, 8:9], st2[:, 9:10], ALU.add, ['fs0', 'fs1'], ['fss'])
            P.act(st2[:, 11:12], st2[:, 10:11], AF.Sqrt, ['fss', 'epst'], ['fsd'], bias=epst[:], scale=1.0 / 1024.0)
            P.recip(st2[:, 12:13], st2[:, 11:12], ['fsd'], ['frs'])
            for cb in range(2):
                bi = (tt % 4) * 2 + cb
                P.stt('dve', V(R1, 0, roff + cb * 512, [[1, 512]]), banks[bi][:, :], st2[:, 12:13],
                      gn[:, cond, cb * 512:(cb + 1) * 512], ALU.mult, ALU.mult, ['b%d' % bi, 'frs', 'gn', 'r1out'], [rr])
            P.tt('dve', xb_ap, xb_ap, res, ALU.add, [xr, rr], [xr])
            P.dma('sp', y_d[tt * 128:(tt + 1) * 128, :], xb_ap, reads=[xr], key='oy%d' % (tt % 4), final=True)
        P.finish()
        P.emit()
    return nc


_NC = None


def _consts():
    t = np.arange(1024)
    row = (t // 64).astype(np.float32)
    col = (t % 64).astype(np.float32)
    inv = (10000.0 ** (-np.arange(16, dtype=np.float32) / 16)).astype(np.float32)
    ar = row[:, None] * inv[None, :]
    ac = col[:, None] * inv[None, :]
    cosr, sinr, cosc, sinc = np.cos(ar), np.sin(ar), np.cos(ac), np.sin(ac)
    ropec = np.concatenate([cosr, cosr, cosc, cosc], axis=1).astype(np.float32)
    ropes = np.concatenate([-sinr, sinr, -sinc, sinc], axis=1).astype(np.float32)
    k = np.zeros((NK, 2), np.float32)
    for i in range(8):
        k[KE + i] = (7 - i, i)
        k[KQ + i] = (i + 1, 8 - i)
        k[KP + i] = (i - 7, -i)
    k[K1] = (1, 1)
    k[K8] = (8, 8)
    karr = np.zeros((2, 64, NK, 32), np.float32)
    karr[0] = k[None, :, 0, None]
    karr[1] = k[None, :, 1, None]
    karr = np.ascontiguousarray(karr.reshape(128, NK * 32))
    s = np.tile(np.arange(8), 16)
    m0 = (s[:, None] <= s[None, :]).astype(np.float32)
    m1 = (s[:, None] >= s[None, :]).astype(np.float32)
    mask = np.concatenate([m0, m1], axis=1)
    return dict(ident=np.eye(128, dtype=np.float32), ropec=ropec, ropes=ropes, karr=karr, mask01=mask)


def kernel(**inp):
    global _NC
    if _NC is None:
        _NC = build()
    f = lambda a: np.ascontiguousarray(np.asarray(a, dtype=np.float32))
    cst = _consts()
    shared = dict(
        w_ada=f(inp['w_ada'][0]), b_ada=f(inp['b_ada'][0]), norm_pre=f(inp['norm_pre'][0]),
        norm_post=f(inp['norm_post'][0]), w_in=f(inp['w_in'][0]), lambda_qk=f(inp['lambda_qk'][0]).reshape(256),
        subln=f(inp['subln'][0]), ssm_A_re=f(inp['ssm_A_re'][0]), ssm_A_im=f(inp['ssm_A_im'][0]),
        ssm_log_dt=f(inp['ssm_log_dt'][0]), ssm_B_re=f(inp['ssm_B_re'][0]), ssm_B_im=f(inp['ssm_B_im'][0]),
        ssm_C_re=f(inp['ssm_C_re'][0]).reshape(2, 512, 64), ssm_C_im=f(inp['ssm_C_im'][0]).reshape(2, 512, 64),
        ssm_D=f(inp['ssm_D'][0]), w_glu=f(inp['w_glu'][0]), b_glu=f(inp['b_glu'][0]), w_out=f(inp['w_out'][0]),
        **cst)
    xp, xs = f(inp['x_prompt']), f(inp['x_sample'])
    in_maps = []
    for c in range(8):
        m = dict(shared)
        m['x'] = np.ascontiguousarray(np.concatenate([xs[c], xp[2 * c], xp[2 * c + 1]], axis=0))
        m['ck'] = f(inp['cache_k'][c, 0]).reshape(256, 512)
        m['cv'] = f(inp['cache_v'][c, 0]).reshape(256, 512)
        m['st'] = f(inp['state_ssm'][c, 0]).reshape(2, 64, 64)
        m['cvec'] = np.ascontiguousarray(np.stack([f(inp['c_ctx']), f(inp['c'][c])], axis=0))
        in_maps.append(m)
    res = run_bass_kernel_spmd(_NC, in_maps, core_ids=list(range(8)))
    R = res.results
    y_p = np.zeros((16, 256, 1024), np.float32)
    y_s = np.zeros((8, 1024, 1024), np.float32)
    nk = np.zeros((16, 1, 256, 4, 128), np.float32)
    nv = np.zeros((16, 1, 256, 4, 128), np.float32)
    ns = np.zeros((16, 1, 2, 2, 32, 64), np.float32)
    for c in range(8):
        y = R[c]['y']
        y_s[c] = y[0:1024]
        y_p[2 * c] = y[1024:1280]
        y_p[2 * c + 1] = y[1280:1536]
        nk[2 * c:2 * c + 2, 0] = R[c]['nk'].reshape(2, 256, 4, 128)
        nv[2 * c:2 * c + 2, 0] = R[c]['nv'].reshape(2, 256, 4, 128)
        ns[2 * c:2 * c + 2, 0] = R[c]['ns'].reshape(2, 2, 2, 32, 64)
    return (y_p, y_s, nk, nv, ns)
```

```python
import contextlib
import math
import os
import numpy as np
import concourse.bass as bass
import concourse.mybir as mybir
from concourse.bass_utils import run_bass_kernel_spmd

F32 = mybir.dt.float32
BF16 = mybir.dt.bfloat16
I32 = mybir.dt.int32
AF = mybir.ActivationFunctionType
ALU = mybir.AluOpType
AX = mybir.AxisListType

NTOK = 1536
NCH = 192
TWO_PI = 2.0 * math.pi
KE, KQ, KP, K1, K8, NK = 0, 8, 16, 24, 25, 26


class Prog:
    ENG = ('pe', 'act', 'dve', 'pool', 'sp')

    def __init__(self, nc):
        self.nc = nc
        self.ops = {e: [] for e in self.ENG}
        self.cnt = {e: 0 for e in self.ENG}
        self.seen = {e: {} for e in self.ENG}
        self.res = {}
        self.dcnt = {}
        self.out_tokens = []
        self.capture = None

    def _deps(self, eng, reads, writes):
        deps = {}

        def add(tok):
            if tok is None:
                return
            s, v = tok
            if deps.get(s, 0) < v:
                deps[s] = v
        for r in reads:
            st = self.res.get(r)
            if st:
                add(st['w'])
        for w in writes:
            st = self.res.get(w)
            if st:
                add(st['w'])
                for s, v in st['r'].items():
                    add((s, v))
        for s, v in deps.items():
            if s == 'pe' and eng == 'pe':
                continue
            if self.seen[eng].get(s, 0) < v:
                self.seen[eng][s] = v
                self.ops[eng].append(('wait', s, v))

    def _commit(self, tok, reads, writes):
        for w in writes:
            self.res[w] = {'w': tok, 'r': {}}
        for r in reads:
            st = self.res.setdefault(r, {'w': None, 'r': {}})
            s, v = tok
            if st['r'].get(s, 0) < v:
                st['r'][s] = v

    @staticmethod
    def _excl(reads, writes):
        isb = lambda n: len(n) == 2 and n[0] == 'b' and n[1].isdigit()
        w = list(writes) + [r for r in reads if isb(r)]
        r = [r for r in reads if not isb(r)]
        return r, w

    def op(self, eng, fn, reads=(), writes=()):
        if self.capture is not None:
            self.capture.append(lambda: self.op(eng, fn, reads, writes))
            return
        reads, writes = self._excl(reads, writes)
        self._deps(eng, reads, writes)
        self.cnt[eng] += 1
        tok = (eng, self.cnt[eng])
        self.ops[eng].append(('op', fn))
        self._commit(tok, reads, writes)

    def dma(self, q, out, in_, reads=(), writes=(), key=None, final=False, **kw):
        if self.capture is not None:
            self.capture.append(lambda: self.dma(q, out, in_, reads, writes, key, final, **kw))
            return
        self._deps(q, reads, writes)
        k = ('dma', key)
        self.dcnt[k] = self.dcnt.get(k, 0) + 16
        tok = (k, self.dcnt[k])
        self.ops[q].append(('dma', out, in_, k, kw))
        self._commit(tok, reads, writes)
        if final:
            self.out_tokens.append(tok)

    def tt(self, eng, out, in0, in1, op, r, w):
        self.op(eng, lambda e: e.tensor_tensor(out=out, in0=in0, in1=in1, op=op), r, w)

    def ts(self, eng, out, in0, s1, s2, op0, op1, r, w):
        if s2 is None:
            self.op(eng, lambda e: e.tensor_scalar(out=out, in0=in0, scalar1=s1, scalar2=None, op0=op0), r, w)
        else:
            self.op(eng, lambda e: e.tensor_scalar(out=out, in0=in0, scalar1=s1, scalar2=s2, op0=op0, op1=op1), r, w)

    def stt(self, eng, out, in0, scalar, in1, op0, op1, r, w, accum=None):
        if accum is None:
            self.op(eng, lambda e: e.scalar_tensor_tensor(out=out, in0=in0, scalar=scalar, in1=in1, op0=op0, op1=op1), r, w)
        else:
            self.op(eng, lambda e: e.scalar_tensor_tensor(out=out, in0=in0, scalar=scalar, in1=in1, op0=op0, op1=op1,
                                                          accum_out=accum), r, w)

    def act(self, out, in_, func, r, w, bias=None, scale=None, accum=None):
        kw = {}
        if bias is not None:
            kw['bias'] = bias
        if scale is not None:
            kw['scale'] = scale
        if accum is not None:
            kw['accum_out'] = accum
        self.op('act', lambda e: e.activation(out=out, in_=in_, func=func, **kw), r, w)

    def cp(self, eng, out, in_, r, w):
        if eng == 'act':
            self.op(eng, lambda e: e.copy(out=out, in_=in_), r, w)
        else:
            self.op(eng, lambda e: e.tensor_copy(out=out, in_=in_), r, w)

    def memset(self, eng, ap, val, w):
        self.op(eng, lambda e: e.memset(ap, val), (), w)

    def mm(self, out, lhsT, rhs, start, stop, r, w):
        self.op('pe', lambda e: e.matmul(out, lhsT=lhsT, rhs=rhs, start=start, stop=stop,
                                         skip_group_check=True), r, w)

    def tr(self, out, in_, ident, r, w):
        self.op('pe', lambda e: e.transpose(out=out, in_=in_, identity=ident), r, w)

    def recip(self, out, in_, r, w):
        self.op('dve', lambda e: e.reciprocal(out=out, in_=in_), r, w)

    def rsum(self, out, in_, r, w):
        self.op('dve', lambda e: e.reduce_sum(out=out, in_=in_, axis=AX.X), r, w)

    def run_q(self, q, n):
        cap, self.capture = self.capture, None
        for _ in range(n):
            if not q:
                break
            q.pop(0)()
        self.capture = cap

    def finish(self):
        last = {}
        for s, v in self.out_tokens:
            last[s] = max(last.get(s, 0), v)
        for s, v in last.items():
            self.ops['sp'].append(('wait', s, v))

    def emit(self):
        nc = self.nc
        keys = list(self.ENG[:4]) + list(self.dcnt.keys())
        with contextlib.ExitStack() as st:
            sems = {}
            for i, k in enumerate(keys):
                sems[k] = st.enter_context(nc.semaphore("s%d" % i))
            block = st.enter_context(nc.Block())

            need = {e_: set() for e_ in self.ENG[:4]}
            for en in self.ENG:
                for item in self.ops[en]:
                    if item[0] == 'wait' and item[1] in need:
                        need[item[1]].add(item[2])
            newidx = {e_: {v: i + 1 for i, v in enumerate(sorted(need[e_]))} for e_ in need}

            def run(engname, e):
                k = 0
                for item in self.ops[engname]:
                    if item[0] == 'wait':
                        v = item[2]
                        if item[1] in newidx:
                            v = newidx[item[1]][v]
                        e.wait_ge(sems[item[1]], v)
                    elif item[0] == 'op':
                        k += 1
                        ins = item[1](e)
                        if k in need[engname]:
                            ins.then_inc(sems[engname], 1)
                    else:
                        _, out, in_, kk, kw = item
                        e.dma_start(out=out, in_=in_, **kw).then_inc(sems[kk], 16)

            @block.tensor
            def _(e):
                run('pe', e)

            @block.scalar
            def _(e):
                run('act', e)

            @block.vector
            def _(e):
                run('dve', e)

            @block.gpsimd
            def _(e):
                run('pool', e)

            @block.sync
            def _(e):
                run('sp', e)


def V(t, row, col, dims, nrows=128):
    a = t[:]
    ps = a.ap[0][0]
    return bass.AP(a.tensor, a.offset + row * ps + col, [[ps, nrows]] + [list(d) for d in dims])


def build():
    nc = bass.Bass("TRN2", target_bir_lowering=False)

    def din(name, shape, dt=F32):
        return nc.dram_tensor(name, list(shape), dt, kind="ExternalInput").ap()

    def dout(name, shape, dt=F32):
        return nc.dram_tensor(name, list(shape), dt, kind="ExternalOutput").ap()

    def dscr(name, shape, dt):
        return nc.dram_tensor(name, list(shape), dt, kind="Internal").ap()

    x_d = din("x", [NTOK, 1024])
    ck_d = din("ck", [256, 512])
    cv_d = din("cv", [256, 512])
    st_d = din("st", [2, 64, 64])
    cvec_d = din("cvec", [2, 1024])
    wada_d = din("w_ada", [1024, 3072])
    bada_d = din("b_ada", [3072])
    npre_d = din("norm_pre", [1024])
    npost_d = din("norm_post", [1024])
    win_d = din("w_in", [1024, 3072])
    lq_d = din("lambda_qk", [256])
    subln_d = din("subln", [128])
    are_d = din("ssm_A_re", [2, 32, 64])
    aim_d = din("ssm_A_im", [2, 32, 64])
    ldt_d = din("ssm_log_dt", [2, 32])
    bre_d = din("ssm_B_re", [2, 32, 64, 16])
    bim_d = din("ssm_B_im", [2, 32, 64, 16])
    cre_d = din("ssm_C_re", [2, 512, 64])
    cim_d = din("ssm_C_im", [2, 512, 64])
    dsk_d = din("ssm_D", [512])
    wglu_d = din("w_glu", [512, 512])
    bglu_d = din("b_glu", [512])
    wout_d = din("w_out", [1024, 1024])
    ident_d = din("ident", [128, 128])
    ropec_d = din("ropec", [1024, 64])
    ropes_d = din("ropes", [1024, 64])
    karr_d = din("karr", [128, NK * 32])
    mask_d = din("mask01", [128, 256])

    y_d = dout("y", [NTOK, 1024])
    nk_d = dout("nk", [512, 512])
    nv_d = dout("nv", [512, 512])
    ns_d = dout("ns", [2, 2, 64, 64])

    QK_s = dscr("QKs", [NTOK, 1024], BF16)
    V_s = dscr("Vs", [NTOK, 512], BF16)
    GA_s = dscr("GAs", [NTOK, 512], BF16)
    UD = dscr("UD", [512, 8, NCH], BF16)
    GSD = dscr("GSD", [512, 8, NCH], BF16)
    YD = dscr("YD", [32, 128, NCH], F32)

    P = Prog(nc)
    es = contextlib.ExitStack()

    def sb(name, shape, dt):
        return es.enter_context(nc.sbuf_tensor("s_" + name, list(shape), dt))

    with es:
        psall = es.enter_context(nc.psum_tensor("psall", [128, 4096], F32))
        banks = [psall[:, i * 512:(i + 1) * 512] for i in range(8)]
        bkb = [b.bitcast(BF16) for b in banks]

        ident = sb("ident", [128, 128], F32)
        identb = sb("identb", [128, 128], BF16)
        epst = sb("epst", [128, 1], F32)
        gm = sb("gm", [128, 8, 2], F32)
        shf = sb("shf", [128, 8, 2], F32)
        gn = sb("gn", [128, 2, 1024], F32)
        sub08 = sb("sub08", [128, 512], F32)
        ropec = sb("ropec", [128, 8, 64], F32)
        ropes = sb("ropes", [128, 8, 64], F32)
        neglam = sb("neglam", [128, 1], F32)
        al4 = sb("al4", [128, 128], F32)
        dsk = sb("dsk", [128, 4], F32)
        bglu = sb("bglu", [128, 4], F32)
        stat = sb("stat", [128, 64], F32)
        W_S = sb("W_S", [128, 32 * 2 * 128], BF16)
        W_Y = sb("W_Y", [128, 32 * 2 * 128], BF16)
        M0 = sb("M0", [128, 32 * 128], BF16)
        hT = sb("hT", [128, 8 * NTOK], BF16)
        catT = hT
        WA = sb("WA", [128, 3 * 4096], BF16)
        xt = [sb("xt%d" % i, [128, 1024], F32) for i in range(2)]
        xn = [sb("xn%d" % i, [128, 1024], BF16) for i in range(2)]
        junk = sb("junk", [128, 1024], BF16)
        R1 = sb("R1", [128, 7680], F32)
        SHb = sb("SHb", [128, 32 * 2 * 198], BF16)
        Xs = sb("Xs", [128, 32 * 196], BF16)
        R2 = sb("R2", [128, 1920], F32)
        ust = sb("ust", [128, 2 * 1536], BF16)

        P.dma('sp', ident[:], ident_d, writes=['ident'], key='c0')
        P.cp('dve', identb[:], ident[:], ['ident'], ['identb'])
        P.dma('sp', V(R1, 0, 7168, [[1, 256]]), mask_d, writes=['mask'], key='c0m')
        P.memset('dve', epst[:], 1e-6, ['epst'])
        P.memset('dve', stat[:], 0.0, ['st_l', 'st_l2', 'st_l3', 'bar60', 'bar61', 'bar62', 'bar63'] + [n % t for t in range(12) for n in ('ssq%d', 'sd%d', 'rs%d')])

        cT = V(R1, 0, 0, [[1, 16]])
        for cond in range(2):
            P.dma('sp', V(R1, 0, cond * 8, [[1, 8]]), cvec_d[cond].rearrange("(j p) -> p j", p=128),
                  writes=['cT'], key='c1', allow_slow_non_contiguous=True)
        csig = V(R1, 0, 16, [[1, 16]])
        P.act(csig, cT, AF.Sigmoid, ['cT'], ['csig'])
        csf = V(R1, 0, 32, [[1, 16]])
        P.tt('dve', csf, cT, csig, ALU.mult, ['cT', 'csig'], ['csf'])
        R1b = R1[:].bitcast(BF16)
        csT = bass.AP(R1b.tensor, R1b.offset + 128, [[R1b.ap[0][0], 128], [1, 16]])
        P.cp('dve', csT, csf, ['csf'], ['csT'])
        csbc = bass.AP(R1b.tensor, R1b.offset + 256, [[R1b.ap[0][0], 128], [256, 8], [128, 2], [1, 128]])
        csf_b = V(R1, 0, 32, [[1, 8], [8, 2], [0, 128]])
        P.cp('dve', csbc, csf_b, ['csf'], ['csbc'])
        bsh = V(R1, 0, 2400, [[1, 8]])
        bsc = V(R1, 0, 2408, [[1, 8]])
        npf = V(R1, 0, 2416, [[1, 8]])
        P.dma('sp', bsh, bada_d[0:1024].rearrange("(j p) -> p j", p=128), writes=['bsh'], key='c2',
              allow_slow_non_contiguous=True)
        P.dma('sp', bsc, bada_d[1024:2048].rearrange("(j p) -> p j", p=128), writes=['bsc'], key='c3',
              allow_slow_non_contiguous=True)
        P.dma('sp', npf, npre_d.rearrange("(j p) -> p j", p=128), writes=['npf'], key='c4',
              allow_slow_non_contiguous=True)
        bgate = V(R1, 0, 2560, [[1, 1024]])
        npost = V(R1, 0, 3584, [[1, 1024]])
        P.dma('sp', bgate, bass.AP(bada_d.tensor, 2048, [[0, 128], [1, 1024]]), writes=['bgate'], key='c5')
        P.dma('sp', npost, bass.AP(npost_d.tensor, 0, [[0, 128], [1, 1024]]), writes=['npost'], key='c6')
        P.dma('sp', V(sub08, 0, 0, [[128, 4], [1, 128]]),
              bass.AP(subln_d.tensor, 0, [[0, 128], [0, 4], [1, 128]]), writes=['sub08'], key='c7')
        P.ts('dve', sub08[:], sub08[:], 0.8, None, ALU.mult, None, ['sub08'], ['sub08'])

        WAv = [V(WA, 0, i * 4096, [[512, 8], [1, 512]]) for i in range(3)]
        wsrc = lambda wd, cb: bass.AP(wd.tensor, cb * 512, [[3072, 128], [128 * 3072, 8], [1, 512]])
        nblk = [0]

        def load_block(wd, cb):
            i = nblk[0] % 3
            nblk[0] += 1
            P.dma('pool', WAv[i], wsrc(wd, cb), writes=['WA%d' % i], key='wa%d' % i)
            return i

        ADA = [(WA, 'adaA'), (hT, 'adaB')]
        P.dma('pool', V(WA, 0, 0, [[1536, 8], [1, 1536]]),
              bass.AP(wada_d.tensor, 0, [[3072, 128], [128 * 3072, 8], [1, 1536]]),
              writes=['WA0', 'WA1', 'WA2', 'adaA'], key='adaA')
        P.dma('pool', V(hT, 0, 0, [[1536, 8], [1, 1536]]),
              bass.AP(wada_d.tensor, 1536, [[3072, 128], [128 * 3072, 8], [1, 1536]]),
              writes=['adaB'] + ['hT%d' % t for t in range(12)], key='adaB')
        for cb in range(6):
            buf, bres = ADA[cb // 3]
            coff = (cb % 3) * 512
            rds = [bres] + (['WA0', 'WA1', 'WA2'] if cb < 3 else [])
            if cb < 4:
                for nt in range(4):
                    col = (cb * 4 + nt) * 2
                    for j in range(8):
                        P.mm(banks[0][:, col:col + 2], V(buf, 0, j * 1536 + coff + nt * 128, [[1, 128]]),
                             bass.AP(R1b.tensor, R1b.offset + 128 + j, [[R1b.ap[0][0], 128], [8, 2]]),
                             j == 0, j == 7, rds + ['csT'], ['b0'])
            else:
                for cond in range(2):
                    bk = banks[1 + (cb - 4) * 2 + cond]
                    bn = 'b%d' % (1 + (cb - 4) * 2 + cond)
                    for j in range(8):
                        P.mm(bk[:, :], bass.AP(R1b.tensor, R1b.offset + 256 + j * 256 + cond * 128, [[R1b.ap[0][0], 128], [1, 128]]),
                             V(buf, 0, j * 1536 + coff, [[1, 512]]), j == 0, j == 7, rds + ['csbc'], [bn])
                    h0_ = (cb - 4) * 512
                    P.tt('dve', gn[:, cond, h0_:h0_ + 512], bk[:, :], V(R1, 0, 2560 + h0_, [[1, 512]]), ALU.add,
                         [bn, 'bgate'], ['gn'])
                    P.tt('pool', gn[:, cond, h0_:h0_ + 512], gn[:, cond, h0_:h0_ + 512], V(R1, 0, 3584 + h0_, [[1, 512]]),
                         ALU.mult, ['gn', 'npost'], ['gn'])
        CB_ORDER = [int(c) for c in os.environ.get("CBO", "450123")]
        win_blk = {}

        def issue_win(k):
            if k < 6 and CB_ORDER[k] not in win_blk:
                win_blk[CB_ORDER[k]] = load_block(win_d, CB_ORDER[k])
        b0v = lambda off: V(banks[0], 0, off, [[2, 8], [1, 2]])
        P.tt('dve', shf[:], b0v(0), V(R1, 0, 2400, [[1, 8], [0, 2]]), ALU.add, ['b0', 'bsh'], ['shf'])
        P.tt('dve', gm[:], b0v(16), V(R1, 0, 2408, [[1, 8], [0, 2]]), ALU.add, ['b0', 'bsc'], ['gm'])
        P.ts('dve', gm[:], gm[:], 1.0, None, ALU.add, None, ['gm'], ['gm'])
        P.tt('dve', gm[:], gm[:], V(R1, 0, 2416, [[1, 8], [0, 2]]), ALU.mult, ['gm', 'npf'], ['gm'])

        lqb = V(R1, 0, 4608, [[1, 256]])
        P.dma('sp', lqb, bass.AP(lq_d.tensor, 0, [[0, 128], [1, 256]]), writes=['lqb'], key='c8')
        lpr = V(R1, 0, 4864, [[64, 2], [1, 64]])
        P.tt('dve', lpr, V(R1, 0, 4608, [[128, 2], [1, 64]]), V(R1, 0, 4672, [[128, 2], [1, 64]]), ALU.mult,
             ['lqb'], ['lpr'])
        P.rsum(stat[:, 0:2], lpr, ['lpr'], ['st_l'])
        P.act(stat[:, 2:4], stat[:, 0:2], AF.Exp, ['st_l'], ['st_l2'])
        P.tt('dve', stat[:, 4:5], stat[:, 3:4], stat[:, 2:3], ALU.subtract, ['st_l2'], ['st_l3'])
        P.ts('dve', neglam[:], stat[:, 4:5], -0.2, None, ALU.add, None, ['st_l3'], ['neglam'])

        P.dma('sp', dsk[:], dsk_d.rearrange("(j p) -> p j", p=128), writes=['dsk'], key='c9',
              allow_slow_non_contiguous=True)
        P.dma('sp', bglu[:], bglu_d.rearrange("(j p) -> p j", p=128), writes=['bglu'], key='c10',
              allow_slow_non_contiguous=True)
        P.dma('sp', ropec[:], ropec_d.rearrange("(t p) f -> p t f", p=128), writes=['ropec'], key='c11')
        P.dma('sp', ropes[:], ropes_d.rearrange("(t p) f -> p t f", p=128), writes=['ropes'], key='c12')


        def p1a_stats(tt):
            xb_ = xt[tt % 2]
            xn_ = xn[tt % 2]
            xr, xnr = 'xt%d' % (tt % 2), 'xn%d' % (tt % 2)
            P.dma('sp', xb_[:], x_d[tt * 128:(tt + 1) * 128, :], writes=[xr], key=xr)
            P.act(junk[:], xb_[:], AF.Square, [xr], ['junk', 'ssq%d' % tt], accum=stat[:, 8 + tt:9 + tt])
            P.act(stat[:, 24 + tt:25 + tt], stat[:, 8 + tt:9 + tt], AF.Sqrt, ['ssq%d' % tt, 'epst'], ['sd%d' % tt],
                  bias=epst[:], scale=1.0 / 1024.0)
            P.recip(stat[:, 40 + tt:41 + tt], stat[:, 24 + tt:25 + tt], ['sd%d' % tt], ['rs%d' % tt])
            P.ts('dve', xn_[:], xb_[:], stat[:, 40 + tt:41 + tt], None, ALU.mult, None, [xr, 'rs%d' % tt], [xnr])

        def p1a_tr(tt):
            cond = 1 if tt < 8 else 0
            xn_ = xn[tt % 2]
            xnr = 'xn%d' % (tt % 2)
            bn = 'b%d' % (tt % 2)
            for j in range(8):
                P.tr(bkb[tt % 2][:, j * 128:(j + 1) * 128], xn_[:, j * 128:(j + 1) * 128], identb[:],
                     [xnr, 'identb'], [bn])
            for j in range(8):
                dst = V(hT, 0, j * NTOK + tt * 128, [[1, 128]])
                if j % 4 != 3:
                    P.act(dst, bkb[tt % 2][:, j * 128:(j + 1) * 128], AF.Identity, [bn, 'gm', 'shf'], ['hT%d' % tt],
                          bias=shf[:, j, cond:cond + 1], scale=gm[:, j, cond:cond + 1])
                else:
                    P.ts('dve', dst, bkb[tt % 2][:, j * 128:(j + 1) * 128], gm[:, j, cond:cond + 1],
                         shf[:, j, cond:cond + 1], ALU.mult, ALU.add, [bn, 'gm', 'shf'], ['hT%d' % tt])

        p1a_stats(0)
        for tt in range(12):
            if tt + 1 < 12:
                p1a_stats(tt + 1)
            p1a_tr(tt)

        KSTOP = os.environ.get('KSTOP', '')
        if KSTOP == '1a':
            P.finish(); P.emit(); return nc
        SHf = SHb[:].bitcast(F32)
        Xf = Xs[:].bitcast(F32)
        WYf = W_Y[:].bitcast(F32)

        def mk(apb):
            def f(row, col, dims, nrows=128):
                ps = apb.ap[0][0]
                return bass.AP(apb.tensor, apb.offset + row * ps + col, [[ps, nrows]] + [list(d) for d in dims])
            return f
        SF, XF, WYF = mk(SHf), mk(Xf), mk(WYf)
        WSF = mk(W_S[:].bitcast(F32))
        SBF = mk(SHb[:])
        XBF = mk(Xs[:])
        setup_q = []
        P.capture = setup_q
        G = 32
        SM = 0
        for d in range(2):
            P.dma('sp', WSF(0, SM + d * 64, [[1, 64]], G), are_d[d], writes=['are'], key='c14')
            P.dma('sp', WSF(0, SM + 128 + d * 64, [[1, 64]], G), aim_d[d], writes=['aim'], key='c15')
        P.dma('sp', WSF(0, SM + 256, [[1, 2]], G), ldt_d.rearrange("d g -> g d"), writes=['ldt'], key='c16',
              allow_slow_non_contiguous=True)
        P.act(WSF(0, SM + 258, [[1, 2]], G), WSF(0, SM + 256, [[1, 2]], G), AF.Exp, ['ldt'], ['dtt'])
        dtb = WSF(0, SM + 258, [[1, 2], [0, 64]], G)
        P.tt('pool', WSF(0, SM + 384, [[64, 2], [1, 64]], G), WSF(0, SM, [[64, 2], [1, 64]], G), dtb, ALU.mult,
             ['are', 'dtt'], ['ardt'])
        P.tt('pool', WSF(0, SM + 512, [[64, 2], [1, 64]], G), WSF(0, SM + 128, [[64, 2], [1, 64]], G), dtb, ALU.mult,
             ['aim', 'dtt'], ['thh'])
        for i_, (off, nm) in enumerate(((SM, 'are'), (SM + 128, 'aim'), (SM + 384, 'ardt'), (SM + 512, 'thh'))):
            P.tr(banks[6][:, i_ * 32:(i_ + 1) * 32], WSF(0, off, [[1, 128]], G), ident[0:32, 0:32], [nm, 'ident'], ['b6'])
        PWRE, PWIM, FT = 0, 832, 1664
        PS = 1728
        P.cp('dve', V(R2, 0, PS, [[1, 128]]), banks[6][:, 0:128], ['b6'], ['PSm'])
        NE = NK * 32
        KA = SF(0, 0, [[1, NE]])
        KAi = bass.AP(SHf.tensor, SHf.offset, [[SHf.ap[0][0], 128], [1, NE]]).bitcast(I32)
        ANG = SF(0, NE, [[1, NE]])
        MAG = SF(0, 2 * NE, [[1, NE]])
        P.dma('sp', KA, karr_d, writes=['B0'], key='c13')
        P.tt('dve', SF(0, NE, [[32, NK], [1, 32]]), SF(0, 0, [[32, NK], [1, 32]]), V(R2, 0, PS + 96, [[0, NK], [1, 32]]),
             ALU.mult, ['B0', 'PSm'], ['B1'])
        P.tt('pool', SF(0, 2 * NE, [[32, NK], [1, 32]]), SF(0, 0, [[32, NK], [1, 32]]), V(R2, 0, PS + 64, [[0, NK], [1, 32]]),
             ALU.mult, ['B0', 'PSm'], ['B2'])
        P.act(MAG, MAG, AF.Exp, ['B2'], ['B2'])
        INV2PI = 1.0 / TWO_PI
        for (woff, dn, shift, dstoff) in ((3 * NE, 'B3', 0.0, PWIM), (4 * NE, 'B4', 0.5 * math.pi, PWRE)):
            W = SF(0, woff, [[1, NE]])
            if shift != 0.0:
                P.ts('pool', ANG, ANG, shift, None, ALU.add, None, ['B1'], ['B1'])
            P.ts('dve', KAi, ANG, INV2PI, None, ALU.mult, None, ['B1', 'B0'], ['B0'])
            P.cp('dve', W, KAi, ['B0'], [dn])
            P.stt('dve', W, W, -TWO_PI, ANG, ALU.mult, ALU.add, [dn, 'B1'], [dn])
            P.ts('pool', W, W, 3.14159, -3.14159, ALU.min, ALU.max, [dn], [dn])
            P.act(W, W, AF.Sin, [dn], [dn])
            P.tt('dve', V(R2, 0, dstoff, [[1, NE]]), W, MAG, ALU.mult, [dn, 'B2'], ['PWT'])
        a_re = V(R2, 0, PWRE + K1 * 32, [[1, 32]])
        a_im = V(R2, 0, PWIM + K1 * 32, [[1, 32]])
        Are_ = V(R2, 0, PS, [[1, 32]])
        Aim_ = V(R2, 0, PS + 32, [[1, 32]])
        q = lambda i: SF(0, 5 * NE + i * 32, [[1, 32]])
        P.ts('pool', q(0), a_re, -1.0, None, ALU.add, None, ['PWT'], ['q0'])
        P.tt('pool', q(1), Are_, Are_, ALU.mult, ['PSm'], ['q1'])
        P.tt('pool', q(2), Aim_, Aim_, ALU.mult, ['PSm'], ['q2'])
        P.tt('pool', q(1), q(1), q(2), ALU.add, ['q1', 'q2'], ['q1'])
        P.recip(q(1), q(1), ['q1'], ['q1'])
        P.tt('pool', q(2), q(0), Are_, ALU.mult, ['q0', 'PSm', 'q2'], ['q2'])
        P.tt('pool', q(3), a_im, Aim_, ALU.mult, ['PWT', 'PSm'], ['q3'])
        P.tt('pool', q(2), q(2), q(3), ALU.add, ['q2', 'q3'], ['q2'])
        P.tt('pool', V(R2, 0, FT, [[1, 32]]), q(2), q(1), ALU.mult, ['q2', 'q1'], ['FTt'])
        P.tt('pool', q(2), a_im, Are_, ALU.mult, ['PWT', 'PSm', 'q2'], ['q2'])
        P.tt('pool', q(3), q(0), Aim_, ALU.mult, ['q0', 'PSm', 'q3'], ['q3'])
        P.tt('pool', q(2), q(2), q(3), ALU.subtract, ['q2', 'q3'], ['q2'])
        P.tt('pool', V(R2, 0, FT + 32, [[1, 32]]), q(2), q(1), ALU.mult, ['q2', 'q1'], ['FTt'])
        if KSTOP == '0b1':
            dbg_d = dout("dbg", [128, 1920])
            P.dma('sp', dbg_d, R2[:], reads=['PWT', 'FTt', 'PSm'], key='dbg', final=True)
            P.finish(); P.emit(); return nc
        P.cp('pool', al4[:, 0:32], V(R2, 0, PWRE + K8 * 32, [[1, 32]]), ['PWT'], ['al4'])
        P.cp('pool', al4[:, 96:128], V(R2, 0, PWRE + K8 * 32, [[1, 32]]), ['PWT'], ['al4'])
        P.cp('pool', al4[:, 64:96], V(R2, 0, PWIM + K8 * 32, [[1, 32]]), ['PWT'], ['al4'])
        P.ts('pool', al4[:, 32:64], V(R2, 0, PWIM + K8 * 32, [[1, 32]]), -1.0, None, ALU.mult, None, ['PWT'], ['al4'])

        GM_RES = ['B0', 'B1', 'B2', 'B3', 'B4', 'are', 'aim', 'ldt', 'dtt', 'ardt', 'thh', 'q0', 'q1', 'q2', 'q3', 'q4', 'q5']
        P.memset('pool', stat[:, 60:61], 0.0, GM_RES + ['gdone', 'bar60'])
        BT, CT = 4992, 6016
        for d in range(2):
            for ri, bd in enumerate((bre_d, bim_d)):
                P.dma('sp', V(R1, d * 64, BT + ri * 512, [[16, 32], [1, 16]], 64), bd[d].rearrange("g p h -> p g h"),
                      writes=['Bt'], key='c17')
            for ri, cd in enumerate((cre_d, cim_d)):
                P.dma('sp', XF(0, 2048 + ri * 512 + d * 64, [[128, 4], [1, 64]]),
                      cd[d].rearrange("(gh q) p -> q gh p", gh=4), reads=['gdone'], writes=['Cin'], key='c18')
        fre = V(R2, 0, FT, [[1, 32], [0, 16]])
        fim = V(R2, 0, FT + 32, [[1, 32], [0, 16]])
        Bre = V(R1, 0, BT, [[16, 32], [1, 16]])
        Bim = V(R1, 0, BT + 512, [[16, 32], [1, 16]])
        t1 = XF(0, 1024, [[16, 32], [1, 16]])
        t2 = XF(0, 1536, [[16, 32], [1, 16]])
        P.tt('pool', t1, Bre, fre, ALU.mult, ['Bt', 'FTt', 'gdone'], ['xt1'])
        P.tt('pool', t2, Bim, fim, ALU.mult, ['Bt', 'FTt', 'gdone'], ['xt2'])
        P.tt('pool', XF(0, 0, [[16, 32], [1, 16]]), t1, t2, ALU.subtract, ['xt1', 'xt2', 'gdone'], ['Bb'])
        P.tt('pool', t1, Bim, fre, ALU.mult, ['Bt', 'FTt', 'xt1'], ['xt1'])
        P.tt('pool', t2, Bre, fim, ALU.mult, ['Bt', 'FTt', 'xt2'], ['xt2'])
        P.tt('pool', XF(0, 512, [[16, 32], [1, 16]]), t1, t2, ALU.add, ['xt1', 'xt2', 'gdone'], ['Bb'])
        for ri in range(2):
            bk, bn = (banks[6], 'b6') if ri == 0 else (banks[7], 'b7')
            for gh in range(4):
                P.tr(bk[:, gh * 128:(gh + 1) * 128], XF(0, 2048 + ri * 512 + gh * 128, [[1, 128]]), ident[:],
                     ['Cin', 'ident'], [bn])
            P.cp('act', V(R1, 0, CT + ri * 512, [[1, 512]]), bk[:, :], [bn], ['Ct'])
        P.memset('pool', stat[:, 61:62], 0.0, ['Cin', 'xt1', 'xt2', 'cpfree', 'bar61'])

        if KSTOP == '0b2':
            P.finish(); P.emit(); return nc

        def pwv(off, kset, g0):
            return V(R2, 0, off + kset * 32 + g0, [[1, 16], [32, 8], [0, 16]])

        T1 = SF(0, 0, [[128, 16], [16, 8], [1, 16]])
        T2 = SF(0, 2048, [[128, 16], [16, 8], [1, 16]])
        engs = ['pool', 'dve']
        ei = [0]

        PWSO = 4992
        P.tt('dve', V(R1, 0, PWSO, [[1, NK * 32]]), V(R2, 0, PWRE, [[1, NK * 32]]), V(R2, 0, PWIM, [[1, NK * 32]]), ALU.add,
             ['PWT', 'gdone', 'Bb'], ['PWS', 'Bt'])
        P.tt('dve', WSF(0, 2048, [[1, 512]]), XF(0, 0, [[1, 512]]), XF(0, 512, [[1, 512]]), ALU.add, ['Bb', 'gdone'], ['Gtab'])
        P.tt('dve', WSF(0, 2560, [[1, 512]]), XF(0, 512, [[1, 512]]), XF(0, 0, [[1, 512]]), ALU.subtract, ['Bb', 'gdone'], ['Gtab'])
        P.tt('dve', WSF(0, 3072, [[1, 512]]), V(R1, 0, CT, [[1, 512]]), V(R1, 0, CT + 512, [[1, 512]]), ALU.add, ['Ct', 'gdone'], ['Gtab'])
        P.tt('dve', WSF(0, 3584, [[1, 512]]), V(R1, 0, CT, [[1, 512]]), V(R1, 0, CT + 512, [[1, 512]]), ALU.subtract, ['Ct', 'gdone'], ['Gtab'])

        def pwsv(kset, g0):
            return V(R1, 0, PWSO + kset * 32 + g0, [[1, 16], [32, 8], [0, 16]])

        def cmul(kset, g0, xre, xs, xd, out_re, out_im, neg_im, rn, wn):
            rn = rn + ['PWT', 'PWS', 'Gtab', 'gdone']
            P.tt('dve', T1, xre, pwsv(kset, g0), ALU.mult, rn, ['T1'])
            P.tt('dve', T2, xs, pwv(PWIM, kset, g0), ALU.mult, rn, ['T2'])
            P.tt('dve', out_re, T1, T2, ALU.subtract, ['T1', 'T2'], [wn])
            P.tt('dve', T2, xd, pwv(PWRE, kset, g0), ALU.mult, rn, ['T2'])
            if neg_im:
                P.tt('dve', out_im, T2, T1, ALU.subtract, ['T1', 'T2'], [wn])
            else:
                P.tt('dve', out_im, T1, T2, ALU.add, ['T1', 'T2'], [wn])

        def half_elem(gh2):
            g0 = gh2 * 16
            Bbre = XF(0, g0 * 16, [[16, 16], [0, 8], [1, 16]])
            Bbim = XF(0, 512 + g0 * 16, [[16, 16], [0, 8], [1, 16]])
            Ctre = V(R1, 0, CT + g0 * 16, [[16, 16], [0, 8], [1, 16]])
            Ctim = V(R1, 0, CT + 512 + g0 * 16, [[16, 16], [0, 8], [1, 16]])
            BeRe = SBF(0, 8192, [[128, 16], [1, 8], [8, 16]])
            BeIm = SBF(0, 10240, [[128, 16], [1, 8], [8, 16]])
            P.memset('pool', stat[:, 62:63], 0.0, ['cpfree', 'Cp', 'Be', 'bar62'])
            CpRe = XBF(0, 2048, [[128, 16], [1, 8], [8, 16]])
            CpNi = XBF(0, 4096, [[128, 16], [1, 8], [8, 16]])
            WYre = V(W_Y, 0, g0 * 256, [[256, 16], [1, 8], [8, 16]])
            WYni = V(W_Y, 0, g0 * 256 + 128, [[256, 16], [1, 8], [8, 16]])
            gv = lambda off: WSF(0, off + g0 * 16, [[16, 16], [0, 8], [1, 16]])
            cmul(KE, g0, Bbre, gv(2048), gv(2560), BeRe, BeIm, False, ['Bb'], 'Be')
            cmul(KQ, g0, Ctre, gv(3072), gv(3584), WYre, WYni, True, ['Ct'], 'W_Y')
            cmul(KP, g0, Ctre, gv(3072), gv(3584), CpRe, CpNi, True, ['Ct', 'Bb'], 'Cp')

        def half_pe(gh2):
            g0 = gh2 * 16
            for q4 in range(4):
                bi = 6 + (q4 % 2)
                for gi in range(4):
                    gl = q4 * 4 + gi
                    for ri in range(2):
                        src = SBF(0, (8192 if ri == 0 else 10240) + gl * 128, [[1, 128]])
                        P.tr(bkb[bi][:, (gi * 2 + ri) * 128:(gi * 2 + ri + 1) * 128], src, identb[:],
                             ['Be', 'identb'], ['b%d' % bi])
                P.cp('act', V(W_S, 0, (g0 + q4 * 4) * 256, [[1, 1024]]), bkb[bi][:, :], ['b%d' % bi, 'gdone'], ['W_S', 'Gtab'])
            for qd in range(4):
                for gi in range(4):
                    gl = qd * 4 + gi
                    for d in range(2):
                        o = banks[6 + d][:, gi * 128:(gi + 1) * 128]
                        P.mm(o, SBF(d * 64, 8192 + gl * 128, [[1, 128]], 64), XBF(d * 64, 2048 + gl * 128, [[1, 128]], 64),
                             True, False, ['Be', 'Cp'], ['b%d' % (6 + d)])
                        P.mm(o, SBF(d * 64, 10240 + gl * 128, [[1, 128]], 64), XBF(d * 64, 4096 + gl * 128, [[1, 128]], 64),
                             False, True, ['Be', 'Cp'], ['b%d' % (6 + d)])
                mt0 = SF(0, 0, [[128, 4], [1, 128]])
                mt1 = SF(0, 2048, [[128, 4], [1, 128]])
                P.tt('dve', mt0, V(banks[6], 0, 0, [[128, 4], [1, 128]]), V(R1, 0, 7168, [[0, 4], [1, 128]]), ALU.mult,
                     ['b6', 'mask', 'T2'], ['T1'])
                P.tt('dve', mt1, V(banks[7], 0, 0, [[128, 4], [1, 128]]), V(R1, 0, 7168 + 128, [[0, 4], [1, 128]]), ALU.mult,
                     ['b7', 'mask', 'T1'], ['T2'])
                P.tt('pool', V(M0, 0, (g0 + qd * 4) * 128, [[128, 4], [1, 128]]), mt0, mt1, ALU.add, ['T1', 'T2'], ['M0'])
        half_elem(0)
        half_pe(0)
        half_elem(1)
        half_pe(1)
        P.capture = None
        SETUP_RES = ['csbc', 'csT', 'csf', 'bsh', 'bsc', 'npf', 'bgate', 'npost', 'lqb', 'lpr', 'cT', 'csig']
        P.memset('dve', stat[:, 63:64], 0.0, SETUP_RES + ['r1free', 'bar63'])
        R1bf = mk(R1b)
        qkst = [R1bf(0, i * 512, [[1, 512]]) for i in range(3)]
        tmpA = [V(R1, 0, 768 + i * 512, [[1, 512]]) for i in range(2)]
        tmpB = [V(R1, 0, 1792 + i * 512, [[1, 512]]) for i in range(2)]
        f32st = [V(R1, 0, 2816 + i * 512, [[1, 512]]) for i in range(2)]
        nb = [0]
        nst = [0]
        pbanks = [2, 3, 4, 5]
        for k_cb, cb in enumerate(CB_ORDER):
            for kk in range(k_cb, min(6, k_cb + 3)):
                issue_win(kk)
            i = win_blk[cb]
            wr = 'WA%d' % i
            if cb < 4:
                for tt in range(12):
                    bi = pbanks[nb[0] % 4]
                    nb[0] += 1
                    bn = 'b%d' % bi
                    bk = banks[bi]
                    for j in range(8):
                        P.mm(bk[:, :], V(hT, 0, j * NTOK + tt * 128, [[1, 128]]),
                             V(WA, 0, i * 4096 + j * 512, [[1, 512]]), j == 0, j == 7, ['hT%d' % tt, wr], [bn])
                    si = nst[0] % 3
                    nst[0] += 1
                    st_, sr = qkst[si], 'qkst%d' % si
                    rows = slice(tt * 128, (tt + 1) * 128)
                    if cb < 2:
                        if tt < 8:
                            ti = tt % 2
                            ta, tb_ = tmpA[ti], tmpB[ti]
                            P.tt('dve', V(R1, 0, 768 + ti * 512, [[64, 8], [1, 64]]), V(bk, 0, 0, [[64, 8], [1, 64]]),
                                 V(ropec, 0, tt * 64, [[0, 8], [1, 64]]), ALU.mult, [bn, 'ropec', 'r1free'], ['tmpA%d' % ti])
                            P.tt('dve', V(R1, 0, 1792 + ti * 512, [[64, 8], [32, 2], [1, 16]]),
                                 V(bk, 0, 16, [[64, 8], [32, 2], [1, 16]]),
                                 V(ropes, 0, tt * 64, [[0, 8], [32, 2], [1, 16]]), ALU.mult, [bn, 'ropes', 'r1free'],
                                 ['tmpB%d' % ti])
                            P.tt('dve', V(R1, 0, 1792 + ti * 512 + 16, [[64, 8], [32, 2], [1, 16]]),
                                 V(bk, 0, 0, [[64, 8], [32, 2], [1, 16]]),
                                 V(ropes, 0, tt * 64 + 16, [[0, 8], [32, 2], [1, 16]]), ALU.mult, [bn, 'ropes', 'r1free'],
                                 ['tmpB%d' % ti])
                            P.tt('pool', st_, ta, tb_, ALU.add, ['tmpA%d' % ti, 'tmpB%d' % ti, 'r1free'], [sr])
                        else:
                            P.cp('act', st_, bk[:, :], [bn, 'r1free'], [sr])
                        P.dma('sp', QK_s[rows, cb * 512:(cb + 1) * 512], st_, reads=[sr], writes=['QKs'], key='qks')
                    elif cb == 2:
                        P.cp('act', st_, bk[:, :], [bn, 'r1free'], [sr])
                        P.dma('sp', V_s[rows, :], st_, reads=[sr], writes=['Vs'], key='vs')
                    else:
                        ti = tt % 2
                        P.act(tmpA[ti], bk[:, :], AF.Silu, [bn, 'r1free'], ['tmpA%d' % ti])
                        P.tt('pool', st_, tmpA[ti], sub08[:], ALU.mult, ['tmpA%d' % ti, 'sub08'], [sr])
                        P.dma('sp', GA_s[rows, :], st_, reads=[sr], writes=['GAs'], key='gas')
                    P.run_q(setup_q, 3)
                    if cb in (1, 2) and tt >= 8:
                        fi = tt % 2
                        P.cp('act', f32st[fi], bk[:, :], [bn, 'r1free'], ['f32st%d' % fi])
                        dst = (nk_d if cb == 1 else nv_d)[(tt - 8) * 128:(tt - 7) * 128, :]
                        P.dma('sp', dst, f32st[fi], reads=['f32st%d' % fi], key='o%d' % fi, final=True)
            else:
                dstD = UD if cb == 4 else GSD
                for ct in range(4):
                    for tb in range(3):
                        bi = pbanks[nb[0] % 4]
                        nb[0] += 1
                        bn = 'b%d' % bi
                        bk = banks[bi]
                        for j in range(8):
                            P.mm(bk[:, :], V(WA, 0, i * 4096 + j * 512 + ct * 128, [[1, 128]]),
                                 V(hT, 0, j * NTOK + tb * 512, [[1, 512]]), j == 0, j == 7,
                                 ['hT%d' % t for t in range(tb * 4, tb * 4 + 4)] + [wr], [bn])
                        ui_ = (ct + (0 if cb == 4 else 4)) % 2
                        sr = 'ust%d' % ui_
                        so = V(ust, 0, ui_ * 1536 + tb * 64, [[192, 8], [1, 64]])
                        src = V(bk, 0, 0, [[1, 8], [8, 64]])
                        if cb == 4:
                            P.cp('act', so, src, [bn], [sr])
                        else:
                            P.act(so, src, AF.Silu, [bn], [sr])
                        if tb == 2:
                            P.dma('sp', bass.AP(dstD.tensor, ct * 128 * 8 * NCH, [[8 * NCH, 128], [1, 8 * NCH]]),
                                  V(ust, 0, ui_ * 1536, [[1, 1536]]), reads=[sr],
                                  writes=['UD' if cb == 4 else 'GSD'], key='ud' if cb == 4 else 'gsd')
                        P.run_q(setup_q, 3)
                        P.run_q(setup_q, 3)

        if KSTOP == '1b':
            P.finish(); P.emit(); return nc
        st2 = sb("st2", [128, 64], F32)
        WAb = mk(WA[:])
        WAf = mk(WA[:].bitcast(F32))
        BUF = [R1bf, WAb]
        QTOK, KTOK, KC, VAUG, QT, KT, PTO, GAH = 0, 1024, 2048, 2304, 3604, 4628, 5908, 8212
        OSEQ = 6932
        GAHS = [GAH, 5908]
        O1 = WAf(0, 5514, [[1, 128]])
        OO = WAf(0, 5514 + 128, [[1, 128]])
        att_state = {'init': False, 'ptc': 0, 'stc': 0, 'units': 0, 'fslot': 0}
        pendB = []

        def att_init():
            P.memset('pool', st2[:], 0.0, ['st2'] + [n_ + str(k_) for k_ in range(3) for n_ in ('r0', 'r1', 'r1n', 'ossq', 'osd', 'ors')] + ['fs0', 'fs1', 'fss', 'fsd', 'frs',
                                            'bar2_60', 'bar2_61', 'bar2_62', 'bar2_63'])
            R1_ALL = ['qkst0', 'qkst1', 'qkst2', 'tmpA0', 'tmpA1', 'tmpB0', 'tmpB1', 'f32st0', 'f32st1',
                      'WA0', 'WA1', 'WA2']
            P.memset('pool', st2[:, 63:64], 0.0, R1_ALL + ['r1att', 'bar2_63'])
            for si in range(2):
                P.memset('pool', BUF[si](0, VAUG + 128, [[130, 10], [1, 2]]), 1.0, ['vaug%d' % si, 'r1att'])

        def att_loads(unit, si):
            (tok0, L, hasc), h = unit
            nt = L // 128
            B = BUF[si]
            sx = str(si)
            P.dma('sp', B(0, QTOK, [[128, nt], [1, 128]]),
                  bass.AP(QK_s.tensor, tok0 * 1024 + h * 128, [[1024, 128], [128 * 1024, nt], [1, 128]]),
                  reads=['r1att', 'QKs'], writes=['qtok' + sx], key='aq' + sx)
            P.dma('sp', B(0, KTOK, [[128, nt], [1, 128]]),
                  bass.AP(QK_s.tensor, tok0 * 1024 + 512 + h * 128, [[1024, 128], [128 * 1024, nt], [1, 128]]),
                  reads=['r1att', 'QKs'], writes=['ktok' + sx], key='ak' + sx)
            P.dma('sp', B(0, VAUG, [[130, nt], [1, 128]]),
                  bass.AP(V_s.tensor, tok0 * 512 + h * 128, [[512, 128], [128 * 512, nt], [1, 128]]),
                  reads=['r1att', 'Vs', 'vaug' + sx], writes=['vaug' + sx], key='av' + sx)
            P.dma('sp', B(0, GAHS[si], [[128, nt], [1, 128]]),
                  bass.AP(GA_s.tensor, tok0 * 512 + h * 128, [[512, 128], [128 * 512, nt], [1, 128]]),
                  reads=['r1att', 'GAs'], writes=['gah' + sx], key='ag' + sx)
            if hasc:
                P.dma('pool', B(0, KC, [[128, 2], [1, 128]]),
                      bass.AP(ck_d.tensor, h * 128, [[512, 128], [128 * 512, 2], [1, 128]]),
                      reads=['r1att'], writes=['kc' + sx], key='akc' + sx)
                P.dma('pool', B(0, VAUG + nt * 130, [[130, 2], [1, 128]]),
                      bass.AP(cv_d.tensor, h * 128, [[512, 128], [128 * 512, 2], [1, 128]]),
                      reads=['r1att', 'vaug' + sx], writes=['vaug' + sx], key='avc' + sx)

        def attention(units, pump_fn=None):
            if not att_state['init']:
                att_init()
                att_state['init'] = True
            att_loads(units[0], att_state['units'] % 2)
            for ui, unit in enumerate(units):
                (tok0, L, hasc), h = unit
                si = att_state['units'] % 2
                att_state['units'] += 1
                B = BUF[si]
                sx = str(si)
                nt = L // 128
                nkt = nt + (2 if hasc else 0)
                for t in range(nt):
                    P.tr(bkb[6][:, t * 128:(t + 1) * 128], B(0, QTOK + t * 128, [[1, 128]]), identb[:],
                         ['qtok' + sx, 'identb'], ['b6'])
                P.cp('act', B(0, QT, [[1, L]]), bkb[6][:, 0:L], ['b6', 'r1att'], ['qT' + sx])
                for t in range(nt):
                    P.tr(bkb[7][:, t * 128:(t + 1) * 128], B(0, KTOK + t * 128, [[1, 128]]), identb[:],
                         ['ktok' + sx, 'identb'], ['b7'])
                P.cp('act', B(0, KT, [[1, L]]), bkb[7][:, 0:L], ['b7', 'r1att'], ['kT' + sx])
                if hasc:
                    for t in range(2):
                        P.tr(bkb[6][:, t * 128:(t + 1) * 128], B(0, KC + t * 128, [[1, 128]]), identb[:],
                             ['kc' + sx, 'identb'], ['b6'])
                    P.cp('act', B(0, KT + L, [[1, 256]]), bkb[6][:, 0:256], ['b6', 'r1att'], ['kT' + sx])
                while pendB:
                    pendB.pop(0)()
                if ui + 1 < len(units):
                    att_loads(units[ui + 1], 1 - si)
                its = []
                for qb, q0 in enumerate(range(0, L, 384)):
                    bs = min(384, L - q0)
                    for kt in range(nkt):
                        its.append((qb, q0, bs, bs // 128, kt))
                slots = {}

                def QK(i):
                    qb, q0, bs, nq, kt = its[i]
                    sb_ = att_state['stc'] % 2
                    att_state['stc'] += 1
                    pb_ = att_state['ptc'] % 3
                    att_state['ptc'] += 1
                    slots[i] = (sb_, pb_)
                    for c in range(2):
                        bi = sb_ * 2 + c
                        P.mm(banks[bi][:, 0:bs], B(c * 64, KT + kt * 128, [[1, 128]], 64),
                             B(c * 64, QT + q0, [[1, bs]], 64), True, True, ['kT' + sx, 'qT' + sx], ['b%d' % bi])

                def EXPPV(i):
                    qb, q0, bs, nq, kt = its[i]
                    sb_, pb_ = slots[i]
                    prn = 'PT%d' % pb_
                    pv = (4, 5) if qb % 2 == 0 else (6, 7)
                    P.act(R1bf(0, PTO + pb_ * 768, [[384, 2], [1, bs]]), V(psall, 0, sb_ * 1024, [[512, 2], [1, bs]]), AF.Exp,
                          ['b%d' % (sb_ * 2), 'b%d' % (sb_ * 2 + 1), 'r1att'], [prn], scale=0.125)
                    for c in range(2):
                        for qt in range(nq):
                            P.mm(banks[pv[c]][:, qt * 129:(qt + 1) * 129],
                                 R1bf(0, PTO + pb_ * 768 + c * 384 + qt * 128, [[1, 128]]),
                                 B(0, VAUG + kt * 130, [[1, 129]]), (kt == 0 and qt == 0), (kt == nkt - 1),
                                 [prn, 'vaug' + sx], ['b%d' % pv[c]])
                    if kt == nkt - 1:
                        for qt in range(nq):
                            t = (q0 // 128) + qt
                            k_ = att_state['fslot'] % 3
                            att_state['fslot'] += 1
                            ks = str(k_)
                            c_ = 16 + k_ * 8
                            OOk = WAf(0, 5514 + 128 + k_ * 128, [[1, 128]])
                            b4n, b5n = 'b%d' % pv[0], 'b%d' % pv[1]
                            a0 = banks[pv[0]][:, qt * 129:qt * 129 + 128]
                            a1 = banks[pv[1]][:, qt * 129:qt * 129 + 128]
                            P.recip(st2[:, c_:c_ + 1], banks[pv[0]][:, qt * 129 + 128:qt * 129 + 129], [b4n], ['r0' + ks])
                            P.recip(st2[:, c_ + 1:c_ + 2], banks[pv[1]][:, qt * 129 + 128:qt * 129 + 129], [b5n], ['r1' + ks])
                            P.tt('dve', st2[:, c_ + 2:c_ + 3], st2[:, c_ + 1:c_ + 2], neglam[:], ALU.mult, ['r1' + ks, 'neglam'],
                                 ['r1n' + ks])
                            P.ts('dve', O1, a1, st2[:, c_ + 2:c_ + 3], None, ALU.mult, None, [b5n, 'r1n' + ks, 'r1att'], ['O1'])
                            P.stt('dve', OOk, a0, st2[:, c_:c_ + 1], O1, ALU.mult, ALU.add, [b4n, 'r0' + ks, 'O1', 'r1att'],
                                  ['OO' + ks])
                            P.stt('dve', O1, OOk, 1.0, OOk, ALU.mult, ALU.mult, ['OO' + ks], ['O1', 'ossq' + ks],
                                  accum=st2[:, c_ + 3:c_ + 4])

                            def stageB(t=t, ks=ks, c_=c_, OOk=OOk, h=h, si=si, sx=sx, B=B):
                                P.act(st2[:, c_ + 4:c_ + 5], st2[:, c_ + 3:c_ + 4], AF.Ln, ['ossq' + ks, 'epst'], ['osd' + ks],
                                      bias=epst[:], scale=1.0 / 128.0)
                                P.act(st2[:, c_ + 5:c_ + 6], st2[:, c_ + 4:c_ + 5], AF.Exp, ['osd' + ks], ['ors' + ks], scale=-0.5)
                                P.stt('dve', WAb(0, OSEQ + t * 512 + h * 128, [[1, 128]]), OOk, st2[:, c_ + 5:c_ + 6],
                                      B(0, GAHS[si] + t * 128, [[1, 128]]), ALU.mult, ALU.mult,
                                      ['OO' + ks, 'ors' + ks, 'gah' + sx, 'r1att'], ['oseq'])
                            pendB.append(stageB)

                QK(0)
                for i in range(len(its)):
                    if i + 1 < len(its):
                        QK(i + 1)
                    npend = len(pendB)
                    EXPPV(i)
                    if npend:
                        pendB.pop(0)()
                    if pump_fn is not None:
                        pump_fn(1)
                while len(pendB) > 3:
                    pendB.pop(0)()
                if h == 3:
                    while pendB:
                        pendB.pop(0)()
                    for t in range(nt):
                        for hh in range(4):
                            P.tr(bkb[6][:, hh * 128:(hh + 1) * 128], WAb(0, OSEQ + t * 512 + hh * 128, [[1, 128]]), identb[:],
                                 ['oseq', 'identb'], ['b6'])
                        P.cp('act', V(catT, 0, tok0 + t * 128, [[NTOK, 4], [1, 128]]),
                             bass.AP(bkb[6].tensor, bkb[6].offset, [[bkb[6].ap[0][0], 128], [128, 4], [1, 128]]),
                             ['b6'] + ['hT%d' % i for i in range(12)], ['catA'])

        SEQ_P = [(1024, 256, False), (1280, 256, False)]
        SEQ_S = [(0, 1024, True)]

        P.run_q(setup_q, 100000)
        if KSTOP == '0b':
            P.finish(); P.emit(); return nc
        for pc in (128, 162):
            P.memset('pool', V(Xs, 0, pc, [[196, 32], [1, 2]]), 0.0, ['Xs', 'Bb', 'Cp', 'Cin', 'xt1', 'xt2', 'cpfree'])
        for (c0_, n_, s0_) in [(0, 128, 1), (128, 32, 131), (160, 32, 165)]:
            P.dma('sp', V(Xs, 0, s0_ - 1, [[196, 32], [1, n_]]),
                  bass.AP(UD.tensor, c0_, [[NCH, 128], [128 * NCH, 32], [1, n_]]),
                  reads=['UD'], writes=['Xs', 'Bb', 'Cp', 'Cin', 'xt1', 'xt2', 'cpfree'], key='xs')
        SEG = [(0, 128, 1), (128, 32, 131), (160, 32, 165)]
        P.memset('pool', SHb[:], 0.0, ['SHb', 'Be', 'T1', 'T2', 'M0tmp'])
        for g in range(32):
            bi = pbanks[nb[0] % 4]
            nb[0] += 1
            bn = 'b%d' % bi
            bk = banks[bi]
            for ri in range(2):
                P.mm(bk[:, ri * 196:(ri + 1) * 196], V(W_S, 0, g * 256 + ri * 128, [[1, 128]]),
                     V(Xs, 0, g * 196, [[1, 196]]), True, True, ['W_S', 'Xs'], [bn])
            P.cp('act' if g % 2 else 'dve', V(SHb, 0, g * 396 + 1, [[198, 2], [1, 196]], 64),
                 V(bk, 0, 0, [[196, 2], [1, 196]], 64), [bn], ['SHb'])
            for k_, (c0, n, s0) in enumerate(SEG):
                P.cp('dve' if g % 2 else 'act', V(SHb, 64, g * 396 + s0 + n - 1, [[198, 2], [-1, n]], 64),
                     V(bk, 64, s0 - 1, [[196, 2], [1, n]], 64), [bn], ['SHb'])
        P.dma('pool', V(W_S, 0, 0, [[1024, 8], [1, 1024]]), bass.AP(wout_d.tensor, 0, [[1024, 128], [128 * 1024, 8], [1, 1024]]),
              writes=['W_S', 'wout'], key='wout')
        P.dma('pool', R1bf(0, 13000, [[512, 4], [1, 512]]), bass.AP(wglu_d.tensor, 0, [[512, 128], [128 * 512, 4], [1, 512]]),
              writes=['Ct', 'mask', 'Bt', 'wglu'], key='wglu')
        h0in = V(R2, 0, 0, [[1, 128]], 64)
        for d in range(2):
            P.dma('sp', V(R2, 0, d * 64, [[1, 64]], 64), st_d[d], writes=['h0in', 'PWT', 'FTt'], key='h0')
        P.tr(banks[6][:, 0:64], h0in, ident[0:64, 0:64], ['h0in', 'ident'], ['b6'])
        Zst = V(R2, 0, 256, [[1, 64]])
        T4 = V(R2, 0, 384, [[1, 128]])
        U2 = V(R2, 0, 512, [[1, 64]])
        FIN = V(R2, 0, 640, [[1, 64]])

        H0SB = 576
        P.cp('dve', V(R2, 0, H0SB, [[1, 64]]), banks[6][:, 0:64], ['b6'], ['h0sb'])

        def chain(eng, seg, zero_init, zoff, toff, uoff, fin_off, fin_name, q, tag, split=None):
            c0, n, s0 = seg
            zr = 'Z' + tag
            Z = V(R2, 0, zoff, [[32, 2], [1, 32]])
            Zb = V(R2, 0, zoff, [[0, 2], [32, 2], [1, 32]])
            A4 = V(al4, 0, 0, [[64, 2], [32, 2], [1, 32]])

            def init():
                if zero_init:
                    P.memset(eng, Z, 0.0, [zr])
                else:
                    P.cp(eng, Z, V(R2, 0, H0SB, [[32, 2], [1, 32]]), ['h0sb'], [zr])
                    P.cp(eng, V(SHb, 0, s0 - 1, [[198, 2], [396, 32]]), Z, [zr], ['SHbw' + tag])
            q.append(init)

            def step(i_, en, to, uo):
                tr_, ur = 'T4' + tag + en, 'U' + tag + en
                T = V(R2, 0, to, [[64, 2], [32, 2], [1, 32]])
                Ta = V(R2, 0, to, [[64, 2], [1, 32]])
                Tb = V(R2, 0, to + 32, [[64, 2], [1, 32]])
                U = V(R2, 0, uo, [[32, 2], [1, 32]])
                Sv = V(SHb, 0, s0 + i_, [[198, 2], [396, 32]])
                P.tt(en, T, Zb, A4, ALU.mult, [zr, 'al4'], [tr_])
                P.tt(en, U, Ta, Tb, ALU.add, [tr_], [ur])
                P.tt(en, Z, U, Sv, ALU.add, [ur, 'SHb'], [zr])
                P.cp(en, Sv, Z, [zr], ['SHbw' + tag])
            for i_ in range(n):
                if split is not None and i_ >= split[0]:
                    split[4].append(lambda i_=i_: step(i_, split[1], split[2], split[3]))
                else:
                    q.append(lambda i_=i_: step(i_, eng, toff, uoff))
            if fin_name is not None:
                q.append(lambda: P.cp(eng, V(R2, 0, fin_off, [[1, 64]]), V(R2, 0, zoff, [[1, 64]]), [zr], [fin_name]))

        rec_q = {'dve': [], 'pool': []}

        def fin_out(si_):
            fn = 'fin%d' % si_
            foff = 640 + (si_ - 1) * 64
            fo_off = 768 + (si_ - 1) * 128
            P.tr(banks[7][0:64, 0:128], V(R2, 0, foff, [[1, 64]]), ident[:], [fn, 'ident'], ['b7'])
            P.cp('act', V(R2, 0, fo_off, [[1, 128]], 64), banks[7][0:64, 0:128], ['b7'], ['fo%d' % si_])
            P.dma('sp', bass.AP(ns_d.tensor, (si_ - 1) * 8192, [[64, 64], [4096, 2], [1, 64]]),
                  V(R2, 0, fo_off, [[64, 2], [1, 64]], 64), reads=['fo%d' % si_], key='ons', final=True)

        for si_, seg in ((1, SEG[1]), (2, SEG[2])):
            chain('pool', seg, True, 320, 1024, 1152, 640 + (si_ - 1) * 64, 'fin%d' % si_, rec_q['pool'], 'p')
        SPLIT = int(os.environ.get('SPLIT', '64'))
        chain('dve', SEG[0], False, 256, 384, 512, 0, None, rec_q['dve'], 's',
              split=(SPLIT, 'pool', 1024, 1152, rec_q['pool']))

        def pump(n=1):
            for e_ in ('dve', 'pool'):
                for _ in range(n):
                    if rec_q[e_]:
                        rec_q[e_].pop(0)()

        if KSTOP == '2':
            P.finish(); P.emit(); return nc

        attention([(sq, h) for sq in SEQ_P + SEQ_S for h in range(4)], pump)
        pump(1000)
        fin_out(1)
        fin_out(2)
        if KSTOP == '3':
            P.finish(); P.emit(); return nc
        YDp = dscr("YDp", [32, 128, 196], F32)
        ATT_RES = ['oseq', 'O1', 'OO0', 'OO1', 'OO2'] + [n + s_ for n in ('qtok', 'ktok', 'kc', 'vaug', 'gah', 'qT', 'kT') for s_ in '01']
        P.memset('pool', st2[:, 60:61], 0.0, ATT_RES + ['WA0', 'WA1', 'WA2', 'wafree', 'bar2_60'])
        for ct in range(4):
            P.dma('sp', WAb(0, 6144 + ct * 1536, [[1, 1536]]),
                  bass.AP(UD.tensor, ct * 128 * 8 * NCH, [[8 * NCH, 128], [1, 8 * NCH]]),
                  reads=['UD', 'wafree'], writes=['uTp%d' % ct], key='utp')
            P.dma('sp', WAb(0, ct * 1536, [[1, 1536]]),
                  bass.AP(GSD.tensor, ct * 128 * 8 * NCH, [[8 * NCH, 128], [1, 8 * NCH]]),
                  reads=['GSD', 'wafree'], writes=['gSp%d' % ct], key='gsp')
        for (c0, n, s0) in SEG:
            P.cp('act', R1bf(0, s0 - 1, [[198, 64], [1, n]], 64), V(SHb, 0, s0 - 1, [[198, 64], [1, n]], 64),
                 ['SHb', 'SHbws', 'SHbwp', 'wafree'], ['HAL'])
            P.cp('dve', R1bf(64, s0 - 1, [[198, 64], [1, n]], 64), V(SHb, 64, s0 + n - 2, [[198, 64], [-1, n]], 64),
                 ['SHb', 'SHbws', 'SHbwp', 'wafree'], ['HAL'])
        for g in range(32):
            bi = g % 4
            bn = 'b%d' % bi
            bk = banks[bi]
            P.mm(bk[:, 0:196], V(W_Y, 0, g * 256, [[1, 128]]), R1bf(0, g * 396, [[1, 196]]), True, False,
                 ['W_Y', 'HAL'], [bn])
            P.mm(bk[:, 0:196], V(W_Y, 0, g * 256 + 128, [[1, 128]]), R1bf(0, g * 396 + 198, [[1, 196]]), False, False,
                 ['W_Y', 'HAL'], [bn])
            P.mm(bk[:, 0:196], V(M0, 0, g * 128, [[1, 128]]), V(Xs, 0, g * 196, [[1, 196]]),
                 False, True, ['M0', 'Xs'], [bn])
            yi = g % 3
            yst = V(R2, 0, 1216 + yi * 196, [[1, 196]])
            P.cp('act' if g % 2 else 'dve', yst, bk[:, 0:196], [bn, 'fo1', 'fo2'], ['yst%d' % yi])
            P.dma('sp', YDp[g], yst, reads=['yst%d' % yi], writes=['YD%d' % (g // 8)], key='yd%d' % (g // 8))
        if KSTOP == '4':
            P.finish(); P.emit(); return nc

        SSM_DEAD = ['SHb', 'SHbws', 'SHbwp', 'HAL', 'Xs', 'W_Y', 'M0', 'PT0', 'PT1', 'PT2', 'r1att'] + ATT_RES
        P.memset('pool', st2[:, 62:63], 0.0, SSM_DEAD + ['gfree', 'bar2_62'])
        M0f = mk(M0[:].bitcast(F32))
        WSb = mk(W_S[:])
        ysf = [WYF(0, 0, [[1, 1536]]), WYF(0, 1536, [[1, 1536]]), M0f(0, 0, [[1, 1536]]), V(R1, 0, 0, [[1, 1536]])]
        for ct in range(4):
            yT = SF(0, ct * 1536, [[192, 8], [1, 192]])
            for (c0, n, s0) in SEG:
                P.dma('sp', SF(0, ct * 1536 + c0, [[192, 8], [1, n]]),
                      bass.AP(YDp.tensor, ct * 1024 * 196 + s0 - 1, [[8 * 196, 128], [196, 8], [1, n]]),
                      reads=['YD%d' % ct, 'gfree'], writes=['yT%d' % ct], key='yt%d' % ct)
            P.stt('dve', ysf[ct], WAb(0, 6144 + ct * 1536, [[1, 1536]]), dsk[:, ct:ct + 1], SF(0, ct * 1536, [[1, 1536]]),
                  ALU.mult, ALU.add, ['uTp%d' % ct, 'yT%d' % ct, 'dsk', 'gfree'], ['ysf%d' % ct])
            P.act(ysf[ct], ysf[ct], AF.Gelu, ['ysf%d' % ct], ['ysf%d' % ct])
            P.cp('dve', R1bf(0, 3072 + ct * 1536, [[1, 1536]]), ysf[ct], ['ysf%d' % ct, 'gfree'], ['ysb%d' % ct])
        for nt_ in range(4):
            for sbk in range(4):
                bi = (nt_ * 4 + sbk) % 4
                bn = 'b%d' % bi
                for mt in range(4):
                    P.mm(banks[bi][:, 0:384], R1bf(0, 13000 + mt * 512 + nt_ * 128, [[1, 128]]),
                         R1bf(0, 3072 + mt * 1536 + sbk * 384, [[1, 384]]), mt == 0, mt == 3,
                         ['wglu'] + ['ysb%d' % m for m in range(4)], [bn])
                sgi = (nt_ * 4 + sbk) % 2
                sg = V(R1, 0, 4608 + sgi * 384, [[1, 384]])
                tq = V(R1, 0, 5632 + sgi * 384, [[1, 384]])
                P.act(sg, banks[bi][:, 0:384], AF.Sigmoid, [bn, 'bglu', 'gfree'], ['sg%d' % sgi], bias=bglu[:, nt_:nt_ + 1])
                yv = bass.AP(ysf[nt_].tensor, ysf[nt_].offset + sbk * 384, [[ysf[nt_].ap[0][0], 128], [1, 384]])
                P.tt('dve', tq, yv, sg, ALU.mult, ['ysf%d' % nt_, 'sg%d' % sgi], ['tq%d' % sgi])
                P.tt('dve' if (nt_ * 4 + sbk) % 2 == 0 else 'pool', V(catT, 0, (4 + nt_) * NTOK + 2 * sbk, [[1, 2], [8, 192]]),
                     V(R1, 0, 5632 + sgi * 384, [[192, 2], [1, 192]]), WAb(0, nt_ * 1536 + sbk * 384, [[192, 2], [1, 192]]),
                     ALU.mult, ['tq%d' % sgi, 'gSp%d' % nt_] + ['hT%d' % i for i in range(12)], ['catS'])
        if KSTOP == '5':
            P.finish(); P.emit(); return nc

        P.memset('pool', st2[:, 61:62], 0.0, ['ysb0', 'ysb1', 'ysb2', 'ysb3', 'sg0', 'sg1', 'tq0', 'tq1', 'ysf3', 'r1out', 'bar2_61'])
        xbufs = [xt[0][:], xt[1][:], V(R1, 0, 2048, [[1, 1024]]), V(R1, 0, 3072, [[1, 1024]])]
        for tt in range(12):
            cond = 1 if tt < 8 else 0
            xb_ap = xbufs[tt % 4]
            xr = 'xo%d' % (tt % 4)
            P.dma('sp', xb_ap, x_d[tt * 128:(tt + 1) * 128, :], reads=['r1out'], writes=[xr, 'xt%d' % (tt % 2)], key=xr)
            roff = [0, 1024, 4096, 5120][tt % 4]
            res = V(R1, 0, roff, [[1, 1024]])
            rr = 'res%d' % (tt % 4)
            for cb in range(2):
                bi = (tt % 4) * 2 + cb
                bn = 'b%d' % bi
                for j in range(8):
                    P.mm(banks[bi][:, :], V(catT, 0, j * NTOK + tt * 128, [[1, 128]]),
                         V(W_S, 0, j * 1024 + cb * 512, [[1, 512]]), j == 0, j == 7, ['catA', 'catS', 'wout'], [bn])
                P.act(junk[:, 0:512], banks[bi][:, :], AF.Square, [bn], ['junk', 'fs%d' % cb], accum=st2[:, 8 + cb:9 + cb])
            P.tt('dve', st2[:, 10:11], st2[:, 8:9], st2[:, 9:10], ALU.add, ['fs0', 'fs1'], ['fss'])
            P.act(st2[:, 11:12], st2[:, 10:11], AF.Sqrt, ['fss', 'epst'], ['fsd'], bias=epst[:], scale=1.0 / 1024.0)
            P.recip(st2[:, 12:13], st2[:, 11:12], ['fsd'], ['frs'])
            for cb in range(2):
                bi = (tt % 4) * 2 + cb
                P.act(V(R1, 0, roff + cb * 512, [[1, 512]]), banks[bi][:, :], AF.Copy, ['b%d' % bi, 'frs', 'r1out'], [rr],
                      scale=st2[:, 12:13])
            P.tt('dve', res, res, gn[:, cond, :], ALU.mult, [rr, 'gn'], [rr])
            P.tt('pool', xb_ap, xb_ap, res, ALU.add, [xr, rr], [xr])
            P.dma('sp', y_d[tt * 128:(tt + 1) * 128, :], xb_ap, reads=[xr], key='oy%d' % (tt % 4), final=True)
        P.finish()
        P.emit()
    return nc


_NC = None


def _consts():
    t = np.arange(1024)
    row = (t // 64).astype(np.float32)
    col = (t % 64).astype(np.float32)
    inv = (10000.0 ** (-np.arange(16, dtype=np.float32) / 16)).astype(np.float32)
    ar = row[:, None] * inv[None, :]
    ac = col[:, None] * inv[None, :]
    cosr, sinr, cosc, sinc = np.cos(ar), np.sin(ar), np.cos(ac), np.sin(ac)
    ropec = np.concatenate([cosr, cosr, cosc, cosc], axis=1).astype(np.float32)
    ropes = np.concatenate([-sinr, sinr, -sinc, sinc], axis=1).astype(np.float32)
    k = np.zeros((NK, 2), np.float32)
    for i in range(8):
        k[KE + i] = (7 - i, i)
        k[KQ + i] = (i + 1, 8 - i)
        k[KP + i] = (i - 7, -i)
    k[K1] = (1, 1)
    k[K8] = (8, 8)
    karr = np.zeros((2, 64, NK, 32), np.float32)
    karr[0] = k[None, :, 0, None]
    karr[1] = k[None, :, 1, None]
    karr = np.ascontiguousarray(karr.reshape(128, NK * 32))
    s = np.tile(np.arange(8), 16)
    m0 = (s[:, None] <= s[None, :]).astype(np.float32)
    m1 = (s[:, None] >= s[None, :]).astype(np.float32)
    mask = np.concatenate([m0, m1], axis=1)
    return dict(ident=np.eye(128, dtype=np.float32), ropec=ropec, ropes=ropes, karr=karr, mask01=mask)


def kernel(**inp):
    global _NC
    if _NC is None:
        _NC = build()
    f = lambda a: np.ascontiguousarray(np.asarray(a, dtype=np.float32))
    cst = _consts()
    shared = dict(
        w_ada=f(inp['w_ada'][0]), b_ada=f(inp['b_ada'][0]), norm_pre=f(inp['norm_pre'][0]),
        norm_post=f(inp['norm_post'][0]), w_in=f(inp['w_in'][0]), lambda_qk=f(inp['lambda_qk'][0]).reshape(256),
        subln=f(inp['subln'][0]), ssm_A_re=f(inp['ssm_A_re'][0]), ssm_A_im=f(inp['ssm_A_im'][0]),
        ssm_log_dt=f(inp['ssm_log_dt'][0]), ssm_B_re=f(inp['ssm_B_re'][0]), ssm_B_im=f(inp['ssm_B_im'][0]),
        ssm_C_re=f(inp['ssm_C_re'][0]).reshape(2, 512, 64), ssm_C_im=f(inp['ssm_C_im'][0]).reshape(2, 512, 64),
        ssm_D=f(inp['ssm_D'][0]), w_glu=f(inp['w_glu'][0]), b_glu=f(inp['b_glu'][0]), w_out=f(inp['w_out'][0]),
        **cst)
    xp, xs = f(inp['x_prompt']), f(inp['x_sample'])
    in_maps = []
    for c in range(8):
        m = dict(shared)
        m['x'] = np.ascontiguousarray(np.concatenate([xs[c], xp[2 * c], xp[2 * c + 1]], axis=0))
        m['ck'] = f(inp['cache_k'][c, 0]).reshape(256, 512)
        m['cv'] = f(inp['cache_v'][c, 0]).reshape(256, 512)
        m['st'] = f(inp['state_ssm'][c, 0]).reshape(2, 64, 64)
        m['cvec'] = np.ascontiguousarray(np.stack([f(inp['c_ctx']), f(inp['c'][c])], axis=0))
        in_maps.append(m)
    res = run_bass_kernel_spmd(_NC, in_maps, core_ids=list(range(8)))
    R = res.results
    y_p = np.zeros((16, 256, 1024), np.float32)
    y_s = np.zeros((8, 1024, 1024), np.float32)
    nk = np.zeros((16, 1, 256, 4, 128), np.float32)
    nv = np.zeros((16, 1, 256, 4, 128), np.float32)
    ns = np.zeros((16, 1, 2, 2, 32, 64), np.float32)
    for c in range(8):
        y = R[c]['y']
        y_s[c] = y[0:1024]
        y_p[2 * c] = y[1024:1280]
        y_p[2 * c + 1] = y[1280:1536]
        nk[2 * c:2 * c + 2, 0] = R[c]['nk'].reshape(2, 256, 4, 128)
        nv[2 * c:2 * c + 2, 0] = R[c]['nv'].reshape(2, 256, 4, 128)
        ns[2 * c:2 * c + 2, 0] = R[c]['ns'].reshape(2, 2, 2, 32, 64)
    return (y_p, y_s, nk, nv, ns)
```

```python
import contextlib
import math
import os
import numpy as np
import concourse.bass as bass
import concourse.mybir as mybir
from concourse.bass_utils import run_bass_kernel_spmd

F32 = mybir.dt.float32
BF16 = mybir.dt.bfloat16
I32 = mybir.dt.int32
AF = mybir.ActivationFunctionType
ALU = mybir.AluOpType
AX = mybir.AxisListType

NTOK = 1536
NCH = 192
TWO_PI = 2.0 * math.pi
KE, KQ, KP, K1, K8, NK = 0, 8, 16, 24, 25, 26


class Prog:
    ENG = ('pe', 'act', 'dve', 'pool', 'sp')

    def __init__(self, nc):
        self.nc = nc
        self.ops = {e: [] for e in self.ENG}
        self.cnt = {e: 0 for e in self.ENG}
        self.seen = {e: {} for e in self.ENG}
        self.res = {}
        self.dcnt = {}
        self.out_tokens = []
        self.capture = None

    def _deps(self, eng, reads, writes):
        deps = {}

        def add(tok):
            if tok is None:
                return
            s, v = tok
            if deps.get(s, 0) < v:
                deps[s] = v
        for r in reads:
            st = self.res.get(r)
            if st:
                add(st['w'])
        for w in writes:
            st = self.res.get(w)
            if st:
                add(st['w'])
                for s, v in st['r'].items():
                    add((s, v))
        for s, v in deps.items():
            if s == 'pe' and eng == 'pe':
                continue
            if self.seen[eng].get(s, 0) < v:
                self.seen[eng][s] = v
                self.ops[eng].append(('wait', s, v))

    def _commit(self, tok, reads, writes):
        for w in writes:
            self.res[w] = {'w': tok, 'r': {}}
        for r in reads:
            st = self.res.setdefault(r, {'w': None, 'r': {}})
            s, v = tok
            if st['r'].get(s, 0) < v:
                st['r'][s] = v

    @staticmethod
    def _excl(reads, writes):
        isb = lambda n: len(n) == 2 and n[0] == 'b' and n[1].isdigit()
        w = list(writes) + [r for r in reads if isb(r)]
        r = [r for r in reads if not isb(r)]
        return r, w

    def op(self, eng, fn, reads=(), writes=()):
        if self.capture is not None:
            self.capture.append(lambda: self.op(eng, fn, reads, writes))
            return
        reads, writes = self._excl(reads, writes)
        self._deps(eng, reads, writes)
        self.cnt[eng] += 1
        tok = (eng, self.cnt[eng])
        self.ops[eng].append(('op', fn))
        self._commit(tok, reads, writes)

    def dma(self, q, out, in_, reads=(), writes=(), key=None, final=False, **kw):
        if self.capture is not None:
            self.capture.append(lambda: self.dma(q, out, in_, reads, writes, key, final, **kw))
            return
        self._deps(q, reads, writes)
        k = ('dma', key)
        self.dcnt[k] = self.dcnt.get(k, 0) + 16
        tok = (k, self.dcnt[k])
        self.ops[q].append(('dma', out, in_, k, kw))
        self._commit(tok, reads, writes)
        if final:
            self.out_tokens.append(tok)

    def tt(self, eng, out, in0, in1, op, r, w):
        self.op(eng, lambda e: e.tensor_tensor(out=out, in0=in0, in1=in1, op=op), r, w)

    def ts(self, eng, out, in0, s1, s2, op0, op1, r, w):
        if s2 is None:
            self.op(eng, lambda e: e.tensor_scalar(out=out, in0=in0, scalar1=s1, scalar2=None, op0=op0), r, w)
        else:
            self.op(eng, lambda e: e.tensor_scalar(out=out, in0=in0, scalar1=s1, scalar2=s2, op0=op0, op1=op1), r, w)

    def stt(self, eng, out, in0, scalar, in1, op0, op1, r, w, accum=None):
        if accum is None:
            self.op(eng, lambda e: e.scalar_tensor_tensor(out=out, in0=in0, scalar=scalar, in1=in1, op0=op0, op1=op1), r, w)
        else:
            self.op(eng, lambda e: e.scalar_tensor_tensor(out=out, in0=in0, scalar=scalar, in1=in1, op0=op0, op1=op1,
                                                          accum_out=accum), r, w)

    def act(self, out, in_, func, r, w, bias=None, scale=None, accum=None):
        kw = {}
        if bias is not None:
            kw['bias'] = bias
        if scale is not None:
            kw['scale'] = scale
        if accum is not None:
            kw['accum_out'] = accum
        self.op('act', lambda e: e.activation(out=out, in_=in_, func=func, **kw), r, w)

    def cp(self, eng, out, in_, r, w):
        if eng == 'act':
            self.op(eng, lambda e: e.copy(out=out, in_=in_), r, w)
        else:
            self.op(eng, lambda e: e.tensor_copy(out=out, in_=in_), r, w)

    def memset(self, eng, ap, val, w):
        self.op(eng, lambda e: e.memset(ap, val), (), w)

    def mm(self, out, lhsT, rhs, start, stop, r, w):
        self.op('pe', lambda e: e.matmul(out, lhsT=lhsT, rhs=rhs, start=start, stop=stop,
                                         skip_group_check=True), r, w)

    def tr(self, out, in_, ident, r, w):
        self.op('pe', lambda e: e.transpose(out=out, in_=in_, identity=ident), r, w)

    def recip(self, out, in_, r, w):
        self.op('dve', lambda e: e.reciprocal(out=out, in_=in_), r, w)

    def rsum(self, out, in_, r, w):
        self.op('dve', lambda e: e.reduce_sum(out=out, in_=in_, axis=AX.X), r, w)

    def run_q(self, q, n):
        cap, self.capture = self.capture, None
        for _ in range(n):
            if not q:
                break
            q.pop(0)()
        self.capture = cap

    def finish(self):
        last = {}
        for s, v in self.out_tokens:
            last[s] = max(last.get(s, 0), v)
        for s, v in last.items():
            self.ops['sp'].append(('wait', s, v))

    def emit(self):
        nc = self.nc
        keys = list(self.ENG[:4]) + list(self.dcnt.keys())
        with contextlib.ExitStack() as st:
            sems = {}
            for i, k in enumerate(keys):
                sems[k] = st.enter_context(nc.semaphore("s%d" % i))
            block = st.enter_context(nc.Block())

            need = {e_: set() for e_ in self.ENG[:4]}
            for en in self.ENG:
                for item in self.ops[en]:
                    if item[0] == 'wait' and item[1] in need:
                        need[item[1]].add(item[2])
            newidx = {e_: {v: i + 1 for i, v in enumerate(sorted(need[e_]))} for e_ in need}

            def run(engname, e):
                k = 0
                for item in self.ops[engname]:
                    if item[0] == 'wait':
                        v = item[2]
                        if item[1] in newidx:
                            v = newidx[item[1]][v]
                        e.wait_ge(sems[item[1]], v)
                    elif item[0] == 'op':
                        k += 1
                        ins = item[1](e)
                        if k in need[engname]:
                            ins.then_inc(sems[engname], 1)
                    else:
                        _, out, in_, kk, kw = item
                        e.dma_start(out=out, in_=in_, **kw).then_inc(sems[kk], 16)

            @block.tensor
            def _(e):
                run('pe', e)

            @block.scalar
            def _(e):
                run('act', e)

            @block.vector
            def _(e):
                run('dve', e)

            @block.gpsimd
            def _(e):
                run('pool', e)

            @block.sync
            def _(e):
                run('sp', e)


def V(t, row, col, dims, nrows=128):
    a = t[:]
    ps = a.ap[0][0]
    return bass.AP(a.tensor, a.offset + row * ps + col, [[ps, nrows]] + [list(d) for d in dims])


def build():
    nc = bass.Bass("TRN2", target_bir_lowering=False)

    def din(name, shape, dt=F32):
        return nc.dram_tensor(name, list(shape), dt, kind="ExternalInput").ap()

    def dout(name, shape, dt=F32):
        return nc.dram_tensor(name, list(shape), dt, kind="ExternalOutput").ap()

    def dscr(name, shape, dt):
        return nc.dram_tensor(name, list(shape), dt, kind="Internal").ap()

    x_d = din("x", [NTOK, 1024])
    ck_d = din("ck", [256, 512])
    cv_d = din("cv", [256, 512])
    st_d = din("st", [2, 64, 64])
    cvec_d = din("cvec", [2, 1024])
    wada_d = din("w_ada", [1024, 3072])
    bada_d = din("b_ada", [3072])
    npre_d = din("norm_pre", [1024])
    npost_d = din("norm_post", [1024])
    win_d = din("w_in", [1024, 3072])
    lq_d = din("lambda_qk", [256])
    subln_d = din("subln", [128])
    are_d = din("ssm_A_re", [2, 32, 64])
    aim_d = din("ssm_A_im", [2, 32, 64])
    ldt_d = din("ssm_log_dt", [2, 32])
    bre_d = din("ssm_B_re", [2, 32, 64, 16])
    bim_d = din("ssm_B_im", [2, 32, 64, 16])
    cre_d = din("ssm_C_re", [2, 512, 64])
    cim_d = din("ssm_C_im", [2, 512, 64])
    dsk_d = din("ssm_D", [512])
    wglu_d = din("w_glu", [512, 512])
    bglu_d = din("b_glu", [512])
    wout_d = din("w_out", [1024, 1024])
    ident_d = din("ident", [128, 128])
    ropec_d = din("ropec", [1024, 64])
    ropes_d = din("ropes", [1024, 64])
    karr_d = din("karr", [128, NK * 32])
    mask_d = din("mask01", [128, 256])

    y_d = dout("y", [NTOK, 1024])
    nk_d = dout("nk", [512, 512])
    nv_d = dout("nv", [512, 512])
    ns_d = dout("ns", [2, 2, 64, 64])

    QK_s = dscr("QKs", [NTOK, 1024], BF16)
    V_s = dscr("Vs", [NTOK, 512], BF16)
    GA_s = dscr("GAs", [NTOK, 512], BF16)
    UD = dscr("UD", [512, 8, NCH], BF16)
    GSD = dscr("GSD", [512, 8, NCH], BF16)
    YD = dscr("YD", [32, 128, NCH], F32)

    P = Prog(nc)
    es = contextlib.ExitStack()

    def sb(name, shape, dt):
        return es.enter_context(nc.sbuf_tensor("s_" + name, list(shape), dt))

    with es:
        psall = es.enter_context(nc.psum_tensor("psall", [128, 4096], F32))
        banks = [psall[:, i * 512:(i + 1) * 512] for i in range(8)]
        bkb = [b.bitcast(BF16) for b in banks]

        ident = sb("ident", [128, 128], F32)
        identb = sb("identb", [128, 128], BF16)
        epst = sb("epst", [128, 1], F32)
        gm = sb("gm", [128, 8, 2], F32)
        shf = sb("shf", [128, 8, 2], F32)
        gn = sb("gn", [128, 2, 1024], F32)
        sub08 = sb("sub08", [128, 512], F32)
        ropec = sb("ropec", [128, 8, 64], F32)
        ropes = sb("ropes", [128, 8, 64], F32)
        neglam = sb("neglam", [128, 1], F32)
        al4 = sb("al4", [128, 128], F32)
        dsk = sb("dsk", [128, 4], F32)
        bglu = sb("bglu", [128, 4], F32)
        stat = sb("stat", [128, 64], F32)
        W_S = sb("W_S", [128, 32 * 2 * 128], BF16)
        W_Y = sb("W_Y", [128, 32 * 2 * 128], BF16)
        M0 = sb("M0", [128, 32 * 128], BF16)
        hT = sb("hT", [128, 8 * NTOK], BF16)
        catT = hT
        WA = sb("WA", [128, 3 * 4096], BF16)
        xt = [sb("xt%d" % i, [128, 1024], F32) for i in range(2)]
        xn = [sb("xn%d" % i, [128, 1024], BF16) for i in range(2)]
        junk = sb("junk", [128, 1024], BF16)
        R1 = sb("R1", [128, 7680], F32)
        SHb = sb("SHb", [128, 32 * 2 * 198], BF16)
        Xs = sb("Xs", [128, 32 * 196], BF16)
        R2 = sb("R2", [128, 1920], F32)
        ust = sb("ust", [128, 2 * 1536], BF16)

        P.dma('sp', ident[:], ident_d, writes=['ident'], key='c0')
        P.cp('dve', identb[:], ident[:], ['ident'], ['identb'])
        P.dma('sp', V(R1, 0, 7168, [[1, 256]]), mask_d, writes=['mask'], key='c0m')
        P.memset('dve', epst[:], 1e-6, ['epst'])
        P.memset('dve', stat[:], 0.0, ['st_l', 'st_l2', 'st_l3', 'bar60', 'bar61', 'bar62', 'bar63'] + [n % t for t in range(12) for n in ('ssq%d', 'sd%d', 'rs%d')])

        cT = V(R1, 0, 0, [[1, 16]])
        for cond in range(2):
            P.dma('sp', V(R1, 0, cond * 8, [[1, 8]]), cvec_d[cond].rearrange("(j p) -> p j", p=128),
                  writes=['cT'], key='c1', allow_slow_non_contiguous=True)
        csig = V(R1, 0, 16, [[1, 16]])
        P.act(csig, cT, AF.Sigmoid, ['cT'], ['csig'])
        csf = V(R1, 0, 32, [[1, 16]])
        P.tt('dve', csf, cT, csig, ALU.mult, ['cT', 'csig'], ['csf'])
        R1b = R1[:].bitcast(BF16)
        csT = bass.AP(R1b.tensor, R1b.offset + 128, [[R1b.ap[0][0], 128], [1, 16]])
        P.cp('dve', csT, csf, ['csf'], ['csT'])
        csbc = bass.AP(R1b.tensor, R1b.offset + 256, [[R1b.ap[0][0], 128], [256, 8], [128, 2], [1, 128]])
        csf_b = V(R1, 0, 32, [[1, 8], [8, 2], [0, 128]])
        P.cp('dve', csbc, csf_b, ['csf'], ['csbc'])
        bsh = V(R1, 0, 2400, [[1, 8]])
        bsc = V(R1, 0, 2408, [[1, 8]])
        npf = V(R1, 0, 2416, [[1, 8]])
        P.dma('sp', bsh, bada_d[0:1024].rearrange("(j p) -> p j", p=128), writes=['bsh'], key='c2',
              allow_slow_non_contiguous=True)
        P.dma('sp', bsc, bada_d[1024:2048].rearrange("(j p) -> p j", p=128), writes=['bsc'], key='c3',
              allow_slow_non_contiguous=True)
        P.dma('sp', npf, npre_d.rearrange("(j p) -> p j", p=128), writes=['npf'], key='c4',
              allow_slow_non_contiguous=True)
        bgate = V(R1, 0, 2560, [[1, 1024]])
        npost = V(R1, 0, 3584, [[1, 1024]])
        P.dma('sp', bgate, bass.AP(bada_d.tensor, 2048, [[0, 128], [1, 1024]]), writes=['bgate'], key='c5')
        P.dma('sp', npost, bass.AP(npost_d.tensor, 0, [[0, 128], [1, 1024]]), writes=['npost'], key='c6')
        P.dma('sp', V(sub08, 0, 0, [[128, 4], [1, 128]]),
              bass.AP(subln_d.tensor, 0, [[0, 128], [0, 4], [1, 128]]), writes=['sub08'], key='c7')
        P.ts('dve', sub08[:], sub08[:], 0.8, None, ALU.mult, None, ['sub08'], ['sub08'])

        WAv = [V(WA, 0, i * 4096, [[512, 8], [1, 512]]) for i in range(3)]
        wsrc = lambda wd, cb: bass.AP(wd.tensor, cb * 512, [[3072, 128], [128 * 3072, 8], [1, 512]])
        nblk = [0]

        def load_block(wd, cb):
            i = nblk[0] % 3
            nblk[0] += 1
            P.dma('pool', WAv[i], wsrc(wd, cb), writes=['WA%d' % i], key='wa%d' % i)
            return i

        ADA = [(WA, 'adaA'), (hT, 'adaB')]
        P.dma('pool', V(WA, 0, 0, [[1536, 8], [1, 1536]]),
              bass.AP(wada_d.tensor, 0, [[3072, 128], [128 * 3072, 8], [1, 1536]]),
              writes=['WA0', 'WA1', 'WA2', 'adaA'], key='adaA')
        P.dma('pool', V(hT, 0, 0, [[1536, 8], [1, 1536]]),
              bass.AP(wada_d.tensor, 1536, [[3072, 128], [128 * 3072, 8], [1, 1536]]),
              writes=['adaB'] + ['hT%d' % t for t in range(12)], key='adaB')
        for cb in range(6):
            buf, bres = ADA[cb // 3]
            coff = (cb % 3) * 512
            rds = [bres] + (['WA0', 'WA1', 'WA2'] if cb < 3 else [])
            if cb < 4:
                for nt in range(4):
                    col = (cb * 4 + nt) * 2
                    for j in range(8):
                        P.mm(banks[0][:, col:col + 2], V(buf, 0, j * 1536 + coff + nt * 128, [[1, 128]]),
                             bass.AP(R1b.tensor, R1b.offset + 128 + j, [[R1b.ap[0][0], 128], [8, 2]]),
                             j == 0, j == 7, rds + ['csT'], ['b0'])
            else:
                for cond in range(2):
                    bk = banks[1 + (cb - 4) * 2 + cond]
                    bn = 'b%d' % (1 + (cb - 4) * 2 + cond)
                    for j in range(8):
                        P.mm(bk[:, :], bass.AP(R1b.tensor, R1b.offset + 256 + j * 256 + cond * 128, [[R1b.ap[0][0], 128], [1, 128]]),
                             V(buf, 0, j * 1536 + coff, [[1, 512]]), j == 0, j == 7, rds + ['csbc'], [bn])
                    h0_ = (cb - 4) * 512
                    P.tt('dve', gn[:, cond, h0_:h0_ + 512], bk[:, :], V(R1, 0, 2560 + h0_, [[1, 512]]), ALU.add,
                         [bn, 'bgate'], ['gn'])
                    P.tt('pool', gn[:, cond, h0_:h0_ + 512], gn[:, cond, h0_:h0_ + 512], V(R1, 0, 3584 + h0_, [[1, 512]]),
                         ALU.mult, ['gn', 'npost'], ['gn'])
        CB_ORDER = [int(c) for c in os.environ.get("CBO", "450123")]
        win_blk = {}

        def issue_win(k):
            if k < 6 and CB_ORDER[k] not in win_blk:
                win_blk[CB_ORDER[k]] = load_block(win_d, CB_ORDER[k])
        b0v = lambda off: V(banks[0], 0, off, [[2, 8], [1, 2]])
        P.tt('dve', shf[:], b0v(0), V(R1, 0, 2400, [[1, 8], [0, 2]]), ALU.add, ['b0', 'bsh'], ['shf'])
        P.tt('dve', gm[:], b0v(16), V(R1, 0, 2408, [[1, 8], [0, 2]]), ALU.add, ['b0', 'bsc'], ['gm'])
        P.ts('dve', gm[:], gm[:], 1.0, None, ALU.add, None, ['gm'], ['gm'])
        P.tt('dve', gm[:], gm[:], V(R1, 0, 2416, [[1, 8], [0, 2]]), ALU.mult, ['gm', 'npf'], ['gm'])

        lqb = V(R1, 0, 4608, [[1, 256]])
        P.dma('sp', lqb, bass.AP(lq_d.tensor, 0, [[0, 128], [1, 256]]), writes=['lqb'], key='c8')
        lpr = V(R1, 0, 4864, [[64, 2], [1, 64]])
        P.tt('dve', lpr, V(R1, 0, 4608, [[128, 2], [1, 64]]), V(R1, 0, 4672, [[128, 2], [1, 64]]), ALU.mult,
             ['lqb'], ['lpr'])
        P.rsum(stat[:, 0:2], lpr, ['lpr'], ['st_l'])
        P.act(stat[:, 2:4], stat[:, 0:2], AF.Exp, ['st_l'], ['st_l2'])
        P.tt('dve', stat[:, 4:5], stat[:, 3:4], stat[:, 2:3], ALU.subtract, ['st_l2'], ['st_l3'])
        P.ts('dve', neglam[:], stat[:, 4:5], -0.2, None, ALU.add, None, ['st_l3'], ['neglam'])

        P.dma('sp', dsk[:], dsk_d.rearrange("(j p) -> p j", p=128), writes=['dsk'], key='c9',
              allow_slow_non_contiguous=True)
        P.dma('sp', bglu[:], bglu_d.rearrange("(j p) -> p j", p=128), writes=['bglu'], key='c10',
              allow_slow_non_contiguous=True)
        P.dma('sp', ropec[:], ropec_d.rearrange("(t p) f -> p t f", p=128), writes=['ropec'], key='c11')
        P.dma('sp', ropes[:], ropes_d.rearrange("(t p) f -> p t f", p=128), writes=['ropes'], key='c12')


        def p1a_stats(tt):
            xb_ = xt[tt % 2]
            xn_ = xn[tt % 2]
            xr, xnr = 'xt%d' % (tt % 2), 'xn%d' % (tt % 2)
            P.dma('sp', xb_[:], x_d[tt * 128:(tt + 1) * 128, :], writes=[xr], key=xr)
            P.act(junk[:], xb_[:], AF.Square, [xr], ['junk', 'ssq%d' % tt], accum=stat[:, 8 + tt:9 + tt])
            P.act(stat[:, 24 + tt:25 + tt], stat[:, 8 + tt:9 + tt], AF.Sqrt, ['ssq%d' % tt, 'epst'], ['sd%d' % tt],
                  bias=epst[:], scale=1.0 / 1024.0)
            P.recip(stat[:, 40 + tt:41 + tt], stat[:, 24 + tt:25 + tt], ['sd%d' % tt], ['rs%d' % tt])
            P.ts('dve', xn_[:], xb_[:], stat[:, 40 + tt:41 + tt], None, ALU.mult, None, [xr, 'rs%d' % tt], [xnr])

        def p1a_tr(tt):
            cond = 1 if tt < 8 else 0
            xn_ = xn[tt % 2]
            xnr = 'xn%d' % (tt % 2)
            bn = 'b%d' % (tt % 2)
            for j in range(8):
                P.tr(bkb[tt % 2][:, j * 128:(j + 1) * 128], xn_[:, j * 128:(j + 1) * 128], identb[:],
                     [xnr, 'identb'], [bn])
            for j in range(8):
                dst = V(hT, 0, j * NTOK + tt * 128, [[1, 128]])
                if j % 4 != 3:
                    P.act(dst, bkb[tt % 2][:, j * 128:(j + 1) * 128], AF.Identity, [bn, 'gm', 'shf'], ['hT%d' % tt],
                          bias=shf[:, j, cond:cond + 1], scale=gm[:, j, cond:cond + 1])
                else:
                    P.ts('dve', dst, bkb[tt % 2][:, j * 128:(j + 1) * 128], gm[:, j, cond:cond + 1],
                         shf[:, j, cond:cond + 1], ALU.mult, ALU.add, [bn, 'gm', 'shf'], ['hT%d' % tt])

        p1a_stats(0)
        for tt in range(12):
            if tt + 1 < 12:
                p1a_stats(tt + 1)
            p1a_tr(tt)

        KSTOP = os.environ.get('KSTOP', '')
        if KSTOP == '1a':
            P.finish(); P.emit(); return nc
        SHf = SHb[:].bitcast(F32)
        Xf = Xs[:].bitcast(F32)
        WYf = W_Y[:].bitcast(F32)

        def mk(apb):
            def f(row, col, dims, nrows=128):
                ps = apb.ap[0][0]
                return bass.AP(apb.tensor, apb.offset + row * ps + col, [[ps, nrows]] + [list(d) for d in dims])
            return f
        SF, XF, WYF = mk(SHf), mk(Xf), mk(WYf)
        WSF = mk(W_S[:].bitcast(F32))
        SBF = mk(SHb[:])
        XBF = mk(Xs[:])
        setup_q = []
        P.capture = setup_q
        G = 32
        SM = 0
        for d in range(2):
            P.dma('sp', WSF(0, SM + d * 64, [[1, 64]], G), are_d[d], writes=['are'], key='c14')
            P.dma('sp', WSF(0, SM + 128 + d * 64, [[1, 64]], G), aim_d[d], writes=['aim'], key='c15')
        P.dma('sp', WSF(0, SM + 256, [[1, 2]], G), ldt_d.rearrange("d g -> g d"), writes=['ldt'], key='c16',
              allow_slow_non_contiguous=True)
        P.act(WSF(0, SM + 258, [[1, 2]], G), WSF(0, SM + 256, [[1, 2]], G), AF.Exp, ['ldt'], ['dtt'])
        dtb = WSF(0, SM + 258, [[1, 2], [0, 64]], G)
        P.tt('pool', WSF(0, SM + 384, [[64, 2], [1, 64]], G), WSF(0, SM, [[64, 2], [1, 64]], G), dtb, ALU.mult,
             ['are', 'dtt'], ['ardt'])
        P.tt('pool', WSF(0, SM + 512, [[64, 2], [1, 64]], G), WSF(0, SM + 128, [[64, 2], [1, 64]], G), dtb, ALU.mult,
             ['aim', 'dtt'], ['thh'])
        for i_, (off, nm) in enumerate(((SM, 'are'), (SM + 128, 'aim'), (SM + 384, 'ardt'), (SM + 512, 'thh'))):
            P.tr(banks[6][:, i_ * 32:(i_ + 1) * 32], WSF(0, off, [[1, 128]], G), ident[0:32, 0:32], [nm, 'ident'], ['b6'])
        PWRE, PWIM, FT = 0, 832, 1664
        PS = 1728
        P.cp('dve', V(R2, 0, PS, [[1, 128]]), banks[6][:, 0:128], ['b6'], ['PSm'])
        NE = NK * 32
        KA = SF(0, 0, [[1, NE]])
        KAi = bass.AP(SHf.tensor, SHf.offset, [[SHf.ap[0][0], 128], [1, NE]]).bitcast(I32)
        ANG = SF(0, NE, [[1, NE]])
        MAG = SF(0, 2 * NE, [[1, NE]])
        P.dma('sp', KA, karr_d, writes=['B0'], key='c13')
        P.tt('dve', SF(0, NE, [[32, NK], [1, 32]]), SF(0, 0, [[32, NK], [1, 32]]), V(R2, 0, PS + 96, [[0, NK], [1, 32]]),
             ALU.mult, ['B0', 'PSm'], ['B1'])
        P.tt('pool', SF(0, 2 * NE, [[32, NK], [1, 32]]), SF(0, 0, [[32, NK], [1, 32]]), V(R2, 0, PS + 64, [[0, NK], [1, 32]]),
             ALU.mult, ['B0', 'PSm'], ['B2'])
        P.act(MAG, MAG, AF.Exp, ['B2'], ['B2'])
        INV2PI = 1.0 / TWO_PI
        for (woff, dn, shift, dstoff) in ((3 * NE, 'B3', 0.0, PWIM), (4 * NE, 'B4', 0.5 * math.pi, PWRE)):
            W = SF(0, woff, [[1, NE]])
            if shift != 0.0:
                P.ts('pool', ANG, ANG, shift, None, ALU.add, None, ['B1'], ['B1'])
            P.ts('dve', KAi, ANG, INV2PI, None, ALU.mult, None, ['B1', 'B0'], ['B0'])
            P.cp('dve', W, KAi, ['B0'], [dn])
            P.stt('dve', W, W, -TWO_PI, ANG, ALU.mult, ALU.add, [dn, 'B1'], [dn])
            P.ts('pool', W, W, 3.14159, -3.14159, ALU.min, ALU.max, [dn], [dn])
            P.act(W, W, AF.Sin, [dn], [dn])
            P.tt('dve', V(R2, 0, dstoff, [[1, NE]]), W, MAG, ALU.mult, [dn, 'B2'], ['PWT'])
        a_re = V(R2, 0, PWRE + K1 * 32, [[1, 32]])
        a_im = V(R2, 0, PWIM + K1 * 32, [[1, 32]])
        Are_ = V(R2, 0, PS, [[1, 32]])
        Aim_ = V(R2, 0, PS + 32, [[1, 32]])
        q = lambda i: SF(0, 5 * NE + i * 32, [[1, 32]])
        P.ts('pool', q(0), a_re, -1.0, None, ALU.add, None, ['PWT'], ['q0'])
        P.tt('pool', q(1), Are_, Are_, ALU.mult, ['PSm'], ['q1'])
        P.tt('pool', q(2), Aim_, Aim_, ALU.mult, ['PSm'], ['q2'])
        P.tt('pool', q(1), q(1), q(2), ALU.add, ['q1', 'q2'], ['q1'])
        P.recip(q(1), q(1), ['q1'], ['q1'])
        P.tt('pool', q(2), q(0), Are_, ALU.mult, ['q0', 'PSm', 'q2'], ['q2'])
        P.tt('pool', q(3), a_im, Aim_, ALU.mult, ['PWT', 'PSm'], ['q3'])
        P.tt('pool', q(2), q(2), q(3), ALU.add, ['q2', 'q3'], ['q2'])
        P.tt('pool', V(R2, 0, FT, [[1, 32]]), q(2), q(1), ALU.mult, ['q2', 'q1'], ['FTt'])
        P.tt('pool', q(2), a_im, Are_, ALU.mult, ['PWT', 'PSm', 'q2'], ['q2'])
        P.tt('pool', q(3), q(0), Aim_, ALU.mult, ['q0', 'PSm', 'q3'], ['q3'])
        P.tt('pool', q(2), q(2), q(3), ALU.subtract, ['q2', 'q3'], ['q2'])
        P.tt('pool', V(R2, 0, FT + 32, [[1, 32]]), q(2), q(1), ALU.mult, ['q2', 'q1'], ['FTt'])
        if KSTOP == '0b1':
            dbg_d = dout("dbg", [128, 1920])
            P.dma('sp', dbg_d, R2[:], reads=['PWT', 'FTt', 'PSm'], key='dbg', final=True)
            P.finish(); P.emit(); return nc
        P.cp('pool', al4[:, 0:32], V(R2, 0, PWRE + K8 * 32, [[1, 32]]), ['PWT'], ['al4'])
        P.cp('pool', al4[:, 96:128], V(R2, 0, PWRE + K8 * 32, [[1, 32]]), ['PWT'], ['al4'])
        P.cp('pool', al4[:, 64:96], V(R2, 0, PWIM + K8 * 32, [[1, 32]]), ['PWT'], ['al4'])
        P.ts('pool', al4[:, 32:64], V(R2, 0, PWIM + K8 * 32, [[1, 32]]), -1.0, None, ALU.mult, None, ['PWT'], ['al4'])

        GM_RES = ['B0', 'B1', 'B2', 'B3', 'B4', 'are', 'aim', 'ldt', 'dtt', 'ardt', 'thh', 'q0', 'q1', 'q2', 'q3', 'q4', 'q5']
        P.memset('pool', stat[:, 60:61], 0.0, GM_RES + ['gdone', 'bar60'])
        BT, CT = 4992, 6016
        for d in range(2):
            for ri, bd in enumerate((bre_d, bim_d)):
                P.dma('sp', V(R1, d * 64, BT + ri * 512, [[16, 32], [1, 16]], 64), bd[d].rearrange("g p h -> p g h"),
                      writes=['Bt'], key='c17')
            for ri, cd in enumerate((cre_d, cim_d)):
                P.dma('sp', XF(0, 2048 + ri * 512 + d * 64, [[128, 4], [1, 64]]),
                      cd[d].rearrange("(gh q) p -> q gh p", gh=4), reads=['gdone'], writes=['Cin'], key='c18')
        fre = V(R2, 0, FT, [[1, 32], [0, 16]])
        fim = V(R2, 0, FT + 32, [[1, 32], [0, 16]])
        Bre = V(R1, 0, BT, [[16, 32], [1, 16]])
        Bim = V(R1, 0, BT + 512, [[16, 32], [1, 16]])
        t1 = XF(0, 1024, [[16, 32], [1, 16]])
        t2 = XF(0, 1536, [[16, 32], [1, 16]])
        P.tt('pool', t1, Bre, fre, ALU.mult, ['Bt', 'FTt', 'gdone'], ['xt1'])
        P.tt('pool', t2, Bim, fim, ALU.mult, ['Bt', 'FTt', 'gdone'], ['xt2'])
        P.tt('pool', XF(0, 0, [[16, 32], [1, 16]]), t1, t2, ALU.subtract, ['xt1', 'xt2', 'gdone'], ['Bb'])
        P.tt('pool', t1, Bim, fre, ALU.mult, ['Bt', 'FTt', 'xt1'], ['xt1'])
        P.tt('pool', t2, Bre, fim, ALU.mult, ['Bt', 'FTt', 'xt2'], ['xt2'])
        P.tt('pool', XF(0, 512, [[16, 32], [1, 16]]), t1, t2, ALU.add, ['xt1', 'xt2', 'gdone'], ['Bb'])
        for ri in range(2):
            bk, bn = (banks[6], 'b6') if ri == 0 else (banks[7], 'b7')
            for gh in range(4):
                P.tr(bk[:, gh * 128:(gh + 1) * 128], XF(0, 2048 + ri * 512 + gh * 128, [[1, 128]]), ident[:],
                     ['Cin', 'ident'], [bn])
            P.cp('act', V(R1, 0, CT + ri * 512, [[1, 512]]), bk[:, :], [bn], ['Ct'])
        P.memset('pool', stat[:, 61:62], 0.0, ['Cin', 'xt1', 'xt2', 'cpfree', 'bar61'])

        if KSTOP == '0b2':
            P.finish(); P.emit(); return nc

        def pwv(off, kset, g0):
            return V(R2, 0, off + kset * 32 + g0, [[1, 16], [32, 8], [0, 16]])

        T1 = SF(0, 0, [[128, 16], [16, 8], [1, 16]])
        T2 = SF(0, 2048, [[128, 16], [16, 8], [1, 16]])
        engs = ['pool', 'dve']
        ei = [0]

        PWSO = 4992
        P.tt('dve', V(R1, 0, PWSO, [[1, NK * 32]]), V(R2, 0, PWRE, [[1, NK * 32]]), V(R2, 0, PWIM, [[1, NK * 32]]), ALU.add,
             ['PWT', 'gdone', 'Bb'], ['PWS', 'Bt'])
        P.tt('dve', WSF(0, 2048, [[1, 512]]), XF(0, 0, [[1, 512]]), XF(0, 512, [[1, 512]]), ALU.add, ['Bb', 'gdone'], ['Gtab'])
        P.tt('dve', WSF(0, 2560, [[1, 512]]), XF(0, 512, [[1, 512]]), XF(0, 0, [[1, 512]]), ALU.subtract, ['Bb', 'gdone'], ['Gtab'])
        P.tt('dve', WSF(0, 3072, [[1, 512]]), V(R1, 0, CT, [[1, 512]]), V(R1, 0, CT + 512, [[1, 512]]), ALU.add, ['Ct', 'gdone'], ['Gtab'])
        P.tt('dve', WSF(0, 3584, [[1, 512]]), V(R1, 0, CT, [[1, 512]]), V(R1, 0, CT + 512, [[1, 512]]), ALU.subtract, ['Ct', 'gdone'], ['Gtab'])

        def pwsv(kset, g0):
            return V(R1, 0, PWSO + kset * 32 + g0, [[1, 16], [32, 8], [0, 16]])

        def cmul(kset, g0, xre, xs, xd, out_re, out_im, neg_im, rn, wn):
            rn = rn + ['PWT', 'PWS', 'Gtab', 'gdone']
            P.tt('dve', T1, xre, pwsv(kset, g0), ALU.mult, rn, ['T1'])
            P.tt('dve', T2, xs, pwv(PWIM, kset, g0), ALU.mult, rn, ['T2'])
            P.tt('dve', out_re, T1, T2, ALU.subtract, ['T1', 'T2'], [wn])
            P.tt('dve', T2, xd, pwv(PWRE, kset, g0), ALU.mult, rn, ['T2'])
            if neg_im:
                P.tt('dve', out_im, T2, T1, ALU.subtract, ['T1', 'T2'], [wn])
            else:
                P.tt('dve', out_im, T1, T2, ALU.add, ['T1', 'T2'], [wn])

        def half_elem(gh2):
            g0 = gh2 * 16
            Bbre = XF(0, g0 * 16, [[16, 16], [0, 8], [1, 16]])
            Bbim = XF(0, 512 + g0 * 16, [[16, 16], [0, 8], [1, 16]])
            Ctre = V(R1, 0, CT + g0 * 16, [[16, 16], [0, 8], [1, 16]])
            Ctim = V(R1, 0, CT + 512 + g0 * 16, [[16, 16], [0, 8], [1, 16]])
            BeRe = SBF(0, 8192, [[128, 16], [1, 8], [8, 16]])
            BeIm = SBF(0, 10240, [[128, 16], [1, 8], [8, 16]])
            P.memset('pool', stat[:, 62:63], 0.0, ['cpfree', 'Cp', 'Be', 'bar62'])
            CpRe = XBF(0, 2048, [[128, 16], [1, 8], [8, 16]])
            CpNi = XBF(0, 4096, [[128, 16], [1, 8], [8, 16]])
            WYre = V(W_Y, 0, g0 * 256, [[256, 16], [1, 8], [8, 16]])
            WYni = V(W_Y, 0, g0 * 256 + 128, [[256, 16], [1, 8], [8, 16]])
            gv = lambda off: WSF(0, off + g0 * 16, [[16, 16], [0, 8], [1, 16]])
            cmul(KE, g0, Bbre, gv(2048), gv(2560), BeRe, BeIm, False, ['Bb'], 'Be')
            cmul(KQ, g0, Ctre, gv(3072), gv(3584), WYre, WYni, True, ['Ct'], 'W_Y')
            cmul(KP, g0, Ctre, gv(3072), gv(3584), CpRe, CpNi, True, ['Ct', 'Bb'], 'Cp')

        def half_pe(gh2):
            g0 = gh2 * 16
            for q4 in range(4):
                bi = 6 + (q4 % 2)
                for gi in range(4):
                    gl = q4 * 4 + gi
                    for ri in range(2):
                        src = SBF(0, (8192 if ri == 0 else 10240) + gl * 128, [[1, 128]])
                        P.tr(bkb[bi][:, (gi * 2 + ri) * 128:(gi * 2 + ri + 1) * 128], src, identb[:],
                             ['Be', 'identb'], ['b%d' % bi])
                P.cp('act', V(W_S, 0, (g0 + q4 * 4) * 256, [[1, 1024]]), bkb[bi][:, :], ['b%d' % bi, 'gdone'], ['W_S', 'Gtab'])
            for qd in range(4):
                for gi in range(4):
                    gl = qd * 4 + gi
                    for d in range(2):
                        o = banks[6 + d][:, gi * 128:(gi + 1) * 128]
                        P.mm(o, SBF(d * 64, 8192 + gl * 128, [[1, 128]], 64), XBF(d * 64, 2048 + gl * 128, [[1, 128]], 64),
                             True, False, ['Be', 'Cp'], ['b%d' % (6 + d)])
                        P.mm(o, SBF(d * 64, 10240 + gl * 128, [[1, 128]], 64), XBF(d * 64, 4096 + gl * 128, [[1, 128]], 64),
                             False, True, ['Be', 'Cp'], ['b%d' % (6 + d)])
                mt0 = SF(0, 0, [[128, 4], [1, 128]])
                mt1 = SF(0, 2048, [[128, 4], [1, 128]])
                P.tt('dve', mt0, V(banks[6], 0, 0, [[128, 4], [1, 128]]), V(R1, 0, 7168, [[0, 4], [1, 128]]), ALU.mult,
                     ['b6', 'mask', 'T2'], ['T1'])
                P.tt('dve', mt1, V(banks[7], 0, 0, [[128, 4], [1, 128]]), V(R1, 0, 7168 + 128, [[0, 4], [1, 128]]), ALU.mult,
                     ['b7', 'mask', 'T1'], ['T2'])
                P.tt('pool', V(M0, 0, (g0 + qd * 4) * 128, [[128, 4], [1, 128]]), mt0, mt1, ALU.add, ['T1', 'T2'], ['M0'])
        half_elem(0)
        half_pe(0)
        half_elem(1)
        half_pe(1)
        P.capture = None
        SETUP_RES = ['csbc', 'csT', 'csf', 'bsh', 'bsc', 'npf', 'bgate', 'npost', 'lqb', 'lpr', 'cT', 'csig']
        P.memset('dve', stat[:, 63:64], 0.0, SETUP_RES + ['r1free', 'bar63'])
        R1bf = mk(R1b)
        qkst = [R1bf(0, i * 512, [[1, 512]]) for i in range(3)]
        tmpA = [V(R1, 0, 768 + i * 512, [[1, 512]]) for i in range(2)]
        tmpB = [V(R1, 0, 1792 + i * 512, [[1, 512]]) for i in range(2)]
        f32st = [V(R1, 0, 2816 + i * 512, [[1, 512]]) for i in range(2)]
        nb = [0]
        nst = [0]
        pbanks = [2, 3, 4, 5]
        for k_cb, cb in enumerate(CB_ORDER):
            for kk in range(k_cb, min(6, k_cb + 3)):
                issue_win(kk)
            i = win_blk[cb]
            wr = 'WA%d' % i
            if cb < 4:
                for tt in range(12):
                    bi = pbanks[nb[0] % 4]
                    nb[0] += 1
                    bn = 'b%d' % bi
                    bk = banks[bi]
                    for j in range(8):
                        P.mm(bk[:, :], V(hT, 0, j * NTOK + tt * 128, [[1, 128]]),
                             V(WA, 0, i * 4096 + j * 512, [[1, 512]]), j == 0, j == 7, ['hT%d' % tt, wr], [bn])
                    si = nst[0] % 3
                    nst[0] += 1
                    st_, sr = qkst[si], 'qkst%d' % si
                    rows = slice(tt * 128, (tt + 1) * 128)
                    if cb < 2:
                        if tt < 8:
                            ti = tt % 2
                            ta, tb_ = tmpA[ti], tmpB[ti]
                            P.tt('dve', V(R1, 0, 768 + ti * 512, [[64, 8], [1, 64]]), V(bk, 0, 0, [[64, 8], [1, 64]]),
                                 V(ropec, 0, tt * 64, [[0, 8], [1, 64]]), ALU.mult, [bn, 'ropec', 'r1free'], ['tmpA%d' % ti])
                            P.tt('dve', V(R1, 0, 1792 + ti * 512, [[64, 8], [32, 2], [1, 16]]),
                                 V(bk, 0, 16, [[64, 8], [32, 2], [1, 16]]),
                                 V(ropes, 0, tt * 64, [[0, 8], [32, 2], [1, 16]]), ALU.mult, [bn, 'ropes', 'r1free'],
                                 ['tmpB%d' % ti])
                            P.tt('dve', V(R1, 0, 1792 + ti * 512 + 16, [[64, 8], [32, 2], [1, 16]]),
                                 V(bk, 0, 0, [[64, 8], [32, 2], [1, 16]]),
                                 V(ropes, 0, tt * 64 + 16, [[0, 8], [32, 2], [1, 16]]), ALU.mult, [bn, 'ropes', 'r1free'],
                                 ['tmpB%d' % ti])
                            P.tt('pool', st_, ta, tb_, ALU.add, ['tmpA%d' % ti, 'tmpB%d' % ti, 'r1free'], [sr])
                        else:
                            P.cp('act', st_, bk[:, :], [bn, 'r1free'], [sr])
                        P.dma('sp', QK_s[rows, cb * 512:(cb + 1) * 512], st_, reads=[sr], writes=['QKs'], key='qks')
                    elif cb == 2:
                        P.cp('act', st_, bk[:, :], [bn, 'r1free'], [sr])
                        P.dma('sp', V_s[rows, :], st_, reads=[sr], writes=['Vs'], key='vs')
                    else:
                        ti = tt % 2
                        P.act(tmpA[ti], bk[:, :], AF.Silu, [bn, 'r1free'], ['tmpA%d' % ti])
                        P.tt('pool', st_, tmpA[ti], sub08[:], ALU.mult, ['tmpA%d' % ti, 'sub08'], [sr])
                        P.dma('sp', GA_s[rows, :], st_, reads=[sr], writes=['GAs'], key='gas')
                    P.run_q(setup_q, 3)
                    if cb in (1, 2) and tt >= 8:
                        fi = tt % 2
                        P.cp('act', f32st[fi], bk[:, :], [bn, 'r1free'], ['f32st%d' % fi])
                        dst = (nk_d if cb == 1 else nv_d)[(tt - 8) * 128:(tt - 7) * 128, :]
                        P.dma('sp', dst, f32st[fi], reads=['f32st%d' % fi], key='o%d' % fi, final=True)
            else:
                dstD = UD if cb == 4 else GSD
                for ct in range(4):
                    for tb in range(3):
                        bi = pbanks[nb[0] % 4]
                        nb[0] += 1
                        bn = 'b%d' % bi
                        bk = banks[bi]
                        for j in range(8):
                            P.mm(bk[:, :], V(WA, 0, i * 4096 + j * 512 + ct * 128, [[1, 128]]),
                                 V(hT, 0, j * NTOK + tb * 512, [[1, 512]]), j == 0, j == 7,
                                 ['hT%d' % t for t in range(tb * 4, tb * 4 + 4)] + [wr], [bn])
                        ui_ = (ct + (0 if cb == 4 else 4)) % 2
                        sr = 'ust%d' % ui_
                        so = V(ust, 0, ui_ * 1536 + tb * 64, [[192, 8], [1, 64]])
                        src = V(bk, 0, 0, [[1, 8], [8, 64]])
                        if cb == 4:
                            P.cp('act', so, src, [bn], [sr])
                        else:
                            P.act(so, src, AF.Silu, [bn], [sr])
                        if tb == 2:
                            P.dma('sp', bass.AP(dstD.tensor, ct * 128 * 8 * NCH, [[8 * NCH, 128], [1, 8 * NCH]]),
                                  V(ust, 0, ui_ * 1536, [[1, 1536]]), reads=[sr],
                                  writes=['UD' if cb == 4 else 'GSD'], key='ud' if cb == 4 else 'gsd')
                        P.run_q(setup_q, 3)
                        P.run_q(setup_q, 3)

        if KSTOP == '1b':
            P.finish(); P.emit(); return nc
        st2 = sb("st2", [128, 64], F32)
        WAb = mk(WA[:])
        WAf = mk(WA[:].bitcast(F32))
        BUF = [R1bf, WAb]
        QTOK, KTOK, KC, VAUG, QT, KT, PTO, GAH = 0, 1024, 2048, 2304, 3604, 4628, 5908, 8212
        OSEQ = 6932
        GAHS = [GAH, 5908]
        O1 = WAf(0, 5514, [[1, 128]])
        OO = WAf(0, 5514 + 128, [[1, 128]])
        att_state = {'init': False, 'ptc': 0, 'stc': 0, 'units': 0, 'fslot': 0}
        pendB = []

        def att_init():
            P.memset('pool', st2[:], 0.0, ['st2'] + [n_ + str(k_) for k_ in range(3) for n_ in ('r0', 'r1', 'r1n', 'ossq', 'osd', 'ors')] + ['fs0', 'fs1', 'fss', 'fsd', 'frs',
                                            'bar2_60', 'bar2_61', 'bar2_62', 'bar2_63'])
            R1_ALL = ['qkst0', 'qkst1', 'qkst2', 'tmpA0', 'tmpA1', 'tmpB0', 'tmpB1', 'f32st0', 'f32st1',
                      'WA0', 'WA1', 'WA2']
            P.memset('pool', st2[:, 63:64], 0.0, R1_ALL + ['r1att', 'bar2_63'])
            for si in range(2):
                P.memset('pool', BUF[si](0, VAUG + 128, [[130, 10], [1, 2]]), 1.0, ['vaug%d' % si, 'r1att'])

        def att_loads(unit, si):
            (tok0, L, hasc), h = unit
            nt = L // 128
            B = BUF[si]
            sx = str(si)
            P.dma('sp', B(0, QTOK, [[128, nt], [1, 128]]),
                  bass.AP(QK_s.tensor, tok0 * 1024 + h * 128, [[1024, 128], [128 * 1024, nt], [1, 128]]),
                  reads=['r1att', 'QKs'], writes=['qtok' + sx], key='aq' + sx)
            P.dma('sp', B(0, KTOK, [[128, nt], [1, 128]]),
                  bass.AP(QK_s.tensor, tok0 * 1024 + 512 + h * 128, [[1024, 128], [128 * 1024, nt], [1, 128]]),
                  reads=['r1att', 'QKs'], writes=['ktok' + sx], key='ak' + sx)
            P.dma('sp', B(0, VAUG, [[130, nt], [1, 128]]),
                  bass.AP(V_s.tensor, tok0 * 512 + h * 128, [[512, 128], [128 * 512, nt], [1, 128]]),
                  reads=['r1att', 'Vs', 'vaug' + sx], writes=['vaug' + sx], key='av' + sx)
            P.dma('sp', B(0, GAHS[si], [[128, nt], [1, 128]]),
                  bass.AP(GA_s.tensor, tok0 * 512 + h * 128, [[512, 128], [128 * 512, nt], [1, 128]]),
                  reads=['r1att', 'GAs'], writes=['gah' + sx], key='ag' + sx)
            if hasc:
                P.dma('pool', B(0, KC, [[128, 2], [1, 128]]),
                      bass.AP(ck_d.tensor, h * 128, [[512, 128], [128 * 512, 2], [1, 128]]),
                      reads=['r1att'], writes=['kc' + sx], key='akc' + sx)
                P.dma('pool', B(0, VAUG + nt * 130, [[130, 2], [1, 128]]),
                      bass.AP(cv_d.tensor, h * 128, [[512, 128], [128 * 512, 2], [1, 128]]),
                      reads=['r1att', 'vaug' + sx], writes=['vaug' + sx], key='avc' + sx)

        def attention(units, pump_fn=None):
            if not att_state['init']:
                att_init()
                att_state['init'] = True
            att_loads(units[0], att_state['units'] % 2)
            for ui, unit in enumerate(units):
                (tok0, L, hasc), h = unit
                si = att_state['units'] % 2
                att_state['units'] += 1
                B = BUF[si]
                sx = str(si)
                nt = L // 128
                nkt = nt + (2 if hasc else 0)
                for t in range(nt):
                    P.tr(bkb[6][:, t * 128:(t + 1) * 128], B(0, QTOK + t * 128, [[1, 128]]), identb[:],
                         ['qtok' + sx, 'identb'], ['b6'])
                P.cp('act', B(0, QT, [[1, L]]), bkb[6][:, 0:L], ['b6', 'r1att'], ['qT' + sx])
                for t in range(nt):
                    P.tr(bkb[7][:, t * 128:(t + 1) * 128], B(0, KTOK + t * 128, [[1, 128]]), identb[:],
                         ['ktok' + sx, 'identb'], ['b7'])
                P.cp('act', B(0, KT, [[1, L]]), bkb[7][:, 0:L], ['b7', 'r1att'], ['kT' + sx])
                if hasc:
                    for t in range(2):
                        P.tr(bkb[6][:, t * 128:(t + 1) * 128], B(0, KC + t * 128, [[1, 128]]), identb[:],
                             ['kc' + sx, 'identb'], ['b6'])
                    P.cp('act', B(0, KT + L, [[1, 256]]), bkb[6][:, 0:256], ['b6', 'r1att'], ['kT' + sx])
                while pendB:
                    pendB.pop(0)()
                if ui + 1 < len(units):
                    att_loads(units[ui + 1], 1 - si)
                its = []
                for qb, q0 in enumerate(range(0, L, 384)):
                    bs = min(384, L - q0)
                    for kt in range(nkt):
                        its.append((qb, q0, bs, bs // 128, kt))
                slots = {}

                def QK(i):
                    qb, q0, bs, nq, kt = its[i]
                    sb_ = att_state['stc'] % 2
                    att_state['stc'] += 1
                    pb_ = att_state['ptc'] % 3
                    att_state['ptc'] += 1
                    slots[i] = (sb_, pb_)
                    for c in range(2):
                        bi = sb_ * 2 + c
                        P.mm(banks[bi][:, 0:bs], B(c * 64, KT + kt * 128, [[1, 128]], 64),
                             B(c * 64, QT + q0, [[1, bs]], 64), True, True, ['kT' + sx, 'qT' + sx], ['b%d' % bi])

                def EXPPV(i):
                    qb, q0, bs, nq, kt = its[i]
                    sb_, pb_ = slots[i]
                    prn = 'PT%d' % pb_
                    pv = (4, 5) if qb % 2 == 0 else (6, 7)
                    P.act(R1bf(0, PTO + pb_ * 768, [[384, 2], [1, bs]]), V(psall, 0, sb_ * 1024, [[512, 2], [1, bs]]), AF.Exp,
                          ['b%d' % (sb_ * 2), 'b%d' % (sb_ * 2 + 1), 'r1att'], [prn], scale=0.125)
                    for c in range(2):
                        for qt in range(nq):
                            P.mm(banks[pv[c]][:, qt * 129:(qt + 1) * 129],
                                 R1bf(0, PTO + pb_ * 768 + c * 384 + qt * 128, [[1, 128]]),
                                 B(0, VAUG + kt * 130, [[1, 129]]), (kt == 0 and qt == 0), (kt == nkt - 1),
                                 [prn, 'vaug' + sx], ['b%d' % pv[c]])
                    if kt == nkt - 1:
                        for qt in range(nq):
                            t = (q0 // 128) + qt
                            k_ = att_state['fslot'] % 3
                            att_state['fslot'] += 1
                            ks = str(k_)
                            c_ = 16 + k_ * 8
                            OOk = WAf(0, 5514 + 128 + k_ * 128, [[1, 128]])
                            b4n, b5n = 'b%d' % pv[0], 'b%d' % pv[1]
                            a0 = banks[pv[0]][:, qt * 129:qt * 129 + 128]
                            a1 = banks[pv[1]][:, qt * 129:qt * 129 + 128]
                            P.recip(st2[:, c_:c_ + 1], banks[pv[0]][:, qt * 129 + 128:qt * 129 + 129], [b4n], ['r0' + ks])
                            P.recip(st2[:, c_ + 1:c_ + 2], banks[pv[1]][:, qt * 129 + 128:qt * 129 + 129], [b5n], ['r1' + ks])
                            P.tt('dve', st2[:, c_ + 2:c_ + 3], st2[:, c_ + 1:c_ + 2], neglam[:], ALU.mult, ['r1' + ks, 'neglam'],
                                 ['r1n' + ks])
                            P.ts('dve', O1, a1, st2[:, c_ + 2:c_ + 3], None, ALU.mult, None, [b5n, 'r1n' + ks, 'r1att'], ['O1'])
                            P.stt('dve', OOk, a0, st2[:, c_:c_ + 1], O1, ALU.mult, ALU.add, [b4n, 'r0' + ks, 'O1', 'r1att'],
                                  ['OO' + ks])
                            P.stt('dve', O1, OOk, 1.0, OOk, ALU.mult, ALU.mult, ['OO' + ks], ['O1', 'ossq' + ks],
                                  accum=st2[:, c_ + 3:c_ + 4])

                            def stageB(t=t, ks=ks, c_=c_, OOk=OOk, h=h, si=si, sx=sx, B=B):
                                P.act(st2[:, c_ + 4:c_ + 5], st2[:, c_ + 3:c_ + 4], AF.Ln, ['ossq' + ks, 'epst'], ['osd' + ks],
                                      bias=epst[:], scale=1.0 / 128.0)
                                P.act(st2[:, c_ + 5:c_ + 6], st2[:, c_ + 4:c_ + 5], AF.Exp, ['osd' + ks], ['ors' + ks], scale=-0.5)
                                P.stt('dve', WAb(0, OSEQ + t * 512 + h * 128, [[1, 128]]), OOk, st2[:, c_ + 5:c_ + 6],
                                      B(0, GAHS[si] + t * 128, [[1, 128]]), ALU.mult, ALU.mult,
                                      ['OO' + ks, 'ors' + ks, 'gah' + sx, 'r1att'], ['oseq'])
                            pendB.append(stageB)

                QK(0)
                for i in range(len(its)):
                    if i + 1 < len(its):
                        QK(i + 1)
                    npend = len(pendB)
                    EXPPV(i)
                    if npend:
                        pendB.pop(0)()
                    if pump_fn is not None:
                        pump_fn(1)
                while len(pendB) > 3:
                    pendB.pop(0)()
                if h == 3:
                    while pendB:
                        pendB.pop(0)()
                    for t in range(nt):
                        for hh in range(4):
                            P.tr(bkb[6][:, hh * 128:(hh + 1) * 128], WAb(0, OSEQ + t * 512 + hh * 128, [[1, 128]]), identb[:],
                                 ['oseq', 'identb'], ['b6'])
                        P.cp('act', V(catT, 0, tok0 + t * 128, [[NTOK, 4], [1, 128]]),
                             bass.AP(bkb[6].tensor, bkb[6].offset, [[bkb[6].ap[0][0], 128], [128, 4], [1, 128]]),
                             ['b6'] + ['hT%d' % i for i in range(12)], ['catA'])

        SEQ_P = [(1024, 256, False), (1280, 256, False)]
        SEQ_S = [(0, 1024, True)]

        P.run_q(setup_q, 100000)
        if KSTOP == '0b':
            P.finish(); P.emit(); return nc
        for pc in (128, 162):
            P.memset('pool', V(Xs, 0, pc, [[196, 32], [1, 2]]), 0.0, ['Xs', 'Bb', 'Cp', 'Cin', 'xt1', 'xt2', 'cpfree'])
        for (c0_, n_, s0_) in [(0, 128, 1), (128, 32, 131), (160, 32, 165)]:
            P.dma('sp', V(Xs, 0, s0_ - 1, [[196, 32], [1, n_]]),
                  bass.AP(UD.tensor, c0_, [[NCH, 128], [128 * NCH, 32], [1, n_]]),
                  reads=['UD'], writes=['Xs', 'Bb', 'Cp', 'Cin', 'xt1', 'xt2', 'cpfree'], key='xs')
        SEG = [(0, 128, 1), (128, 32, 131), (160, 32, 165)]
        P.memset('pool', SHb[:], 0.0, ['SHb', 'Be', 'T1', 'T2', 'M0tmp'])
        for g in range(32):
            bi = pbanks[nb[0] % 4]
            nb[0] += 1
            bn = 'b%d' % bi
            bk = banks[bi]
            for ri in range(2):
                P.mm(bk[:, ri * 196:(ri + 1) * 196], V(W_S, 0, g * 256 + ri * 128, [[1, 128]]),
                     V(Xs, 0, g * 196, [[1, 196]]), True, True, ['W_S', 'Xs'], [bn])
            P.cp('act' if g % 2 else 'dve', V(SHb, 0, g * 396 + 1, [[198, 2], [1, 196]], 64),
                 V(bk, 0, 0, [[196, 2], [1, 196]], 64), [bn], ['SHb'])
            for k_, (c0, n, s0) in enumerate(SEG):
                P.cp('dve' if g % 2 else 'act', V(SHb, 64, g * 396 + s0 + n - 1, [[198, 2], [-1, n]], 64),
                     V(bk, 64, s0 - 1, [[196, 2], [1, n]], 64), [bn], ['SHb'])
        P.dma('pool', V(W_S, 0, 0, [[1024, 8], [1, 1024]]), bass.AP(wout_d.tensor, 0, [[1024, 128], [128 * 1024, 8], [1, 1024]]),
              writes=['W_S', 'wout'], key='wout')
        P.dma('pool', R1bf(0, 13000, [[512, 4], [1, 512]]), bass.AP(wglu_d.tensor, 0, [[512, 128], [128 * 512, 4], [1, 512]]),
              writes=['Ct', 'mask', 'Bt', 'wglu'], key='wglu')
        h0in = V(R2, 0, 0, [[1, 128]], 64)
        for d in range(2):
            P.dma('sp', V(R2, 0, d * 64, [[1, 64]], 64), st_d[d], writes=['h0in', 'PWT', 'FTt'], key='h0')
        P.tr(banks[6][:, 0:64], h0in, ident[0:64, 0:64], ['h0in', 'ident'], ['b6'])
        Zst = V(R2, 0, 256, [[1, 64]])
        T4 = V(R2, 0, 384, [[1, 128]])
        U2 = V(R2, 0, 512, [[1, 64]])
        FIN = V(R2, 0, 640, [[1, 64]])

        H0SB = 576
        P.cp('dve', V(R2, 0, H0SB, [[1, 64]]), banks[6][:, 0:64], ['b6'], ['h0sb'])

        def chain(eng, seg, zero_init, zoff, toff, uoff, fin_off, fin_name, q, tag, split=None):
            c0, n, s0 = seg
            zr = 'Z' + tag
            Z = V(R2, 0, zoff, [[32, 2], [1, 32]])
            Zb = V(R2, 0, zoff, [[0, 2], [32, 2], [1, 32]])
            A4 = V(al4, 0, 0, [[64, 2], [32, 2], [1, 32]])

            def init():
                if zero_init:
                    P.memset(eng, Z, 0.0, [zr])
                else:
                    P.cp(eng, Z, V(R2, 0, H0SB, [[32, 2], [1, 32]]), ['h0sb'], [zr])
                    P.cp(eng, V(SHb, 0, s0 - 1, [[198, 2], [396, 32]]), Z, [zr], ['SHbw' + tag])
            q.append(init)

            def step(i_, en, to, uo):
                tr_, ur = 'T4' + tag + en, 'U' + tag + en
                T = V(R2, 0, to, [[64, 2], [32, 2], [1, 32]])
                Ta = V(R2, 0, to, [[64, 2], [1, 32]])
                Tb = V(R2, 0, to + 32, [[64, 2], [1, 32]])
                U = V(R2, 0, uo, [[32, 2], [1, 32]])
                Sv = V(SHb, 0, s0 + i_, [[198, 2], [396, 32]])
                P.tt(en, T, Zb, A4, ALU.mult, [zr, 'al4'], [tr_])
                P.tt(en, U, Ta, Tb, ALU.add, [tr_], [ur])
                P.tt(en, Z, U, Sv, ALU.add, [ur, 'SHb'], [zr])
                P.cp(en, Sv, Z, [zr], ['SHbw' + tag])
            for i_ in range(n):
                if split is not None and i_ >= split[0]:
                    split[4].append(lambda i_=i_: step(i_, split[1], split[2], split[3]))
                else:
                    q.append(lambda i_=i_: step(i_, eng, toff, uoff))
            if fin_name is not None:
                q.append(lambda: P.cp(eng, V(R2, 0, fin_off, [[1, 64]]), V(R2, 0, zoff, [[1, 64]]), [zr], [fin_name]))

        rec_q = {'dve': [], 'pool': []}

        def fin_out(si_):
            fn = 'fin%d' % si_
            foff = 640 + (si_ - 1) * 64
            fo_off = 768 + (si_ - 1) * 128
            P.tr(banks[7][0:64, 0:128], V(R2, 0, foff, [[1, 64]]), ident[:], [fn, 'ident'], ['b7'])
            P.cp('act', V(R2, 0, fo_off, [[1, 128]], 64), banks[7][0:64, 0:128], ['b7'], ['fo%d' % si_])
            P.dma('sp', bass.AP(ns_d.tensor, (si_ - 1) * 8192, [[64, 64], [4096, 2], [1, 64]]),
                  V(R2, 0, fo_off, [[64, 2], [1, 64]], 64), reads=['fo%d' % si_], key='ons', final=True)

        for si_, seg in ((1, SEG[1]), (2, SEG[2])):
            chain('pool', seg, True, 320, 1024, 1152, 640 + (si_ - 1) * 64, 'fin%d' % si_, rec_q['pool'], 'p')
        SPLIT = int(os.environ.get('SPLIT', '64'))
        chain('dve', SEG[0], False, 256, 384, 512, 0, None, rec_q['dve'], 's',
              split=(SPLIT, 'pool', 1024, 1152, rec_q['pool']))

        def pump(n=1):
            for e_ in ('dve', 'pool'):
                for _ in range(n):
                    if rec_q[e_]:
                        rec_q[e_].pop(0)()

        if KSTOP == '2':
            P.finish(); P.emit(); return nc

        attention([(sq, h) for sq in SEQ_P + SEQ_S for h in range(4)], pump)
        pump(1000)
        fin_out(1)
        fin_out(2)
        if KSTOP == '3':
            P.finish(); P.emit(); return nc
        YDp = dscr("YDp", [32, 128, 196], F32)
        ATT_RES = ['oseq', 'O1', 'OO0', 'OO1', 'OO2'] + [n + s_ for n in ('qtok', 'ktok', 'kc', 'vaug', 'gah', 'qT', 'kT') for s_ in '01']
        P.memset('pool', st2[:, 60:61], 0.0, ATT_RES + ['WA0', 'WA1', 'WA2', 'wafree', 'bar2_60'])
        for ct in range(4):
            P.dma('sp', WAb(0, 6144 + ct * 1536, [[1, 1536]]),
                  bass.AP(UD.tensor, ct * 128 * 8 * NCH, [[8 * NCH, 128], [1, 8 * NCH]]),
                  reads=['UD', 'wafree'], writes=['uTp%d' % ct], key='utp')
            P.dma('sp', WAb(0, ct * 1536, [[1, 1536]]),
                  bass.AP(GSD.tensor, ct * 128 * 8 * NCH, [[8 * NCH, 128], [1, 8 * NCH]]),
                  reads=['GSD', 'wafree'], writes=['gSp%d' % ct], key='gsp')
        for (c0, n, s0) in SEG:
            P.cp('act', R1bf(0, s0 - 1, [[198, 64], [1, n]], 64), V(SHb, 0, s0 - 1, [[198, 64], [1, n]], 64),
                 ['SHb', 'SHbws', 'SHbwp', 'wafree'], ['HAL'])
            P.cp('dve', R1bf(64, s0 - 1, [[198, 64], [1, n]], 64), V(SHb, 64, s0 + n - 2, [[198, 64], [-1, n]], 64),
                 ['SHb', 'SHbws', 'SHbwp', 'wafree'], ['HAL'])
        for g in range(32):
            bi = g % 4
            bn = 'b%d' % bi
            bk = banks[bi]
            P.mm(bk[:, 0:196], V(W_Y, 0, g * 256, [[1, 128]]), R1bf(0, g * 396, [[1, 196]]), True, False,
                 ['W_Y', 'HAL'], [bn])
            P.mm(bk[:, 0:196], V(W_Y, 0, g * 256 + 128, [[1, 128]]), R1bf(0, g * 396 + 198, [[1, 196]]), False, False,
                 ['W_Y', 'HAL'], [bn])
            P.mm(bk[:, 0:196], V(M0, 0, g * 128, [[1, 128]]), V(Xs, 0, g * 196, [[1, 196]]),
                 False, True, ['M0', 'Xs'], [bn])
            yi = g % 3
            yst = V(R2, 0, 1216 + yi * 196, [[1, 196]])
            P.cp('act' if g % 2 else 'dve', yst, bk[:, 0:196], [bn, 'fo1', 'fo2'], ['yst%d' % yi])
            P.dma('sp', YDp[g], yst, reads=['yst%d' % yi], writes=['YD%d' % (g // 8)], key='yd%d' % (g // 8))
        if KSTOP == '4':
            P.finish(); P.emit(); return nc

        SSM_DEAD = ['SHb', 'SHbws', 'SHbwp', 'HAL', 'Xs', 'W_Y', 'M0', 'PT0', 'PT1', 'PT2', 'r1att'] + ATT_RES
        P.memset('pool', st2[:, 62:63], 0.0, SSM_DEAD + ['gfree', 'bar2_62'])
        M0f = mk(M0[:].bitcast(F32))
        WSb = mk(W_S[:])
        ysf = [WYF(0, 0, [[1, 1536]]), WYF(0, 1536, [[1, 1536]]), M0f(0, 0, [[1, 1536]]), V(R1, 0, 0, [[1, 1536]])]
        for ct in range(4):
            yT = SF(0, ct * 1536, [[192, 8], [1, 192]])
            for (c0, n, s0) in SEG:
                P.dma('sp', SF(0, ct * 1536 + c0, [[192, 8], [1, n]]),
                      bass.AP(YDp.tensor, ct * 1024 * 196 + s0 - 1, [[8 * 196, 128], [196, 8], [1, n]]),
                      reads=['YD%d' % ct, 'gfree'], writes=['yT%d' % ct], key='yt%d' % ct)
            P.stt('dve', ysf[ct], WAb(0, 6144 + ct * 1536, [[1, 1536]]), dsk[:, ct:ct + 1], SF(0, ct * 1536, [[1, 1536]]),
                  ALU.mult, ALU.add, ['uTp%d' % ct, 'yT%d' % ct, 'dsk', 'gfree'], ['ysf%d' % ct])
            P.act(ysf[ct], ysf[ct], AF.Gelu, ['ysf%d' % ct], ['ysf%d' % ct])
            P.cp('dve', R1bf(0, 3072 + ct * 1536, [[1, 1536]]), ysf[ct], ['ysf%d' % ct, 'gfree'], ['ysb%d' % ct])
        for nt_ in range(4):
            for sbk in range(4):
                bi = (nt_ * 4 + sbk) % 4
                bn = 'b%d' % bi
                for mt in range(4):
                    P.mm(banks[bi][:, 0:384], R1bf(0, 13000 + mt * 512 + nt_ * 128, [[1, 128]]),
                         R1bf(0, 3072 + mt * 1536 + sbk * 384, [[1, 384]]), mt == 0, mt == 3,
                         ['wglu'] + ['ysb%d' % m for m in range(4)], [bn])
                sgi = (nt_ * 4 + sbk) % 2
                sg = V(R1, 0, 4608 + sgi * 384, [[1, 384]])
                tq = V(R1, 0, 5632 + sgi * 384, [[1, 384]])
                P.act(sg, banks[bi][:, 0:384], AF.Sigmoid, [bn, 'bglu', 'gfree'], ['sg%d' % sgi], bias=bglu[:, nt_:nt_ + 1])
                yv = bass.AP(ysf[nt_].tensor, ysf[nt_].offset + sbk * 384, [[ysf[nt_].ap[0][0], 128], [1, 384]])
                P.tt('dve', tq, yv, sg, ALU.mult, ['ysf%d' % nt_, 'sg%d' % sgi], ['tq%d' % sgi])
                P.tt('dve' if (nt_ * 4 + sbk) % 2 == 0 else 'pool', V(catT, 0, (4 + nt_) * NTOK + 2 * sbk, [[1, 2], [8, 192]]),
                     V(R1, 0, 5632 + sgi * 384, [[192, 2], [1, 192]]), WAb(0, nt_ * 1536 + sbk * 384, [[192, 2], [1, 192]]),
                     ALU.mult, ['tq%d' % sgi, 'gSp%d' % nt_] + ['hT%d' % i for i in range(12)], ['catS'])
        if KSTOP == '5':
            P.finish(); P.emit(); return nc

        P.memset('pool', st2[:, 61:62], 0.0, ['ysb0', 'ysb1', 'ysb2', 'ysb3', 'sg0', 'sg1', 'tq0', 'tq1', 'ysf3', 'r1out', 'bar2_61'])
        xbufs = [xt[0][:], xt[1][:], V(R1, 0, 2048, [[1, 1024]]), V(R1, 0, 3072, [[1, 1024]])]
        def xload(t_):
            P.dma('sp', xbufs[t_ % 4], x_d[t_ * 128:(t_ + 1) * 128, :], reads=['r1out'],
                  writes=['xo%d' % (t_ % 4), 'xt%d' % (t_ % 2)], key='xo%d' % (t_ % 4))
        for t_ in range(4):
            xload(t_)
        for tt in range(12):
            cond = 1 if tt < 8 else 0
            xb_ap = xbufs[tt % 4]
            xr = 'xo%d' % (tt % 4)
            roff = [0, 1024, 4096, 5120][tt % 4]
            res = V(R1, 0, roff, [[1, 1024]])
            rr = 'res%d' % (tt % 4)
            for cb in range(2):
                bi = (tt % 4) * 2 + cb
                bn = 'b%d' % bi
                for j in range(8):
                    P.mm(banks[bi][:, :], V(catT, 0, j * NTOK + tt * 128, [[1, 128]]),
                         V(W_S, 0, j * 1024 + cb * 512, [[1, 512]]), j == 0, j == 7, ['catA', 'catS', 'wout'], [bn])
            b0n, b1n = 'b%d' % ((tt % 4) * 2), 'b%d' % ((tt % 4) * 2 + 1)
            pair = V(psall, 0, (tt % 4) * 1024, [[1, 1024]])
            P.act(junk[:, :], pair, AF.Square, [b0n, b1n], ['junk', 'fss'], accum=st2[:, 10:11])
            P.act(st2[:, 11:12], st2[:, 10:11], AF.Ln, ['fss', 'epst'], ['fsd'], bias=epst[:], scale=1.0 / 1024.0)
            P.act(st2[:, 12:13], st2[:, 11:12], AF.Exp, ['fsd'], ['frs'], scale=-0.5)
            P.act(res, pair, AF.Copy, [b0n, b1n, 'frs', 'r1out'], [rr], scale=st2[:, 12:13])
            P.tt('dve', res, res, gn[:, cond, :], ALU.mult, [rr, 'gn'], [rr])
            P.tt('pool', xb_ap, xb_ap, res, ALU.add, [xr, rr], [xr])
            P.dma('sp', y_d[tt * 128:(tt + 1) * 128, :], xb_ap, reads=[xr], key='oy%d' % (tt % 4), final=True)
            if tt + 4 < 12:
                xload(tt + 4)
        P.finish()
        P.emit()
    return nc


_NC = None


def _consts():
    t = np.arange(1024)
    row = (t // 64).astype(np.float32)
    col = (t % 64).astype(np.float32)
    inv = (10000.0 ** (-np.arange(16, dtype=np.float32) / 16)).astype(np.float32)
    ar = row[:, None] * inv[None, :]
    ac = col[:, None] * inv[None, :]
    cosr, sinr, cosc, sinc = np.cos(ar), np.sin(ar), np.cos(ac), np.sin(ac)
    ropec = np.concatenate([cosr, cosr, cosc, cosc], axis=1).astype(np.float32)
    ropes = np.concatenate([-sinr, sinr, -sinc, sinc], axis=1).astype(np.float32)
    k = np.zeros((NK, 2), np.float32)
    for i in range(8):
        k[KE + i] = (7 - i, i)
        k[KQ + i] = (i + 1, 8 - i)
        k[KP + i] = (i - 7, -i)
    k[K1] = (1, 1)
    k[K8] = (8, 8)
    karr = np.zeros((2, 64, NK, 32), np.float32)
    karr[0] = k[None, :, 0, None]
    karr[1] = k[None, :, 1, None]
    karr = np.ascontiguousarray(karr.reshape(128, NK * 32))
    s = np.tile(np.arange(8), 16)
    m0 = (s[:, None] <= s[None, :]).astype(np.float32)
    m1 = (s[:, None] >= s[None, :]).astype(np.float32)
    mask = np.concatenate([m0, m1], axis=1)
    return dict(ident=np.eye(128, dtype=np.float32), ropec=ropec, ropes=ropes, karr=karr, mask01=mask)


def kernel(**inp):
    global _NC
    if _NC is None:
        _NC = build()
    f = lambda a: np.ascontiguousarray(np.asarray(a, dtype=np.float32))
    cst = _consts()
    shared = dict(
        w_ada=f(inp['w_ada'][0]), b_ada=f(inp['b_ada'][0]), norm_pre=f(inp['norm_pre'][0]),
        norm_post=f(inp['norm_post'][0]), w_in=f(inp['w_in'][0]), lambda_qk=f(inp['lambda_qk'][0]).reshape(256),
        subln=f(inp['subln'][0]), ssm_A_re=f(inp['ssm_A_re'][0]), ssm_A_im=f(inp['ssm_A_im'][0]),
        ssm_log_dt=f(inp['ssm_log_dt'][0]), ssm_B_re=f(inp['ssm_B_re'][0]), ssm_B_im=f(inp['ssm_B_im'][0]),
        ssm_C_re=f(inp['ssm_C_re'][0]).reshape(2, 512, 64), ssm_C_im=f(inp['ssm_C_im'][0]).reshape(2, 512, 64),
        ssm_D=f(inp['ssm_D'][0]), w_glu=f(inp['w_glu'][0]), b_glu=f(inp['b_glu'][0]), w_out=f(inp['w_out'][0]),
        **cst)
    xp, xs = f(inp['x_prompt']), f(inp['x_sample'])
    in_maps = []
    for c in range(8):
        m = dict(shared)
        m['x'] = np.ascontiguousarray(np.concatenate([xs[c], xp[2 * c], xp[2 * c + 1]], axis=0))
        m['ck'] = f(inp['cache_k'][c, 0]).reshape(256, 512)
        m['cv'] = f(inp['cache_v'][c, 0]).reshape(256, 512)
        m['st'] = f(inp['state_ssm'][c, 0]).reshape(2, 64, 64)
        m['cvec'] = np.ascontiguousarray(np.stack([f(inp['c_ctx']), f(inp['c'][c])], axis=0))
        in_maps.append(m)
    res = run_bass_kernel_spmd(_NC, in_maps, core_ids=list(range(8)))
    R = res.results
    y_p = np.zeros((16, 256, 1024), np.float32)
    y_s = np.zeros((8, 1024, 1024), np.float32)
    nk = np.zeros((16, 1, 256, 4, 128), np.float32)
    nv = np.zeros((16, 1, 256, 4, 128), np.float32)
    ns = np.zeros((16, 1, 2, 2, 32, 64), np.float32)
    for c in range(8):
        y = R[c]['y']
        y_s[c] = y[0:1024]
        y_p[2 * c] = y[1024:1280]
        y_p[2 * c + 1] = y[1280:1536]
        nk[2 * c:2 * c + 2, 0] = R[c]['nk'].reshape(2, 256, 4, 128)
        nv[2 * c:2 * c + 2, 0] = R[c]['nv'].reshape(2, 256, 4, 128)
        ns[2 * c:2 * c + 2, 0] = R[c]['ns'].reshape(2, 2, 2, 32, 64)
    return (y_p, y_s, nk, nv, ns)
```
